# Optimizing a Trainium2 kernel written in Bass

```python
import math
import jax, jax.numpy as jnp
from jax import lax
import numpy as np

D_MODEL = 1024
BATCH = 8
SEQ = 4096
DEPTH = 1

D_ATTN = D_MODEL // 2
D_LRU = D_MODEL - D_ATTN
DIFF_HEAD_DIM = 64
N_DIFF_HEADS = D_ATTN // (2 * DIFF_HEAD_DIM)
DIFF_V_DIM = 2 * DIFF_HEAD_DIM
Q_BLOCK = 128
NUM_BUCKETS = 32
MAX_DISTANCE = 128
LRU_BLOCKS = 8
LRU_BLOCK_DIM = D_LRU // LRU_BLOCKS
CONV_WIDTH = 4
LRU_C = 8.0
D_FF = int(math.ceil(D_MODEL * 8 / 3 / 256) * 256)
D_IN = 3 * D_ATTN + 2 * D_LRU
NORM_EPS = 1e-6

kernel_name = "hybrid_diffattn_rglru_encoder_layer"


def rms_norm(x, g):
    xf = x.astype(jnp.float32)
    y = xf * lax.rsqrt(jnp.mean(xf * xf, axis=-1, keepdims=True) + NORM_EPS)
    return (y * g.astype(jnp.float32)).astype(x.dtype)


def t5_bucket(rel):
    half = NUM_BUCKETS // 2
    ret = jnp.where(rel > 0, half, 0)
    n = jnp.abs(rel)
    max_exact = half // 2
    nf = jnp.maximum(n, 1).astype(jnp.float32)
    large = max_exact + (jnp.log(nf / max_exact) / math.log(MAX_DISTANCE / max_exact)
                         * (half - max_exact)).astype(jnp.int32)
    large = jnp.minimum(large, half - 1)
    return ret + jnp.where(n < max_exact, n, large)


def diff_attention(q, k, v, lam, rel_bias, subln_g, lambda_init):
    B, S, _ = q.shape
    H, Dh, Dv = N_DIFF_HEADS, DIFF_HEAD_DIM, DIFF_V_DIM
    nb = S // Q_BLOCK
    q = q.reshape(B, S, H, 2, Dh)
    k = k.reshape(B, S, H, 2, Dh)
    v = v.reshape(B, S, H, Dv).transpose(0, 2, 1, 3)
    k1 = k[:, :, :, 0].transpose(0, 2, 1, 3)
    k2 = k[:, :, :, 1].transpose(0, 2, 1, 3)
    q1b = q[:, :, :, 0].transpose(0, 2, 1, 3).reshape(B, H, nb, Q_BLOCK, Dh).transpose(2, 0, 1, 3, 4)
    q2b = q[:, :, :, 1].transpose(0, 2, 1, 3).reshape(B, H, nb, Q_BLOCK, Dh).transpose(2, 0, 1, 3, 4)
    starts = jnp.arange(nb, dtype=jnp.int32) * Q_BLOCK
    key_pos = jnp.arange(S, dtype=jnp.int32)
    scale = 1.0 / math.sqrt(Dh)
    table = rel_bias.astype(jnp.float32)

    def one_block(args):
        q1_blk, q2_blk, start = args
        q_pos = start + jnp.arange(Q_BLOCK, dtype=jnp.int32)
        bucket = t5_bucket(key_pos[None, :] - q_pos[:, None])
        bias = table[bucket].transpose(2, 0, 1)[None]
        s1 = jnp.einsum('bhqd,bhkd->bhqk', q1_blk, k1).astype(jnp.float32) * scale + bias
        s2 = jnp.einsum('bhqd,bhkd->bhqk', q2_blk, k2).astype(jnp.float32) * scale + bias
        attn = jax.nn.softmax(s1, axis=-1) - lam * jax.nn.softmax(s2, axis=-1)
        return jnp.einsum('bhqk,bhkd->bhqd', attn.astype(v.dtype), v)

    o = lax.map(one_block, (q1b, q2b, starts))
    o = o.transpose(1, 0, 3, 2, 4).reshape(B, S, H, Dv)
    o = rms_norm(o, subln_g) * (1.0 - lambda_init)
    return o.reshape(B, S, H * Dv)


def block_diag_linear(x, w, b):
    B, S, _ = x.shape
    xb = x.reshape(B, S, LRU_BLOCKS, LRU_BLOCK_DIM)
    return jnp.einsum('bsnd,nde->bsne', xb, w).reshape(B, S, D_LRU) + b


def centred_depthwise_conv(x, w, b):
    S = x.shape[1]
    xp = jnp.pad(x, ((0, 0), (CONV_WIDTH // 2, CONV_WIDTH - 1 - CONV_WIDTH // 2), (0, 0)))
    out = sum(xp[:, t:t + S] * w[t] for t in range(CONV_WIDTH))
    return out + b


def rglru_direction(xc, w_r, b_r, w_i, b_i, lam, reverse):
    r = jax.nn.sigmoid(block_diag_linear(xc, w_r, b_r))
    i = jax.nn.sigmoid(block_diag_linear(xc, w_i, b_i))
    log_a = -LRU_C * r * jax.nn.softplus(-lam)
    a = jnp.exp(log_a)
    u = jnp.sqrt(-jnp.expm1(2.0 * log_a)) * (i * xc)

    def combine(left, right):
        a1, b1 = left
        a2, b2 = right
        return a1 * a2, a2 * b1 + b2

    _, h = lax.associative_scan(combine, (a, u), reverse=reverse, axis=1)
    return h


def bidirectional_rglru_group(xr, gr, conv_w, conv_b, w_rg, b_rg, w_ig, b_ig, lru_lambda):
    xf = xr.astype(jnp.float32)
    xc = centred_depthwise_conv(xf, conv_w.astype(jnp.float32), conv_b.astype(jnp.float32))
    h_fwd = rglru_direction(xc, w_rg[0].astype(jnp.float32), b_rg[0].astype(jnp.float32),
                            w_ig[0].astype(jnp.float32), b_ig[0].astype(jnp.float32),
                            lru_lambda[0].astype(jnp.float32), reverse=False)
    h_bwd = rglru_direction(xc, w_rg[1].astype(jnp.float32), b_rg[1].astype(jnp.float32),
                            w_ig[1].astype(jnp.float32), b_ig[1].astype(jnp.float32),
                            lru_lambda[1].astype(jnp.float32), reverse=True)
    y = jax.nn.gelu(gr.astype(jnp.float32)) * (h_fwd + h_bwd)
    return y.astype(xr.dtype)


def setup_inputs(seed: int = 0) -> dict:
    key = jax.random.key(seed)
    ks = jax.random.split(key, 24)
    f32 = jnp.float32

    def nrm(k, shape, scale):
        return jax.random.normal(k, shape, f32) * scale

    def gain(k, shape):
        return 1.0 + 0.05 * jax.random.normal(k, shape, f32)

    u = jax.random.uniform(ks[13], (DEPTH, 2, D_LRU), f32, 0.9, 0.999)
    a0 = u ** (1.0 / LRU_C)
    lru_lambda = jnp.log(a0) - jnp.log1p(-a0)
    return {
        "x": jax.random.normal(ks[0], (BATCH, SEQ, D_MODEL), f32),
        "attn_norm_g": gain(ks[1], (DEPTH, D_MODEL)),
        "w_in": nrm(ks[2], (DEPTH, D_MODEL, D_IN), D_MODEL ** -0.5),
        "lambda_q1": nrm(ks[3], (DEPTH, DIFF_HEAD_DIM), 0.1),
        "lambda_k1": nrm(ks[4], (DEPTH, DIFF_HEAD_DIM), 0.1),
        "lambda_q2": nrm(ks[5], (DEPTH, DIFF_HEAD_DIM), 0.1),
        "lambda_k2": nrm(ks[6], (DEPTH, DIFF_HEAD_DIM), 0.1),
        "subln_g": gain(ks[7], (DEPTH, DIFF_V_DIM)),
        "rel_bias": nrm(ks[8], (NUM_BUCKETS, N_DIFF_HEADS), 0.5),
        "conv_w": nrm(ks[9], (DEPTH, CONV_WIDTH, D_LRU), CONV_WIDTH ** -0.5),
        "conv_b": nrm(ks[10], (DEPTH, D_LRU), 0.01),
        "w_rg": nrm(ks[11], (DEPTH, 2, LRU_BLOCKS, LRU_BLOCK_DIM, LRU_BLOCK_DIM), LRU_BLOCK_DIM ** -0.5),
        "b_rg": nrm(ks[12], (DEPTH, 2, D_LRU), 0.01),
        "w_ig": nrm(ks[14], (DEPTH, 2, LRU_BLOCKS, LRU_BLOCK_DIM, LRU_BLOCK_DIM), LRU_BLOCK_DIM ** -0.5),
        "b_ig": nrm(ks[15], (DEPTH, 2, D_LRU), 0.01),
        "lru_lambda": lru_lambda,
        "w_out": nrm(ks[16], (DEPTH, D_MODEL, D_MODEL), D_MODEL ** -0.5),
        "ffn_norm_g": gain(ks[17], (DEPTH, D_MODEL)),
        "w_gate": nrm(ks[18], (DEPTH, D_MODEL, D_FF), D_MODEL ** -0.5),
        "w_up": nrm(ks[19], (DEPTH, D_MODEL, D_FF), D_MODEL ** -0.5),
        "w_down": nrm(ks[20], (DEPTH, D_FF, D_MODEL), D_FF ** -0.5),
        "final_norm_g": gain(ks[21], (D_MODEL,)),
    }


def reference(x, attn_norm_g, w_in, lambda_q1, lambda_k1, lambda_q2, lambda_k2, subln_g,
              rel_bias, conv_w, conv_b, w_rg, b_rg, w_ig, b_ig, lru_lambda, w_out,
              ffn_norm_g, w_gate, w_up, w_down, final_norm_g):
    h = x
    for l in range(DEPTH):
        lambda_init = 0.8 - 0.6 * math.exp(-0.3 * l)
        n = rms_norm(h, attn_norm_g[l])
        proj = jnp.einsum('bsd,de->bse', n, w_in[l])
        q, k, v, xr, gr = jnp.split(
            proj, [D_ATTN, 2 * D_ATTN, 3 * D_ATTN, 3 * D_ATTN + D_LRU], axis=-1)
        lam = (jnp.exp(jnp.sum(lambda_q1[l].astype(jnp.float32) * lambda_k1[l].astype(jnp.float32)))
               - jnp.exp(jnp.sum(lambda_q2[l].astype(jnp.float32) * lambda_k2[l].astype(jnp.float32)))
               + lambda_init)
        y_attn = diff_attention(q, k, v, lam, rel_bias, subln_g[l], lambda_init)
        y_lru = bidirectional_rglru_group(xr, gr, conv_w[l], conv_b[l], w_rg[l], b_rg[l],
                                          w_ig[l], b_ig[l], lru_lambda[l])
        mixed = jnp.concatenate([y_attn, y_lru], axis=-1)
        h = h + jnp.einsum('bse,ed->bsd', mixed, w_out[l])
        n2 = rms_norm(h, ffn_norm_g[l])
        g = jnp.einsum('bsd,df->bsf', n2, w_gate[l])
        up = jnp.einsum('bsd,df->bsf', n2, w_up[l])
        h = h + jnp.einsum('bsf,fd->bsd', jax.nn.silu(g) * up, w_down[l])
    return rms_norm(h, final_norm_g)
```

```python
import math
import contextlib
import numpy as np
import ml_dtypes
import concourse.bass as bass
import concourse.mybir as mybir
from concourse.bass_utils import run_bass_kernel_spmd

F32 = mybir.dt.float32
BF16 = mybir.dt.bfloat16
ALU = mybir.AluOpType
AF = mybir.ActivationFunctionType

P = 128
T = 4096
D = 1024
NT = T // P
TB = 512
NB = T // TB
DC = D // P
D_IN = 2560
D_FF = 2816
FC = D_FF // P
NH = 4
EPS = 1e-6
LAMBDA_INIT = 0.8 - 0.6 * math.exp(-0.3 * 0)
GV_LEN = 1280
STRIP_W = 1153
GELU_C1 = 2.0 * math.sqrt(2.0 / math.pi)
GELU_C2 = GELU_C1 * 0.044715


class Res:
    __slots__ = ("name", "w", "rs")

    def __init__(self, name=""):
        self.name = name
        self.w = None
        self.rs = []


class DmaSem:
    def __init__(self, nc, es, name):
        self.sem = es.enter_context(nc.semaphore(name))
        self.cnt = 0


class Eng:
    def __init__(self, nc, es, e, name, is_pe=False):
        self.e = e
        self.name = name
        self.sem = es.enter_context(nc.semaphore("s_" + name))
        self.cnt = 0
        self.seen = {}
        self.is_pe = is_pe

    def wait(self, tok):
        sem, val = tok[0], tok[1]
        k = tok[3]
        if self.seen.get(k, 0) >= val:
            return
        self.e.wait_ge(sem, val)
        self.seen[k] = val


class Sched:
    def __init__(self, nc, es):
        self.nc = nc
        self.es = es
        self.pe = Eng(nc, es, nc.tensor, "pe", is_pe=True)
        self.act = Eng(nc, es, nc.scalar, "act")
        self.dve = Eng(nc, es, nc.vector, "dve")
        self.pool = Eng(nc, es, nc.gpsimd, "pool")
        self.sp = Eng(nc, es, nc.sync, "sp")
        self.engs = [self.pe, self.act, self.dve, self.pool, self.sp]
        self.dsems = []
        self.all_dsems = []
        self.nsem = 0

    def dsem(self, name, in_barrier=True):
        d = DmaSem(self.nc, self.es, name)
        if in_barrier:
            self.dsems.append(d)
        self.all_dsems.append(d)
        return d

    def _need(self, eng, tok, raw):
        if tok[2] is eng and eng.is_pe:
            return
        eng.wait(tok)

    def _deps(self, eng, reads, writes):
        for r in reads:
            if r.w is not None:
                self._need(eng, r.w, True)
        for w in writes:
            if w.w is not None:
                self._need(eng, w.w, False)
            for t in w.rs:
                self._need(eng, t, False)

    def _record(self, tok, reads, writes):
        for r in reads:
            rs = [t for t in r.rs if t[3] != tok[3]]
            rs.append(tok)
            r.rs = rs
        for w in writes:
            w.w = tok
            w.rs = []

    def op(self, eng, fn, reads=(), writes=(), inc=True):
        self._deps(eng, reads, writes)
        ins = fn(eng.e)
        if inc:
            eng.cnt += 1
            ins.then_inc(eng.sem, 1)
            tok = (eng.sem, eng.cnt, eng, eng.name)
        else:
            tok = (eng.sem, eng.cnt + 1, eng, eng.name)
        self._record(tok, reads, writes)
        return ins

    def dma(self, q, out, in_, dsem, reads=(), writes=()):
        self._deps(q, reads, writes)
        ins = q.e.dma_start(out=out, in_=in_)
        dsem.cnt += 16
        ins.then_inc(dsem.sem, 16)
        tok = (dsem.sem, dsem.cnt, None, id(dsem))
        self._record(tok, reads, writes)
        return ins

    def barrier(self, final=False):
        for e in self.engs:
            if final:
                for d in self.all_dsems:
                    if d.cnt > 0:
                        e.wait((d.sem, d.cnt, None, id(d)))
            for f in self.engs:
                if f is not e and f.cnt > 0:
                    e.wait((f.sem, f.cnt, f, f.name))
            for d in self.dsems:
                if d.cnt > 0:
                    e.wait((d.sem, d.cnt, None, id(d)))


def build_nc(stage=99, dbg=None):
    nc = bass.Bass("TRN2", target_bir_lowering=False)

    def din(name, shape, dt=F32):
        return nc.dram_tensor(name, list(shape), dt, kind="ExternalInput")

    x_d = din("x", [T, D]).ap()
    w_in_d = din("w_in", [D, D_IN]).ap()
    w_out_d = din("w_out", [D, D]).ap()
    w_gate_d = din("w_gate", [D, D_FF]).ap()
    w_up_d = din("w_up", [D, D_FF]).ap()
    w_down_d = din("w_down", [D_FF, D]).ap()
    prm_d = din("prm", [P, 64]).ap()
    lamv_d = din("lamv", [P, 256]).ap()
    gfin_d = din("gfin", [P, D]).ap()
    wbd_d = din("wbd", [P, 16, P]).ap()
    relb_d = din("relb", [32, 4]).ap()
    oh_d = din("oh", [32, GV_LEN]).ap()
    ident_d = din("ident", [P, P], BF16).ap()
    aident_d = din("aident", [P, P], BF16).ap()
    out_d = nc.dram_tensor("out", [T, D], F32, kind="ExternalOutput").ap()

    w_in_b = nc.dram_tensor("w_in_b", [D, D_IN], BF16, kind="Internal").ap()
    w_out_b = nc.dram_tensor("w_out_b", [D, D], BF16, kind="Internal").ap()
    w_gate_b = nc.dram_tensor("w_gate_b", [D, D_FF], BF16, kind="Internal").ap()
    w_up_b = nc.dram_tensor("w_up_b", [D, D_FF], BF16, kind="Internal").ap()
    w_down_b = nc.dram_tensor("w_down_b", [D_FF, D], BF16, kind="Internal").ap()
    gv_t = nc.dram_tensor("gv_scr", [NH, GV_LEN], F32, kind="Internal")
    dbg_outs = {}
    if dbg:
        for nm, shp in dbg.items():
            dbg_outs[nm] = nc.dram_tensor("dbg_" + nm, list(shp), F32, kind="ExternalOutput").ap()

    es = contextlib.ExitStack()
    with es:
        S = Sched(nc, es)
        pe, act, dve, pool, sp = S.pe, S.act, S.dve, S.pool, S.sp

        def sbuf(st, name, shape, dt, side="right"):
            return st.enter_context(nc.sbuf_tensor("sb_" + name, list(shape), dt, side=side))

        def lsbuf(st, name, shape, dt):
            return sbuf(st, name, shape, dt, side="left")

        def psum(st, name, shape, dt=F32):
            return st.enter_context(nc.psum_tensor("ps_" + name, list(shape), dt))

        prm = lsbuf(es, "prm", [P, 64], F32)
        lamv = lsbuf(es, "lamv", [P, 256], F32)
        ident = lsbuf(es, "ident", [P, P], BF16)
        aident = lsbuf(es, "aident", [P, P], BF16)
        ones_b = lsbuf(es, "ones_b", [P, P], BF16)
        ones_f = lsbuf(es, "ones_f", [P, P], F32)
        wbd = lsbuf(es, "wbd", [P, 16, P], BF16)
        cst = lsbuf(es, "cst", [P, 48], F32)
        cbias = lsbuf(es, "cbias", [P, 2 * NH], F32)
        mixL = lsbuf(es, "mixL", [P, 4, T], BF16)
        R_q = [[Res() for b in range(NB)] for h in range(NH)]
        R_k = [[Res() for b in range(NB)] for h in range(NH)]
        R_v = [Res() for i in range(NT)]
        R_ones = Res("ones"); R_prm = Res("prm"); R_cst = Res("cst"); R_strip = Res("strip"); R_const = Res("const")
        R_wbd = Res("wbd")
        R_mix = [[Res("mix%d_%d" % (c, b)) for b in range(NB)] for c in range(DC)]
        R_wscr = {k: Res("wscr_" + k) for k in ("in_lru", "in_qkv", "out", "gate", "up", "down")}

        PC_G1 = 0
        PC_G2 = 8
        PC_CW = 16
        PC_CB = 32
        PC_BR = 36
        PC_BI = 44
        PC_LL = 52
        PC_SG = 60
        CC_NLAM = 0
        CC_SG = 1
        CC_CS = 2
        CC_CS2 = 10
        CC_NB = 18
        d_prm = S.dsem("d_prm")
        d_wbd = S.dsem("d_wbd", in_barrier=False)
        cast_list = (("in_lru", w_in_d[:, 1536:D_IN], w_in_b[:, 1536:D_IN], D, 1024),
                     ("in_qkv", w_in_d[:, 0:1536], w_in_b[:, 0:1536], D, 1536), ("out", w_out_d, w_out_b, D, D),
                     ("gate", w_gate_d, w_gate_b, D, D_FF), ("up", w_up_d, w_up_b, D, D_FF),
                     ("down", w_down_d, w_down_b, D_FF, D))

        def emit_casts(keys, after=()):
            for key, src, dst, rows, cols in sorted((c for c in cast_list if c[0] in keys), key=lambda c: keys.index(c[0])):
                d_c = S.dsem("d_cast_" + key, in_barrier=False)
                for r0 in range(0, rows, 256):
                    S.dma(pool, dst[r0:r0 + 256, :], src[r0:r0 + 256, :], d_c, reads=list(after), writes=[R_wscr[key]])

        S.op(pool, lambda e: e.memset(ones_b[:], 1.0), writes=[R_ones])
        S.op(pool, lambda e: e.memset(ones_f[:], 1.0), writes=[R_ones])
        for (o, i) in ((prm, prm_d), (lamv, lamv_d), (ident, ident_d), (aident, aident_d)):
            S.dma(sp, o[:], i, d_prm, writes=[R_prm, R_const])
        emit_casts(("in_lru",), after=[R_prm])
        S.dma(pool, wbd[:], wbd_d, d_wbd, reads=[R_prm], writes=[R_wbd])
        if True:
            tmpl = lsbuf(es, "tmpl", [P, 64], F32)
            tmpc = lsbuf(es, "tmpc", [P, 16], F32)
            R_tmp = Res()
            for i in range(2):
                S.op(dve, lambda e, i=i: e.tensor_tensor(out=tmpl[:], in0=lamv[:, i * 128:i * 128 + 64],
                                                         in1=lamv[:, i * 128 + 64:i * 128 + 128], op=ALU.mult),
                     reads=[R_prm], writes=[R_tmp])
                S.op(dve, lambda e, i=i: e.reduce_sum(out=tmpc[:, i:i + 1], in_=tmpl[:], axis=mybir.AxisListType.X),
                     reads=[R_tmp], writes=[R_tmp])
            S.op(act, lambda e: e.activation(out=tmpc[:, 2:4], in_=tmpc[:, 0:2], func=AF.Exp), reads=[R_tmp], writes=[R_tmp])
            S.op(dve, lambda e: e.scalar_tensor_tensor(out=cst[:, CC_NLAM:CC_NLAM + 1], in0=tmpc[:, 3:4], scalar=-LAMBDA_INIT,
                                                       in1=tmpc[:, 2:3], op0=ALU.add, op1=ALU.subtract),
                 reads=[R_tmp], writes=[R_cst])
            S.op(dve, lambda e: e.tensor_scalar(out=cst[:, CC_SG:CC_SG + 1], in0=prm[:, PC_SG:PC_SG + 1],
                                                scalar1=(1.0 - LAMBDA_INIT), scalar2=None, op0=ALU.mult),
                 reads=[R_prm], writes=[R_cst])
            S.op(act, lambda e: e.activation(out=tmpc[:, 4:12], in_=prm[:, PC_LL:PC_LL + 8], func=AF.Exp, scale=-1.0),
                 reads=[R_prm], writes=[R_tmp])
            S.op(act, lambda e: e.activation(out=tmpc[:, 4:12], in_=tmpc[:, 4:12], func=AF.Ln, bias=1.0),
                 reads=[R_tmp], writes=[R_tmp])
            S.op(dve, lambda e: e.tensor_scalar(out=cst[:, CC_CS:CC_CS + 8], in0=tmpc[:, 4:12], scalar1=-8.0, scalar2=None,
                                                op0=ALU.mult), reads=[R_tmp], writes=[R_cst])
            S.op(dve, lambda e: e.tensor_scalar(out=cst[:, CC_CS2:CC_CS2 + 8], in0=tmpc[:, 4:12], scalar1=-16.0, scalar2=None,
                                                op0=ALU.mult), reads=[R_tmp], writes=[R_cst])
            S.op(dve, lambda e: e.tensor_scalar(out=cst[:, CC_NB:CC_NB + 16], in0=prm[:, PC_BR:PC_BR + 16], scalar1=-1.0,
                                                scalar2=None, op0=ALU.mult), reads=[R_prm], writes=[R_cst])

        def setup_strips(strip32):
            st0 = contextlib.ExitStack()
            with st0:
                relb = sbuf(st0, "relb", [32, 4], F32)
                oh = sbuf(st0, "oh", [32, GV_LEN], F32)
                gv_sb = sbuf(st0, "gv_sb", [NH, GV_LEN], F32)
                ps_gv = psum(st0, "ps_gv", [NH, 3, 512], F32)
                R_rel = Res(); R_gv = Res(); R_psgv = Res(); R_tmp = Res()
                d_gv = S.dsem("d_gv")
                d_rel = S.dsem("d_rel")
                S.dma(sp, relb[:], relb_d, d_rel, writes=[R_rel])
                S.dma(sp, oh[:], oh_d, d_rel, writes=[R_rel])
                for k in range(3):
                    n = min(512, GV_LEN - k * 512)
                    S.op(pe, lambda e, k=k, n=n: e.matmul(ps_gv[:, k, 0:n], lhsT=relb[:], rhs=oh[:, k * 512:k * 512 + n],
                                                          start=True, stop=True), reads=[R_rel], writes=[R_psgv])
                for k in range(3):
                    n = min(512, GV_LEN - k * 512)
                    S.op(dve, lambda e, k=k, n=n: e.tensor_copy(out=gv_sb[:, k * 512:k * 512 + n], in_=ps_gv[:, k, 0:n]),
                         reads=[R_psgv], writes=[R_gv])
                S.dma(sp, gv_t.ap(), gv_sb[:], d_gv, reads=[R_gv], writes=[R_tmp])
                for h in range(NH):
                    src = bass.AP(tensor=gv_t, offset=h * GV_LEN, ap=[[1, P], [1, STRIP_W]])
                    S.dma(sp, strip32[:, h, :], src, d_gv, reads=[R_tmp], writes=[R_strip])
                S.barrier()

        st1 = contextlib.ExitStack()
        with st1:
            nT = sbuf(st1, "nT", [P, DC, T], BF16)
            R_nT = [Res("nT%d" % i) for i in range(NT)]
            st1a = contextlib.ExitStack()
            with st1a:
                XR = 3
                xt = [sbuf(st1a, "xt%d" % i, [P, D], F32) for i in range(XR)]
                R_xt = [Res() for _ in range(XR)]
                d_xt = [S.dsem("d_xt%d" % i) for i in range(XR)]
                nb = [sbuf(st1a, "nb%d" % i, [P, D], BF16) for i in range(2)]
                R_nb = [Res() for _ in range(2)]
                junk = sbuf(st1a, "junk", [P, D], BF16)
                R_junk = Res()
                stat = sbuf(st1a, "stat", [P, 3 * NT], F32)
                R_st = [Res() for _ in range(NT)]
                ptr = [psum(st1a, "ptr%d" % i, [P, DC, P], BF16) for i in range(2)]
                R_ptr = [Res() for _ in range(2)]
                def norm_back(i):
                    s2 = i % 2
                    S.op(dve, lambda e: e.tensor_tensor(
                        out=nT[:, :, i * P:(i + 1) * P], in0=ptr[s2][:],
                        in1=prm[:, PC_G1:PC_G1 + DC].unsqueeze(2).to_broadcast([P, DC, P]), op=ALU.mult),
                         reads=[R_ptr[s2], R_prm], writes=[R_nT[i]])

                for i in range(NT):
                    s3, s2 = i % XR, i % 2
                    S.dma(sp, xt[s3][:], x_d[i * P:(i + 1) * P, :], d_xt[s3], writes=[R_xt[s3]])
                    S.op(act, lambda e, i=i, s3=s3: e.activation(out=junk[:], in_=xt[s3][:], func=AF.Square,
                                                                 accum_out=stat[:, i:i + 1]),
                         reads=[R_xt[s3]], writes=[R_junk, R_st[i]])
                    S.op(act, lambda e, i=i: e.activation(out=stat[:, NT + i:NT + i + 1], in_=stat[:, i:i + 1], func=AF.Ln,
                                                          scale=1.0 / D, bias=EPS), reads=[R_st[i]], writes=[R_st[i]])
                    S.op(act, lambda e, i=i: e.activation(out=stat[:, 2 * NT + i:2 * NT + i + 1],
                                                          in_=stat[:, NT + i:NT + i + 1], func=AF.Exp, scale=-0.5),
                         reads=[R_st[i]], writes=[R_st[i]])
                    S.op(dve, lambda e, i=i, s3=s3, s2=s2: e.tensor_scalar(out=nb[s2][:], in0=xt[s3][:],
                                                                          scalar1=stat[:, 2 * NT + i:2 * NT + i + 1],
                                                                          scalar2=None, op0=ALU.mult),
                         reads=[R_xt[s3], R_st[i]], writes=[R_nb[s2]])
                    for j in range(DC):
                        S.op(pe, lambda e, j=j, s2=s2: e.transpose(out=ptr[s2][:, j, :], in_=nb[s2][:, j * P:(j + 1) * P],
                                                                   identity=ident[:]),
                             reads=[R_nb[s2], R_const], writes=[R_ptr[s2]], inc=(j == DC - 1))
                    if i >= 1:
                        norm_back(i - 1)
                norm_back(NT - 1)
                emit_casts(("in_qkv", "out", "down", "gate", "up"), after=[R_xt[(NT - 1) % XR]])
                S.barrier()
            if "nT" in dbg_outs:
                stx = contextlib.ExitStack()
                with stx:
                    tmpf = sbuf(stx, "tmpf", [P, DC, 512], F32)
                    R_t = Res(); d_dbg = S.dsem("d_dbg1")
                    for b in range(NB):
                        S.op(dve, lambda e, b=b: e.tensor_copy(out=tmpf[:], in_=nT[:, :, b * TB:(b + 1) * TB]),
                             reads=R_nT, writes=[R_t])
                        S.dma(sp, dbg_outs["nT"][:, :, b * TB:(b + 1) * TB], tmpf[:], d_dbg, reads=[R_t])
                    S.barrier()
            if stage <= 1:
                return nc
            w_in_v = w_in_b.rearrange("(j p) n -> p j n", p=P)
            st1b = contextlib.ExitStack()
            with st1b:
                XRp = sbuf(st1b, "XRp", [P, T + 4], F32)
                Gb = sbuf(st1b, "Gb", [P, T], BF16)
                XC = sbuf(st1b, "XC", [P, T], F32)
                XCb = sbuf(st1b, "XCb", [P, T], BF16)
                Hf = sbuf(st1b, "Hf", [P, T], F32)
                wxr = sbuf(st1b, "wxr", [P, DC, P], BF16)
                wgr = sbuf(st1b, "wgr", [P, DC, P], BF16)
                tA = [sbuf(st1b, "tA_%d" % i, [P, TB], F32) for i in range(2)]
                tA2 = [sbuf(st1b, "tA2_%d" % i, [P, TB], F32) for i in range(2)]
                tU = [sbuf(st1b, "tU_%d" % i, [P, TB], F32) for i in range(2)]
                Rb = sbuf(st1b, "Rb", [P, T], BF16)
                IXb = sbuf(st1b, "IXb", [P, T], BF16)
                R_Rb = [Res() for _ in range(NB)]; R_IX = [Res() for _ in range(NB)]
                pz = [psum(st1b, "pz%d" % i, [P, TB], F32) for i in range(4)]
                R_XR = Res(); R_Gb = Res(); R_XC = Res(); R_XCb = Res(); R_Hf = Res()
                R_wxr = Res(); R_wgr = Res()
                R_E1 = [Res(), Res()]; R_A = [Res(), Res()]; R_A2 = [Res(), Res()]; R_E2 = [Res(), Res()]; R_U = [Res(), Res()]
                R_pz = [Res() for _ in range(4)]
                d_wxr = S.dsem("d_wxr"); d_wgr = S.dsem("d_wgr")
                S.op(dve, lambda e: e.memset(XRp[:, 0:2], 0.0), writes=[R_XR])
                S.op(dve, lambda e: e.memset(XRp[:, T + 2:T + 4], 0.0), writes=[R_XR])
                for c in range(4):
                    S.dma(sp, wxr[:], w_in_v[:, :, 1536 + c * P:1536 + (c + 1) * P], d_wxr, reads=[R_wscr["in_lru"]], writes=[R_wxr])
                    S.dma(sp, wgr[:], w_in_v[:, :, 2048 + c * P:2048 + (c + 1) * P], d_wgr, reads=[R_wscr["in_lru"]], writes=[R_wgr])
                    for b in range(NB):
                        blk = slice(b * TB, (b + 1) * TB)
                        px, pg_ = b % 2, 2 + b % 2
                        for j in range(DC):
                            S.op(pe, lambda e, j=j, px=px, blk=blk: e.matmul(pz[px][:], lhsT=wxr[:, j, :], rhs=nT[:, j, blk],
                                                                             start=(j == 0), stop=(j == DC - 1)),
                                 reads=[R_wxr] + R_nT[4 * b:4 * b + 4], writes=[R_pz[px]], inc=(j == DC - 1))
                        for j in range(DC):
                            S.op(pe, lambda e, j=j, pg_=pg_, blk=blk: e.matmul(pz[pg_][:], lhsT=wgr[:, j, :], rhs=nT[:, j, blk],
                                                                               start=(j == 0), stop=(j == DC - 1)),
                                 reads=[R_wgr] + R_nT[4 * b:4 * b + 4], writes=[R_pz[pg_]], inc=(j == DC - 1))
                        S.op(act, lambda e, px=px, b=b: e.copy(out=XRp[:, 2 + b * TB:2 + (b + 1) * TB], in_=pz[px][:]),
                             reads=[R_pz[px]], writes=[R_XR])
                        S.op(act, lambda e, pg_=pg_, blk=blk: e.copy(out=Gb[:, blk], in_=pz[pg_][:]),
                             reads=[R_pz[pg_]], writes=[R_Gb])
                    T1 = Hf[:]
                    S.op(dve, lambda e: e.tensor_tensor(out=T1, in0=Gb[:], in1=Gb[:], op=ALU.mult), reads=[R_Gb], writes=[R_Hf])
                    S.op(dve, lambda e: e.tensor_scalar(out=T1, in0=T1, scalar1=GELU_C2, scalar2=GELU_C1, op0=ALU.mult, op1=ALU.add),
                         reads=[R_Hf], writes=[R_Hf])
                    S.op(dve, lambda e: e.tensor_tensor(out=T1, in0=T1, in1=Gb[:], op=ALU.mult), reads=[R_Hf, R_Gb], writes=[R_Hf])
                    cw0 = PC_CW + c * 4
                    S.op(act, lambda e, c=c, cw0=cw0: e.activation(out=XC[:], in_=XRp[:, 0:T], func=AF.Identity,
                                                                   scale=prm[:, cw0:cw0 + 1], bias=prm[:, PC_CB + c:PC_CB + c + 1]),
                         reads=[R_XR, R_prm], writes=[R_XC])
                    S.op(act, lambda e: e.activation(out=T1, in_=T1, func=AF.Exp, scale=-1.0), reads=[R_Hf], writes=[R_Hf])
                    S.op(act, lambda e: e.activation(out=T1, in_=T1, func=AF.Ln, bias=1.0), reads=[R_Hf], writes=[R_Hf])
                    S.op(act, lambda e: e.activation(out=T1, in_=T1, func=AF.Exp, scale=-1.0), reads=[R_Hf], writes=[R_Hf])
                    for tap in range(1, 4):
                        S.op(dve, lambda e, tap=tap, cw0=cw0: e.scalar_tensor_tensor(
                            out=XC[:], in0=XRp[:, tap:tap + T], scalar=prm[:, cw0 + tap:cw0 + tap + 1], in1=XC[:],
                            op0=ALU.mult, op1=ALU.add), reads=[R_XR, R_XC, R_prm], writes=[R_XC])
                    S.op(act, lambda e: e.copy(out=XCb[:], in_=XC[:]), reads=[R_XC], writes=[R_XCb])
                    HBv = XRp
                    for d in range(2):
                        m_r = d * 8 + c
                        m_i = d * 8 + 4 + c
                        col = d * 4 + c
                        order = list(range(NB)) if d == 0 else list(range(NB - 1, -1, -1))
                        for bi, b in enumerate(order):
                            blk = slice(b * TB, (b + 1) * TB)
                            pr, pi_ = bi % 2, 2 + bi % 2
                            S.op(pe, lambda e, pr=pr, blk=blk: e.matmul(pz[pr][:], lhsT=wbd[:, m_r, :], rhs=XCb[:, blk], start=True, stop=True),
                                 reads=[R_wbd, R_XCb], writes=[R_pz[pr]])
                            S.op(pe, lambda e, pi_=pi_, blk=blk: e.matmul(pz[pi_][:], lhsT=wbd[:, m_i, :], rhs=XCb[:, blk], start=True, stop=True),
                                 reads=[R_wbd, R_XCb], writes=[R_pz[pi_]])
                            S.op(act, lambda e, pr=pr, blk=blk: e.activation(out=Rb[:, blk], in_=pz[pr][:], func=AF.Sigmoid,
                                                                             bias=prm[:, PC_BR + col:PC_BR + col + 1]),
                                 reads=[R_pz[pr], R_prm], writes=[R_Rb[b]])
                            S.op(act, lambda e, pi_=pi_, blk=blk: e.activation(out=IXb[:, blk], in_=pz[pi_][:], func=AF.Sigmoid,
                                                                               bias=prm[:, PC_BI + col:PC_BI + col + 1]),
                                 reads=[R_pz[pi_], R_prm], writes=[R_IX[b]])
                            S.op(dve, lambda e, blk=blk: e.tensor_tensor(out=IXb[:, blk], in0=IXb[:, blk], in1=XC[:, blk], op=ALU.mult),
                                 reads=[R_IX[b], R_XC], writes=[R_IX[b]])
                        if d == 0:
                            S.op(dve, lambda e: e.tensor_tensor(out=Gb[:], in0=Gb[:], in1=T1, op=ALU.mult), reads=[R_Hf, R_Gb], writes=[R_Gb])
                        for bi, b in enumerate(order):
                            blk = slice(b * TB, (b + 1) * TB)
                            k2 = bi % 2
                            A, A2, U = tA[k2], tA2[k2], tU[k2]
                            rA, rA2, rU = R_A[k2], R_A2[k2], R_U[k2]
                            S.op(act, lambda e, A=A, blk=blk: e.activation(out=A[:], in_=Rb[:, blk], func=AF.Exp,
                                                                           scale=cst[:, CC_CS + col:CC_CS + col + 1]),
                                 reads=[R_Rb[b], R_cst], writes=[rA])
                            S.op(act, lambda e, A2=A2, blk=blk: e.activation(out=A2[:], in_=Rb[:, blk], func=AF.Exp,
                                                                             scale=cst[:, CC_CS2 + col:CC_CS2 + col + 1]),
                                 reads=[R_Rb[b], R_cst], writes=[rA2])
                            S.op(act, lambda e, A2=A2: e.activation(out=A2[:], in_=A2[:], func=AF.Ln, scale=-1.0, bias=1.0),
                                 reads=[rA2], writes=[rA2])
                            S.op(act, lambda e, A2=A2: e.activation(out=A2[:], in_=A2[:], func=AF.Exp, scale=0.5), reads=[rA2], writes=[rA2])
                            S.op(dve, lambda e, blk=blk, U=U, A2=A2: e.tensor_tensor(out=U[:], in0=A2[:], in1=IXb[:, blk], op=ALU.mult),
                                 reads=[rA2, R_IX[b]], writes=[rU])
                            if d == 0:
                                init = 0.0 if b == 0 else Hf[:, b * TB - 1:b * TB]
                                S.op(dve, lambda e, blk=blk, init=init, A=A, U=U: e.tensor_tensor_scan(
                                    out=Hf[:, blk], data0=A[:], data1=U[:], initial=init, op0=ALU.mult, op1=ALU.add),
                                     reads=[rA, rU, R_Hf], writes=[R_Hf])
                            else:
                                lo, hi = 2 + b * TB, 2 + (b + 1) * TB
                                init = 0.0 if bi == 0 else HBv[:, hi:hi + 1]
                                S.op(dve, lambda e, lo=lo, hi=hi, init=init, A=A, U=U: e.tensor_tensor_scan(
                                    out=HBv[:, hi - 1:lo - 1:-1], data0=A[:, ::-1], data1=U[:, ::-1], initial=init,
                                    op0=ALU.mult, op1=ALU.add),
                                     reads=[rA, rU, R_XR], writes=[R_XR])
                    S.op(dve, lambda e: e.tensor_tensor(out=Hf[:], in0=Hf[:], in1=HBv[:, 2:T + 2], op=ALU.add),
                         reads=[R_Hf, R_XR], writes=[R_Hf])
                    S.op(dve, lambda e, c=c: e.tensor_tensor(out=mixL[:, c, :], in0=Hf[:], in1=Gb[:], op=ALU.mult),
                         reads=[R_Hf, R_Gb], writes=R_mix[4 + c])
                S.barrier()
            if stage <= 2:
                return nc
            st_qkv = contextlib.ExitStack()
            st_qkv.__enter__()
            qT = lsbuf(st_qkv, "qT", [P, NH, T], BF16)
            kT = lsbuf(st_qkv, "kT", [P, NH, T], BF16)
            Vt = lsbuf(st_qkv, "Vt", [P, NT, 512], BF16)
            st1c = contextlib.ExitStack()
            with st1c:
                pq = [psum(st1c, "pq%d" % i, [P, TB], F32) for i in range(4)]
                R_pq = [Res() for _ in range(4)]
                stv = contextlib.ExitStack()
                with stv:
                    wv = sbuf(stv, "wv", [P, DC, 256], BF16)
                    R_wv = Res(); d_wv = S.dsem("d_wv")
                    for vh in range(2):
                        S.dma(sp, wv[:], w_in_v[:, :, 1024 + vh * 256:1024 + (vh + 1) * 256], d_wv, reads=[R_wscr["in_qkv"]], writes=[R_wv])
                        for i in range(NT):
                            pp = i % 4
                            for j in range(DC):
                                S.op(pe, lambda e, j=j, pp=pp, i=i: e.matmul(pq[pp][:, 0:256], lhsT=nT[:, j, i * P:(i + 1) * P], rhs=wv[:, j, :],
                                                                           start=(j == 0), stop=(j == DC - 1)),
                                     reads=[R_wv, R_nT[i]], writes=[R_pq[pp]], inc=(j == DC - 1))
                            vcols = slice(vh * 256, (vh + 1) * 256)
                            if i % 2 == 0:
                                S.op(act, lambda e, pp=pp, i=i, vcols=vcols: e.copy(out=Vt[:, i, vcols], in_=pq[pp][:, 0:256]),
                                     reads=[R_pq[pp]], writes=[R_v[i]])
                            else:
                                S.op(dve, lambda e, pp=pp, i=i, vcols=vcols: e.tensor_copy(out=Vt[:, i, vcols], in_=pq[pp][:, 0:256]),
                                     reads=[R_pq[pp]], writes=[R_v[i]])
                    S.barrier()
                wqk = [sbuf(st1c, "wqk%d" % i, [P, DC, P], BF16) for i in range(2)]
                R_wqk = [Res(), Res()]
                d_wqk = [S.dsem("d_wqk0"), S.dsem("d_wqk1")]
                n = 0
                for h in range(NH):
                    for typ in range(2):
                        ws = n % 2
                        c0 = typ * 512 + h * P
                        S.dma(sp, wqk[ws][:], w_in_v[:, :, c0:c0 + P], d_wqk[ws], reads=[R_wscr["in_qkv"]], writes=[R_wqk[ws]])
                        for b in range(NB):
                            blk = slice(b * TB, (b + 1) * TB)
                            pp = (n * NB + b) % 4
                            for j in range(DC):
                                S.op(pe, lambda e, j=j, pp=pp, blk=blk, ws=ws: e.matmul(pq[pp][:], lhsT=wqk[ws][:, j, :], rhs=nT[:, j, blk],
                                                                                       start=(j == 0), stop=(j == DC - 1)),
                                     reads=[R_wqk[ws]] + R_nT[4 * b:4 * b + 4], writes=[R_pq[pp]], inc=(j == DC - 1))
                            if typ == 0:
                                S.op(act, lambda e, pp=pp, blk=blk, h=h: e.activation(out=qT[:, h, blk], in_=pq[pp][:], func=AF.Copy, scale=0.125),
                                     reads=[R_pq[pp]], writes=[R_q[h][b]])
                            else:
                                S.op(dve, lambda e, pp=pp, blk=blk, h=h: e.tensor_copy(out=kT[:, h, blk], in_=pq[pp][:]),
                                     reads=[R_pq[pp]], writes=[R_k[h][b]])
                        n += 1
                S.barrier()
        if stage <= 3:
            st_qkv.close()
            return nc
        st_mix = contextlib.ExitStack()
        st_mix.__enter__()
        mixA = sbuf(st_mix, "mixA", [P, 4, T], BF16)
        mixs = (mixA, mixL)
        strip32 = lsbuf(st_qkv, "strip32", [P, NH, STRIP_W], F32)
        setup_strips(strip32)
        st2 = contextlib.ExitStack()
        with st2:
            NPT = 4
            NSL = 3
            PT = [sbuf(st2, "PT%d" % i, [P, 2, TB], BF16) for i in range(NPT)]
            accb = [sbuf(st2, "accb0", [P, 2, TB], BF16)] * 2
            acc32 = sbuf(st2, "acc32", [P, 2, TB], F32)
            rz = sbuf(st2, "rz", [P, 2, TB], F32)
            oo = sbuf(st2, "oo", [P, 2, TB], F32)
            R_acc = [Res()] * 2; R_a32 = Res()
            R_rz = Res(); R_oo = Res()
            ps_s = [psum(st2, "ps_s%d" % i, [P, 2, TB], F32) for i in range(NSL)]
            ps_o = psum(st2, "ps_o", [P, 2, TB], F32)
            R_sl = [Res() for _ in range(NSL)]; R_PT = [Res() for _ in range(NPT)]; R_o = Res()
            units = [(h, qb) for h in range(NH) for qb in range(NB)]
            steps = [(u, kt) for u in range(len(units)) for kt in range(NT)]

            def emit_S(g):
                u, kt = steps[g]
                h, qb = units[u]
                sl = g % NSL
                qblk = slice(qb * TB, (qb + 1) * TB)
                ktl = slice(kt * P, (kt + 1) * P)
                delta = kt * P - qb * TB
                near = (-218 < delta < 602)
                for w in range(2):
                    rows = slice(w * 64, (w + 1) * 64)
                    S.op(pe, lambda e, w=w, rows=rows: e.matmul(
                        ps_s[sl][:, w, :], lhsT=kT[rows, h, ktl], rhs=qT[rows, h, qblk], start=True, stop=True),
                         reads=[R_k[h][kt // 4], R_q[h][qb]], writes=[R_sl[sl]], inc=(w == 1))
                pl = g % NPT
                if near:
                    J = delta + 640
                    S.op(dve, lambda e: e.tensor_tensor(out=ps_s[sl][:], in0=ps_s[sl][:],
                                                        in1=strip32[:, h, J:J - TB:-1].unsqueeze(1).to_broadcast([P, 2, TB]), op=ALU.add),
                         reads=[R_sl[sl], R_strip], writes=[R_sl[sl]])
                    S.op(act, lambda e: e.activation(out=PT[pl][:], in_=ps_s[sl][:], func=AF.Exp),
                         reads=[R_sl[sl]], writes=[R_PT[pl]])
                else:
                    cc = STRIP_W - 1 if delta > 0 else 0
                    S.op(act, lambda e: e.activation(out=PT[pl][:], in_=ps_s[sl][:], func=AF.Exp, bias=strip32[:, h, cc:cc + 1]),
                         reads=[R_sl[sl], R_strip], writes=[R_PT[pl]])

            def emit_PV(g):
                u, kt = steps[g]
                h, qb = units[u]
                pl = g % NPT
                for w in range(2):
                    S.op(pe, lambda e, w=w: e.matmul(ps_o[:, w, :], lhsT=Vt[:, kt, h * P:(h + 1) * P], rhs=PT[pl][:, w, :],
                                                     start=(kt == 0), stop=(kt == NT - 1)),
                         reads=[R_v[kt], R_PT[pl]], writes=[R_o], inc=(w == 1))
                j = kt % 4
                ab = (g // 4) % 2
                if j == 1:
                    pp = (g - 1) % NPT
                    S.op(dve, lambda e: e.tensor_tensor(out=accb[ab][:], in0=PT[pp][:], in1=PT[pl][:], op=ALU.add),
                         reads=[R_PT[pp], R_PT[pl]], writes=[R_acc[ab]])
                elif j >= 2:
                    S.op(dve, lambda e: e.tensor_tensor(out=accb[ab][:], in0=accb[ab][:], in1=PT[pl][:], op=ALU.add),
                         reads=[R_PT[pl], R_acc[ab]], writes=[R_acc[ab]])
                if j == 3:
                    if kt == 3:
                        S.op(dve, lambda e: e.tensor_copy(out=acc32[:], in_=accb[ab][:]), reads=[R_acc[ab]], writes=[R_a32])
                    else:
                        S.op(dve, lambda e: e.tensor_tensor(out=acc32[:], in0=acc32[:], in1=accb[ab][:], op=ALU.add),
                             reads=[R_acc[ab], R_a32], writes=[R_a32])

            def epi_A1(u):
                S.op(dve, lambda e: e.tensor_copy(out=oo[:], in_=ps_o[:]), reads=[R_o], writes=[R_oo])

            def epi_A2(u, sla):
                for w in range(2):
                    S.op(pe, lambda e, w=w: e.matmul(ps_s[sla][:, w, :], lhsT=ones_f[:], rhs=acc32[:, w, :], start=True, stop=True),
                         reads=[R_a32, R_const, R_ones], writes=[R_sl[sla]], inc=(w == 1))
                S.op(act, lambda e: e.activation(out=rz[:], in_=ps_s[sla][:], func=AF.Ln), reads=[R_sl[sla]], writes=[R_rz])
                S.op(act, lambda e: e.activation(out=rz[:], in_=rz[:], func=AF.Exp, scale=-1.0), reads=[R_rz], writes=[R_rz])
                S.op(dve, lambda e: e.tensor_tensor(out=oo[:], in0=oo[:], in1=rz[:], op=ALU.mult), reads=[R_oo, R_rz], writes=[R_oo])
                S.op(dve, lambda e: e.scalar_tensor_tensor(out=oo[:, 0, :], in0=oo[:, 1, :], scalar=cst[:, CC_NLAM:CC_NLAM + 1],
                                                           in1=oo[:, 0, :], op0=ALU.mult, op1=ALU.add),
                     reads=[R_oo, R_cst], writes=[R_oo])
                S.op(act, lambda e: e.activation(out=rz[:, 1, :], in_=oo[:, 0, :], func=AF.Square), reads=[R_oo, R_rz], writes=[R_rz])

            def epi_B(u, slb):
                h, qb = units[u]
                qblk = slice(qb * TB, (qb + 1) * TB)
                S.op(pe, lambda e: e.matmul(ps_s[slb][:, 0, :], lhsT=ones_f[:], rhs=rz[:, 1, :], start=True, stop=True),
                     reads=[R_rz, R_const, R_ones], writes=[R_sl[slb]])
                S.op(act, lambda e: e.activation(out=rz[:, 0, :], in_=ps_s[slb][:, 0, :], func=AF.Ln, scale=1.0 / P, bias=EPS),
                     reads=[R_sl[slb]], writes=[R_rz])
                S.op(act, lambda e: e.activation(out=rz[:, 0, :], in_=rz[:, 0, :], func=AF.Exp, scale=-0.5), reads=[R_rz], writes=[R_rz])
                S.op(dve, lambda e: e.scalar_tensor_tensor(out=mixA[:, h, qblk], in0=oo[:, 0, :], scalar=cst[:, CC_SG:CC_SG + 1],
                                                           in1=rz[:, 0, :], op0=ALU.mult, op1=ALU.mult),
                     reads=[R_oo, R_rz, R_cst], writes=[R_mix[h][qb]])

            pendA = None
            pendB = None
            NS = len(steps)
            DEPTH = 2
            for g in range(NS + DEPTH):
                if g < NS:
                    emit_S(g)
                if g >= DEPTH:
                    gp = g - DEPTH
                    emit_PV(gp)
                    u1, kt1 = steps[gp]
                    if kt1 == NT - 1:
                        epi_A1(u1)
                        pendA = u1
                    elif kt1 == 1 and pendA is not None:
                        epi_A2(pendA, (g + 1) % NSL)
                        pendB = pendA
                        pendA = None
                    elif kt1 == 6 and pendB is not None:
                        epi_B(pendB, (g + 1) % NSL)
                        pendB = None
            if pendA is not None:
                epi_A2(pendA, 0)
                epi_B(pendA, 1)
            S.barrier()
        st_qkv.close()
        if stage <= 4:
            st_mix.close()
            return nc
        st3 = contextlib.ExitStack()
        sbuf3 = lsbuf
        with st3:
            wout = lsbuf(st3, "wout", [P, DC, D], BF16)
            wd = lsbuf(st3, "wd", [P, FC, D], BF16)
            hblk = lsbuf(st3, "hblk", [P, 4, D], F32)
            nb2s = [lsbuf(st3, "nb2_%d" % i, [P, D], BF16) for i in range(2)]
            junk3 = lsbuf(st3, "junk3", [P, D], BF16)
            actT = lsbuf(st3, "actT", [P, FC, TB], BF16)
            wg = [lsbuf(st3, "wg%d" % i, [P, DC, 256], BF16) for i in range(2)]
            wu = [lsbuf(st3, "wu%d" % i, [P, DC, 256], BF16) for i in range(2)]
            sg = [lsbuf(st3, "sg%d" % i, [P, TB], F32) for i in range(2)]
            ot = [lsbuf(st3, "ot%d" % i, [P, D], F32) for i in range(2)]
            stat3 = lsbuf(st3, "stat3", [P, 8], F32)
            gfin = lsbuf(st3, "gfin", [P, D], F32)
            R_gfin = Res(); d_gfin = S.dsem("d_gfin")
            S.dma(sp, gfin[:], gfin_d, d_gfin, writes=[R_gfin])
            R_wout = Res(); R_wd = Res(); R_h = [Res() for _ in range(4)]; R_nb2s = [Res(), Res()]; R_j3 = Res(); R_act = [Res() for _ in range(FC)]
            R_wg = [Res(), Res()]; R_wu = [Res(), Res()]; R_sg = [Res(), Res()]; R_ot = [Res(), Res()]; R_st3 = [Res() for _ in range(4)]
            d_wout = S.dsem("d_wout"); d_wd = S.dsem("d_wd"); d_hs = [S.dsem("d_h%d" % i) for i in range(4)]
            d_wg = [S.dsem("d_wg0"), S.dsem("d_wg1")]; d_wu = [S.dsem("d_wu0"), S.dsem("d_wu1")]
            d_ot = [S.dsem("d_ot0"), S.dsem("d_ot1")]
            po = [psum(st3, "po%d" % i, [P, TB], F32) for i in range(2)]
            pg = [psum(st3, "pg%d" % i, [P, TB], F32) for i in range(2)]
            pu = [psum(st3, "pu%d" % i, [P, TB], F32) for i in range(2)]
            ptr3s = [psum(st3, "ptr3_%d" % i, [P, DC, P], BF16) for i in range(2)]
            R_po = [Res(), Res()]; R_pg = [Res(), Res()]; R_pu = [Res(), Res()]; R_ptr3s = [Res(), Res()]
            S.dma(sp, wout[:], w_out_b.rearrange("(e p) n -> p e n", p=P), d_wout, reads=[R_wscr["out"]], writes=[R_wout])
            S.dma(sp, wd[:], w_down_b.rearrange("(f p) n -> p f n", p=P), d_wd, reads=[R_wscr["down"]], writes=[R_wd])
            w_gate_v = w_gate_b.rearrange("(j p) n -> p j n", p=P)
            w_up_v = w_up_b.rearrange("(j p) n -> p j n", p=P)
            npo = 0
            not_ = 0
            nfg = 0
            for b in range(NB):
                if b == 0:
                    for s in range(4):
                        S.dma(sp, hblk[:, s, :], x_d[s * P:(s + 1) * P, :], d_hs[s], writes=[R_h[s]])
                for s in range(4):
                    tok = slice(b * TB + s * P, b * TB + (s + 1) * P)
                    for n2 in range(2):
                        pp = npo % 2; npo += 1
                        cols = slice(n2 * 512, (n2 + 1) * 512)
                        for e_ in range(DC):
                            S.op(pe, lambda e, e_=e_, pp=pp, tok=tok, cols=cols: e.matmul(po[pp][:], lhsT=mixs[e_ // 4][:, e_ % 4, tok], rhs=wout[:, e_, cols],
                                                                                        start=(e_ == 0), stop=(e_ == DC - 1)),
                                 reads=[R_wout, R_mix[e_][b]], writes=[R_po[pp]], inc=(e_ == DC - 1))
                        S.op(dve, lambda e, pp=pp, s=s, cols=cols: e.tensor_tensor(out=hblk[:, s, cols], in0=hblk[:, s, cols], in1=po[pp][:], op=ALU.add),
                             reads=[R_po[pp], R_h[s]], writes=[R_h[s]])
                def n2_front(s):
                    S.op(act, lambda e: e.activation(out=junk3[:], in_=hblk[:, s, :], func=AF.Square, accum_out=stat3[:, s:s + 1]),
                         reads=[R_h[s]], writes=[R_j3, R_st3[s]])
                    S.op(act, lambda e: e.activation(out=stat3[:, s:s + 1], in_=stat3[:, s:s + 1], func=AF.Ln, scale=1.0 / D, bias=EPS),
                         reads=[R_st3[s]], writes=[R_st3[s]])
                    S.op(act, lambda e: e.activation(out=stat3[:, s:s + 1], in_=stat3[:, s:s + 1], func=AF.Exp, scale=-0.5),
                         reads=[R_st3[s]], writes=[R_st3[s]])
                    nb2, R_nb2, ptr3, R_ptr3 = nb2s[s % 2], R_nb2s[s % 2], ptr3s[s % 2], R_ptr3s[s % 2]
                    S.op(dve, lambda e: e.tensor_scalar(out=nb2[:], in0=hblk[:, s, :], scalar1=stat3[:, s:s + 1], scalar2=None, op0=ALU.mult),
                         reads=[R_h[s], R_st3[s]], writes=[R_nb2])
                    for j in range(DC):
                        S.op(pe, lambda e, j=j: e.transpose(out=ptr3[:, j, :], in_=nb2[:, j * P:(j + 1) * P], identity=ident[:]),
                             reads=[R_nb2, R_const], writes=[R_ptr3], inc=(j == DC - 1))

                def n2_back(s):
                    tok = slice(b * TB + s * P, b * TB + (s + 1) * P)
                    ptr3, R_ptr3 = ptr3s[s % 2], R_ptr3s[s % 2]
                    for hf in range(2):
                        S.op(dve, lambda e, hf=hf: e.tensor_tensor(
                            out=mixs[hf][:, :, tok], in0=ptr3[:, 4 * hf:4 * hf + 4, :],
                            in1=prm[:, PC_G2 + 4 * hf:PC_G2 + 4 * hf + 4].unsqueeze(2).to_broadcast([P, 4, P]), op=ALU.mult),
                             reads=[R_ptr3, R_prm], writes=[R_mix[c][b] for c in range(4 * hf, 4 * hf + 4)])

                for s in range(5):
                    if s < 4:
                        n2_front(s)
                    if s >= 1:
                        n2_back(s - 1)
                blk = slice(b * TB, (b + 1) * TB)
                for fg in range(FC // 2):
                    ws = nfg % 2; nfg += 1
                    S.dma(sp, wg[ws][:], w_gate_v[:, :, fg * 256:(fg + 1) * 256], d_wg[ws], reads=[R_wscr["gate"]], writes=[R_wg[ws]])
                    S.dma(sp, wu[ws][:], w_up_v[:, :, fg * 256:(fg + 1) * 256], d_wu[ws], reads=[R_wscr["up"]], writes=[R_wu[ws]])
                    for f2 in range(2):
                        f = fg * 2 + f2
                        pp = f % 2
                        fc = slice(f2 * P, (f2 + 1) * P)
                        for j in range(DC):
                            S.op(pe, lambda e, j=j, pp=pp, ws=ws, fc=fc: e.matmul(pg[pp][:], lhsT=wg[ws][:, j, fc], rhs=mixs[j // 4][:, j % 4, blk],
                                                                                 start=(j == 0), stop=(j == DC - 1)),
                                 reads=[R_wg[ws]] + [R_mix[j][b]], writes=[R_pg[pp]], inc=(j == DC - 1))
                        for j in range(DC):
                            S.op(pe, lambda e, j=j, pp=pp, ws=ws, fc=fc: e.matmul(pu[pp][:], lhsT=wu[ws][:, j, fc], rhs=mixs[j // 4][:, j % 4, blk],
                                                                                 start=(j == 0), stop=(j == DC - 1)),
                                 reads=[R_wu[ws]] + [R_mix[j][b]], writes=[R_pu[pp]], inc=(j == DC - 1))
                        S.op(act, lambda e, pp=pp: e.activation(out=sg[pp][:], in_=pg[pp][:], func=AF.Silu), reads=[R_pg[pp]], writes=[R_sg[pp]])
                        S.op(dve, lambda e, pp=pp, f=f: e.tensor_tensor(out=actT[:, f, :], in0=pu[pp][:], in1=sg[pp][:], op=ALU.mult),
                             reads=[R_pu[pp], R_sg[pp]], writes=[R_act[f]])
                for s in range(4):
                    for n2 in range(2):
                        pp = npo % 2; npo += 1
                        cols = slice(n2 * 512, (n2 + 1) * 512)
                        for f in range(FC):
                            S.op(pe, lambda e, f=f, pp=pp, s=s, cols=cols: e.matmul(po[pp][:], lhsT=actT[:, f, s * P:(s + 1) * P], rhs=wd[:, f, cols],
                                                                                  start=(f == 0), stop=(f == FC - 1)),
                                 reads=[R_wd, R_act[f]], writes=[R_po[pp]], inc=(f == FC - 1))
                        S.op(dve, lambda e, pp=pp, s=s, cols=cols: e.tensor_tensor(out=hblk[:, s, cols], in0=hblk[:, s, cols], in1=po[pp][:], op=ALU.add),
                             reads=[R_po[pp], R_h[s]], writes=[R_h[s]])
                    S.op(act, lambda e, s=s: e.activation(out=junk3[:], in_=hblk[:, s, :], func=AF.Square, accum_out=stat3[:, 4 + s:5 + s]),
                         reads=[R_h[s]], writes=[R_j3, R_st3[s]])
                    S.op(act, lambda e, s=s: e.activation(out=stat3[:, 4 + s:5 + s], in_=stat3[:, 4 + s:5 + s], func=AF.Ln, scale=1.0 / D, bias=EPS),
                         reads=[R_st3[s]], writes=[R_st3[s]])
                    S.op(act, lambda e, s=s: e.activation(out=stat3[:, 4 + s:5 + s], in_=stat3[:, 4 + s:5 + s], func=AF.Exp, scale=-0.5),
                         reads=[R_st3[s]], writes=[R_st3[s]])
                    os_ = not_ % 2; not_ += 1
                    S.op(dve, lambda e, s=s, os_=os_: e.scalar_tensor_tensor(out=ot[os_][:], in0=hblk[:, s, :], scalar=stat3[:, 4 + s:5 + s],
                                                                            in1=gfin[:], op0=ALU.mult, op1=ALU.mult),
                         reads=[R_h[s], R_st3[s], R_gfin], writes=[R_ot[os_]])
                    S.dma(pool, out_d[b * TB + s * P:b * TB + (s + 1) * P, :], ot[os_][:], d_ot[os_], reads=[R_ot[os_]])
                    if b + 1 < NB:
                        S.dma(sp, hblk[:, s, :], x_d[(b + 1) * TB + s * P:(b + 1) * TB + (s + 1) * P, :], d_hs[s], writes=[R_h[s]])
            S.barrier(final=True)
        st_mix.close()
    return nc


def t5_bucket_np(rel):
    half = 16
    ret = np.where(rel > 0, half, 0)
    n = np.abs(rel)
    max_exact = half // 2
    nf = np.maximum(n, 1).astype(np.float32)
    large = max_exact + (np.log(nf / max_exact) / math.log(128 / max_exact) * (half - max_exact)).astype(np.int32)
    large = np.minimum(large, half - 1)
    return ret + np.where(n < max_exact, n, large)


def host_constants():
    m = np.arange(GV_LEN)
    bucket = t5_bucket_np(m - 640)
    oh = np.zeros((32, GV_LEN), np.float32)
    oh[bucket, m] = 1.0
    ident = np.eye(P, dtype=np.float32).astype(ml_dtypes.bfloat16)
    aident = np.ascontiguousarray(np.eye(P, dtype=np.float32)[::-1]).astype(ml_dtypes.bfloat16)
    return oh, ident, aident


def pack_inputs(inp):
    f = np.float32
    prm = np.zeros((P, 64), f)
    prm[:, 0:8] = inp["attn_norm_g"][0].reshape(DC, P).T
    prm[:, 8:16] = inp["ffn_norm_g"][0].reshape(DC, P).T
    cw = inp["conv_w"][0]
    prm[:, 16:32] = cw.reshape(4, 4, P).transpose(2, 1, 0).reshape(P, 16)
    prm[:, 32:36] = inp["conv_b"][0].reshape(4, P).T
    prm[:, 36:44] = inp["b_rg"][0].reshape(2, 4, P).transpose(2, 0, 1).reshape(P, 8)
    prm[:, 44:52] = inp["b_ig"][0].reshape(2, 4, P).transpose(2, 0, 1).reshape(P, 8)
    prm[:, 52:60] = inp["lru_lambda"][0].reshape(2, 4, P).transpose(2, 0, 1).reshape(P, 8)
    prm[:, 60] = inp["subln_g"][0]
    lamv = np.zeros((P, 256), f)
    lamv[:, 0:64] = inp["lambda_q1"][0][None]
    lamv[:, 64:128] = inp["lambda_k1"][0][None]
    lamv[:, 128:192] = inp["lambda_q2"][0][None]
    lamv[:, 192:256] = inp["lambda_k2"][0][None]
    gfin = np.ascontiguousarray(np.broadcast_to(inp["final_norm_g"][None, :], (P, D))).astype(f)
    wbd = np.zeros((P, 16, P), f)
    for d in range(2):
        for typ, key in enumerate(("w_rg", "w_ig")):
            w = inp[key][0, d]
            for c in range(4):
                m = d * 8 + typ * 4 + c
                wbd[0:64, m, 0:64] = w[2 * c]
                wbd[64:128, m, 64:128] = w[2 * c + 1]
    oh, ident, aident = host_constants()
    shared = {
        "w_in": np.ascontiguousarray(inp["w_in"][0]), "w_out": np.ascontiguousarray(inp["w_out"][0]),
        "w_gate": np.ascontiguousarray(inp["w_gate"][0]), "w_up": np.ascontiguousarray(inp["w_up"][0]),
        "w_down": np.ascontiguousarray(inp["w_down"][0]),
        "prm": prm, "lamv": lamv, "gfin": gfin, "wbd": wbd,
        "relb": np.ascontiguousarray(inp["rel_bias"]).astype(f), "oh": oh, "ident": ident, "aident": aident,
    }
    return shared


def kernel(**inputs):
    inp = {k: np.asarray(v) for k, v in inputs.items()}
    shared = pack_inputs(inp)
    nc = build_nc()
    x = inp["x"]
    in_maps = []
    for c in range(8):
        m = dict(shared)
        m["x"] = np.ascontiguousarray(x[c])
        in_maps.append(m)
    res = run_bass_kernel_spmd(nc, in_maps, core_ids=list(range(8)))
    out = np.stack([np.asarray(r["out"]) for r in res.results], axis=0)
    return out.astype(np.float32)
```

```python
import math
import contextlib
import numpy as np
import ml_dtypes
import concourse.bass as bass
import concourse.mybir as mybir
from concourse.bass_utils import run_bass_kernel_spmd

F32 = mybir.dt.float32
BF16 = mybir.dt.bfloat16
ALU = mybir.AluOpType
AF = mybir.ActivationFunctionType

P = 128
T = 4096
D = 1024
NT = T // P
TB = 512
NB = T // TB
DC = D // P
D_IN = 2560
D_FF = 2816
FC = D_FF // P
NH = 4
EPS = 1e-6
LAMBDA_INIT = 0.8 - 0.6 * math.exp(-0.3 * 0)
GV_LEN = 1280
STRIP_W = 1153
GELU_C1 = 2.0 * math.sqrt(2.0 / math.pi)
GELU_C2 = GELU_C1 * 0.044715


class Res:
    __slots__ = ("name", "w", "rs")

    def __init__(self, name=""):
        self.name = name
        self.w = None
        self.rs = []


class DmaSem:
    def __init__(self, nc, es, name):
        self.sem = es.enter_context(nc.semaphore(name))
        self.cnt = 0


class Eng:
    def __init__(self, nc, es, e, name, is_pe=False):
        self.e = e
        self.name = name
        self.sem = es.enter_context(nc.semaphore("s_" + name))
        self.cnt = 0
        self.seen = {}
        self.is_pe = is_pe

    def wait(self, tok):
        sem, val = tok[0], tok[1]
        k = tok[3]
        if self.seen.get(k, 0) >= val:
            return
        self.e.wait_ge(sem, val)
        self.seen[k] = val


class Sched:
    def __init__(self, nc, es):
        self.nc = nc
        self.es = es
        self.pe = Eng(nc, es, nc.tensor, "pe", is_pe=True)
        self.act = Eng(nc, es, nc.scalar, "act")
        self.dve = Eng(nc, es, nc.vector, "dve")
        self.pool = Eng(nc, es, nc.gpsimd, "pool")
        self.sp = Eng(nc, es, nc.sync, "sp")
        self.engs = [self.pe, self.act, self.dve, self.pool, self.sp]
        self.dsems = []
        self.all_dsems = []
        self.nsem = 0

    def dsem(self, name, in_barrier=True):
        d = DmaSem(self.nc, self.es, name)
        if in_barrier:
            self.dsems.append(d)
        self.all_dsems.append(d)
        return d

    def _need(self, eng, tok, raw):
        if tok[2] is eng and eng.is_pe:
            return
        eng.wait(tok)

    def _deps(self, eng, reads, writes):
        for r in reads:
            if r.w is not None:
                self._need(eng, r.w, True)
        for w in writes:
            if w.w is not None:
                self._need(eng, w.w, False)
            for t in w.rs:
                self._need(eng, t, False)

    def _record(self, tok, reads, writes):
        for r in reads:
            rs = [t for t in r.rs if t[3] != tok[3]]
            rs.append(tok)
            r.rs = rs
        for w in writes:
            w.w = tok
            w.rs = []

    def op(self, eng, fn, reads=(), writes=(), inc=True):
        self._deps(eng, reads, writes)
        ins = fn(eng.e)
        if inc:
            eng.cnt += 1
            ins.then_inc(eng.sem, 1)
            tok = (eng.sem, eng.cnt, eng, eng.name)
        else:
            tok = (eng.sem, eng.cnt + 1, eng, eng.name)
        self._record(tok, reads, writes)
        return ins

    def dma(self, q, out, in_, dsem, reads=(), writes=()):
        self._deps(q, reads, writes)
        ins = q.e.dma_start(out=out, in_=in_)
        dsem.cnt += 16
        ins.then_inc(dsem.sem, 16)
        tok = (dsem.sem, dsem.cnt, None, id(dsem))
        self._record(tok, reads, writes)
        return ins

    def barrier(self, final=False):
        for e in self.engs:
            if final:
                for d in self.all_dsems:
                    if d.cnt > 0:
                        e.wait((d.sem, d.cnt, None, id(d)))
            for f in self.engs:
                if f is not e and f.cnt > 0:
                    e.wait((f.sem, f.cnt, f, f.name))
            for d in self.dsems:
                if d.cnt > 0:
                    e.wait((d.sem, d.cnt, None, id(d)))


def build_nc(stage=99, dbg=None):
    nc = bass.Bass("TRN2", target_bir_lowering=False)

    def din(name, shape, dt=F32):
        return nc.dram_tensor(name, list(shape), dt, kind="ExternalInput")

    x_d = din("x", [T, D]).ap()
    w_in_d = din("w_in", [D, D_IN]).ap()
    w_out_d = din("w_out", [D, D]).ap()
    w_gate_d = din("w_gate", [D, D_FF]).ap()
    w_up_d = din("w_up", [D, D_FF]).ap()
    w_down_d = din("w_down", [D_FF, D]).ap()
    prm_d = din("prm", [P, 64]).ap()
    lamv_d = din("lamv", [P, 256]).ap()
    gfin_d = din("gfin", [P, D]).ap()
    wbd_d = din("wbd", [P, 16, P]).ap()
    relb_d = din("relb", [32, 4]).ap()
    oh_d = din("oh", [32, GV_LEN]).ap()
    ident_d = din("ident", [P, P], BF16).ap()
    aident_d = din("aident", [P, P], BF16).ap()
    out_d = nc.dram_tensor("out", [T, D], F32, kind="ExternalOutput").ap()

    w_in_b = nc.dram_tensor("w_in_b", [D, D_IN], BF16, kind="Internal").ap()
    w_out_b = nc.dram_tensor("w_out_b", [D, D], BF16, kind="Internal").ap()
    w_gate_b = nc.dram_tensor("w_gate_b", [D, D_FF], BF16, kind="Internal").ap()
    w_up_b = nc.dram_tensor("w_up_b", [D, D_FF], BF16, kind="Internal").ap()
    w_down_b = nc.dram_tensor("w_down_b", [D_FF, D], BF16, kind="Internal").ap()
    gv_t = nc.dram_tensor("gv_scr", [NH, GV_LEN], F32, kind="Internal")
    dbg_outs = {}
    if dbg:
        for nm, shp in dbg.items():
            dbg_outs[nm] = nc.dram_tensor("dbg_" + nm, list(shp), F32, kind="ExternalOutput").ap()

    es = contextlib.ExitStack()
    with es:
        S = Sched(nc, es)
        pe, act, dve, pool, sp = S.pe, S.act, S.dve, S.pool, S.sp

        def sbuf(st, name, shape, dt, side="right"):
            return st.enter_context(nc.sbuf_tensor("sb_" + name, list(shape), dt, side=side))

        def lsbuf(st, name, shape, dt):
            return sbuf(st, name, shape, dt, side="left")

        def psum(st, name, shape, dt=F32):
            return st.enter_context(nc.psum_tensor("ps_" + name, list(shape), dt))

        prm = lsbuf(es, "prm", [P, 64], F32)
        lamv = lsbuf(es, "lamv", [P, 256], F32)
        ident = lsbuf(es, "ident", [P, P], BF16)
        aident = lsbuf(es, "aident", [P, P], BF16)
        ones_b = lsbuf(es, "ones_b", [P, P], BF16)
        ones_f = lsbuf(es, "ones_f", [P, P], F32)
        wbd = lsbuf(es, "wbd", [P, 16, P], BF16)
        cst = lsbuf(es, "cst", [P, 48], F32)
        cbias = lsbuf(es, "cbias", [P, 2 * NH], F32)
        mixL = lsbuf(es, "mixL", [P, 4, T], BF16)
        R_q = [[Res() for b in range(NB)] for h in range(NH)]
        R_k = [[Res() for b in range(NB)] for h in range(NH)]
        R_v = [Res() for i in range(NT)]
        R_ones = Res("ones"); R_prm = Res("prm"); R_cst = Res("cst"); R_strip = Res("strip"); R_const = Res("const")
        R_wbd = Res("wbd")
        R_mix = [[Res("mix%d_%d" % (c, b)) for b in range(NB)] for c in range(DC)]
        R_wscr = {k: Res("wscr_" + k) for k in ("in_lru", "in_qkv", "out", "gate", "up", "down")}

        PC_G1 = 0
        PC_G2 = 8
        PC_CW = 16
        PC_CB = 32
        PC_BR = 36
        PC_BI = 44
        PC_LL = 52
        PC_SG = 60
        CC_NLAM = 0
        CC_SG = 1
        CC_CS = 2
        CC_CS2 = 10
        CC_NB = 18
        d_prm = S.dsem("d_prm")
        d_wbd = S.dsem("d_wbd", in_barrier=False)
        cast_list = (("in_lru", w_in_d[:, 1536:D_IN], w_in_b[:, 1536:D_IN], D, 1024),
                     ("in_qkv", w_in_d[:, 0:1536], w_in_b[:, 0:1536], D, 1536), ("out", w_out_d, w_out_b, D, D),
                     ("gate", w_gate_d, w_gate_b, D, D_FF), ("up", w_up_d, w_up_b, D, D_FF),
                     ("down", w_down_d, w_down_b, D_FF, D))

        def emit_casts(keys, after=()):
            for key, src, dst, rows, cols in sorted((c for c in cast_list if c[0] in keys), key=lambda c: keys.index(c[0])):
                d_c = S.dsem("d_cast_" + key, in_barrier=False)
                for r0 in range(0, rows, 256):
                    S.dma(pool, dst[r0:r0 + 256, :], src[r0:r0 + 256, :], d_c, reads=list(after), writes=[R_wscr[key]])

        S.op(pool, lambda e: e.memset(ones_b[:], 1.0), writes=[R_ones])
        S.op(pool, lambda e: e.memset(ones_f[:], 1.0), writes=[R_ones])
        for (o, i) in ((prm, prm_d), (lamv, lamv_d), (ident, ident_d), (aident, aident_d)):
            S.dma(sp, o[:], i, d_prm, writes=[R_prm, R_const])
        emit_casts(("in_lru",), after=[R_prm])
        S.dma(pool, wbd[:], wbd_d, d_wbd, reads=[R_prm], writes=[R_wbd])
        if True:
            tmpl = lsbuf(es, "tmpl", [P, 64], F32)
            tmpc = lsbuf(es, "tmpc", [P, 16], F32)
            R_tmp = Res()
            for i in range(2):
                S.op(dve, lambda e, i=i: e.tensor_tensor(out=tmpl[:], in0=lamv[:, i * 128:i * 128 + 64],
                                                         in1=lamv[:, i * 128 + 64:i * 128 + 128], op=ALU.mult),
                     reads=[R_prm], writes=[R_tmp])
                S.op(dve, lambda e, i=i: e.reduce_sum(out=tmpc[:, i:i + 1], in_=tmpl[:], axis=mybir.AxisListType.X),
                     reads=[R_tmp], writes=[R_tmp])
            S.op(act, lambda e: e.activation(out=tmpc[:, 2:4], in_=tmpc[:, 0:2], func=AF.Exp), reads=[R_tmp], writes=[R_tmp])
            S.op(dve, lambda e: e.scalar_tensor_tensor(out=cst[:, CC_NLAM:CC_NLAM + 1], in0=tmpc[:, 3:4], scalar=-LAMBDA_INIT,
                                                       in1=tmpc[:, 2:3], op0=ALU.add, op1=ALU.subtract),
                 reads=[R_tmp], writes=[R_cst])
            S.op(dve, lambda e: e.tensor_scalar(out=cst[:, CC_SG:CC_SG + 1], in0=prm[:, PC_SG:PC_SG + 1],
                                                scalar1=(1.0 - LAMBDA_INIT), scalar2=None, op0=ALU.mult),
                 reads=[R_prm], writes=[R_cst])
            S.op(act, lambda e: e.activation(out=tmpc[:, 4:12], in_=prm[:, PC_LL:PC_LL + 8], func=AF.Exp, scale=-1.0),
                 reads=[R_prm], writes=[R_tmp])
            S.op(act, lambda e: e.activation(out=tmpc[:, 4:12], in_=tmpc[:, 4:12], func=AF.Ln, bias=1.0),
                 reads=[R_tmp], writes=[R_tmp])
            S.op(dve, lambda e: e.tensor_scalar(out=cst[:, CC_CS:CC_CS + 8], in0=tmpc[:, 4:12], scalar1=-8.0, scalar2=None,
                                                op0=ALU.mult), reads=[R_tmp], writes=[R_cst])
            S.op(dve, lambda e: e.tensor_scalar(out=cst[:, CC_CS2:CC_CS2 + 8], in0=tmpc[:, 4:12], scalar1=-16.0, scalar2=None,
                                                op0=ALU.mult), reads=[R_tmp], writes=[R_cst])
            S.op(dve, lambda e: e.tensor_scalar(out=cst[:, CC_NB:CC_NB + 16], in0=prm[:, PC_BR:PC_BR + 16], scalar1=-1.0,
                                                scalar2=None, op0=ALU.mult), reads=[R_prm], writes=[R_cst])

        def setup_strips(strip32):
            st0 = contextlib.ExitStack()
            with st0:
                relb = sbuf(st0, "relb", [32, 4], F32)
                oh = sbuf(st0, "oh", [32, GV_LEN], F32)
                gv_sb = sbuf(st0, "gv_sb", [NH, GV_LEN], F32)
                ps_gv = psum(st0, "ps_gv", [NH, 3, 512], F32)
                R_rel = Res(); R_gv = Res(); R_psgv = Res(); R_tmp = Res()
                d_gv = S.dsem("d_gv")
                d_rel = S.dsem("d_rel")
                S.dma(sp, relb[:], relb_d, d_rel, writes=[R_rel])
                S.dma(sp, oh[:], oh_d, d_rel, writes=[R_rel])
                for k in range(3):
                    n = min(512, GV_LEN - k * 512)
                    S.op(pe, lambda e, k=k, n=n: e.matmul(ps_gv[:, k, 0:n], lhsT=relb[:], rhs=oh[:, k * 512:k * 512 + n],
                                                          start=True, stop=True), reads=[R_rel], writes=[R_psgv])
                for k in range(3):
                    n = min(512, GV_LEN - k * 512)
                    S.op(dve, lambda e, k=k, n=n: e.tensor_copy(out=gv_sb[:, k * 512:k * 512 + n], in_=ps_gv[:, k, 0:n]),
                         reads=[R_psgv], writes=[R_gv])
                S.dma(sp, gv_t.ap(), gv_sb[:], d_gv, reads=[R_gv], writes=[R_tmp])
                for h in range(NH):
                    src = bass.AP(tensor=gv_t, offset=h * GV_LEN, ap=[[1, P], [1, STRIP_W]])
                    S.dma(sp, strip32[:, h, :], src, d_gv, reads=[R_tmp], writes=[R_strip])
                S.barrier()

        st1 = contextlib.ExitStack()
        with st1:
            nT = sbuf(st1, "nT", [P, DC, T], BF16)
            R_nT = [Res("nT%d" % i) for i in range(NT)]
            st1a = contextlib.ExitStack()
            with st1a:
                XR = 3
                xt = [sbuf(st1a, "xt%d" % i, [P, D], F32) for i in range(XR)]
                R_xt = [Res() for _ in range(XR)]
                d_xt = [S.dsem("d_xt%d" % i) for i in range(XR)]
                nb = [sbuf(st1a, "nb%d" % i, [P, D], BF16) for i in range(2)]
                R_nb = [Res() for _ in range(2)]
                junk = sbuf(st1a, "junk", [P, D], BF16)
                R_junk = Res()
                stat = sbuf(st1a, "stat", [P, 3 * NT], F32)
                R_st = [Res() for _ in range(NT)]
                ptr = [psum(st1a, "ptr%d" % i, [P, DC, P], BF16) for i in range(2)]
                R_ptr = [Res() for _ in range(2)]
                def norm_back(i):
                    s2 = i % 2
                    S.op(dve, lambda e: e.tensor_tensor(
                        out=nT[:, :, i * P:(i + 1) * P], in0=ptr[s2][:],
                        in1=prm[:, PC_G1:PC_G1 + DC].unsqueeze(2).to_broadcast([P, DC, P]), op=ALU.mult),
                         reads=[R_ptr[s2], R_prm], writes=[R_nT[i]])

                for i in range(NT):
                    s3, s2 = i % XR, i % 2
                    S.dma(sp, xt[s3][:], x_d[i * P:(i + 1) * P, :], d_xt[s3], writes=[R_xt[s3]])
                    S.op(act, lambda e, i=i, s3=s3: e.activation(out=junk[:], in_=xt[s3][:], func=AF.Square,
                                                                 accum_out=stat[:, i:i + 1]),
                         reads=[R_xt[s3]], writes=[R_junk, R_st[i]])
                    S.op(act, lambda e, i=i: e.activation(out=stat[:, NT + i:NT + i + 1], in_=stat[:, i:i + 1], func=AF.Ln,
                                                          scale=1.0 / D, bias=EPS), reads=[R_st[i]], writes=[R_st[i]])
                    S.op(act, lambda e, i=i: e.activation(out=stat[:, 2 * NT + i:2 * NT + i + 1],
                                                          in_=stat[:, NT + i:NT + i + 1], func=AF.Exp, scale=-0.5),
                         reads=[R_st[i]], writes=[R_st[i]])
                    S.op(dve, lambda e, i=i, s3=s3, s2=s2: e.tensor_scalar(out=nb[s2][:], in0=xt[s3][:],
                                                                          scalar1=stat[:, 2 * NT + i:2 * NT + i + 1],
                                                                          scalar2=None, op0=ALU.mult),
                         reads=[R_xt[s3], R_st[i]], writes=[R_nb[s2]])
                    for j in range(DC):
                        S.op(pe, lambda e, j=j, s2=s2: e.transpose(out=ptr[s2][:, j, :], in_=nb[s2][:, j * P:(j + 1) * P],
                                                                   identity=ident[:]),
                             reads=[R_nb[s2], R_const], writes=[R_ptr[s2]], inc=(j == DC - 1))
                    if i >= 1:
                        norm_back(i - 1)
                norm_back(NT - 1)
                emit_casts(("in_qkv", "out", "down", "gate", "up"), after=[R_xt[(NT - 1) % XR]])
                S.barrier()
            if "nT" in dbg_outs:
                stx = contextlib.ExitStack()
                with stx:
                    tmpf = sbuf(stx, "tmpf", [P, DC, 512], F32)
                    R_t = Res(); d_dbg = S.dsem("d_dbg1")
                    for b in range(NB):
                        S.op(dve, lambda e, b=b: e.tensor_copy(out=tmpf[:], in_=nT[:, :, b * TB:(b + 1) * TB]),
                             reads=R_nT, writes=[R_t])
                        S.dma(sp, dbg_outs["nT"][:, :, b * TB:(b + 1) * TB], tmpf[:], d_dbg, reads=[R_t])
                    S.barrier()
            if stage <= 1:
                return nc
            w_in_v = w_in_b.rearrange("(j p) n -> p j n", p=P)
            st1b = contextlib.ExitStack()
            with st1b:
                XRp = sbuf(st1b, "XRp", [P, T + 4], F32)
                Gb = sbuf(st1b, "Gb", [P, T], BF16)
                XC = sbuf(st1b, "XC", [P, T], F32)
                XCb = sbuf(st1b, "XCb", [P, T], BF16)
                Hf = sbuf(st1b, "Hf", [P, T], F32)
                wxr = sbuf(st1b, "wxr", [P, DC, P], BF16)
                wgr = sbuf(st1b, "wgr", [P, DC, P], BF16)
                tA = [sbuf(st1b, "tA_%d" % i, [P, TB], F32) for i in range(2)]
                tA2 = [sbuf(st1b, "tA2_%d" % i, [P, TB], F32) for i in range(2)]
                tU = [sbuf(st1b, "tU_%d" % i, [P, TB], F32) for i in range(2)]
                Rb = sbuf(st1b, "Rb", [P, T], BF16)
                IXb = sbuf(st1b, "IXb", [P, T], BF16)
                R_Rb = [Res() for _ in range(NB)]; R_IX = [Res() for _ in range(NB)]
                pz = [psum(st1b, "pz%d" % i, [P, TB], F32) for i in range(4)]
                R_XR = Res(); R_Gb = Res(); R_XC = Res(); R_XCb = Res(); R_Hf = Res()
                R_wxr = Res(); R_wgr = Res()
                R_E1 = [Res(), Res()]; R_A = [Res(), Res()]; R_A2 = [Res(), Res()]; R_E2 = [Res(), Res()]; R_U = [Res(), Res()]
                R_pz = [Res() for _ in range(4)]
                d_wxr = S.dsem("d_wxr"); d_wgr = S.dsem("d_wgr")
                S.op(dve, lambda e: e.memset(XRp[:, 0:2], 0.0), writes=[R_XR])
                S.op(dve, lambda e: e.memset(XRp[:, T + 2:T + 4], 0.0), writes=[R_XR])
                for c in range(4):
                    S.dma(sp, wxr[:], w_in_v[:, :, 1536 + c * P:1536 + (c + 1) * P], d_wxr, reads=[R_wscr["in_lru"]], writes=[R_wxr])
                    S.dma(sp, wgr[:], w_in_v[:, :, 2048 + c * P:2048 + (c + 1) * P], d_wgr, reads=[R_wscr["in_lru"]], writes=[R_wgr])
                    for b in range(NB):
                        blk = slice(b * TB, (b + 1) * TB)
                        px, pg_ = b % 2, 2 + b % 2
                        for j in range(DC):
                            S.op(pe, lambda e, j=j, px=px, blk=blk: e.matmul(pz[px][:], lhsT=wxr[:, j, :], rhs=nT[:, j, blk],
                                                                             start=(j == 0), stop=(j == DC - 1)),
                                 reads=[R_wxr] + R_nT[4 * b:4 * b + 4], writes=[R_pz[px]], inc=(j == DC - 1))
                        for j in range(DC):
                            S.op(pe, lambda e, j=j, pg_=pg_, blk=blk: e.matmul(pz[pg_][:], lhsT=wgr[:, j, :], rhs=nT[:, j, blk],
                                                                               start=(j == 0), stop=(j == DC - 1)),
                                 reads=[R_wgr] + R_nT[4 * b:4 * b + 4], writes=[R_pz[pg_]], inc=(j == DC - 1))
                        S.op(act, lambda e, px=px, b=b: e.copy(out=XRp[:, 2 + b * TB:2 + (b + 1) * TB], in_=pz[px][:]),
                             reads=[R_pz[px]], writes=[R_XR])
                        S.op(act, lambda e, pg_=pg_, blk=blk: e.copy(out=Gb[:, blk], in_=pz[pg_][:]),
                             reads=[R_pz[pg_]], writes=[R_Gb])
                    T1 = Hf[:]
                    S.op(dve, lambda e: e.tensor_tensor(out=T1, in0=Gb[:], in1=Gb[:], op=ALU.mult), reads=[R_Gb], writes=[R_Hf])
                    S.op(dve, lambda e: e.tensor_scalar(out=T1, in0=T1, scalar1=GELU_C2, scalar2=GELU_C1, op0=ALU.mult, op1=ALU.add),
                         reads=[R_Hf], writes=[R_Hf])
                    S.op(dve, lambda e: e.tensor_tensor(out=T1, in0=T1, in1=Gb[:], op=ALU.mult), reads=[R_Hf, R_Gb], writes=[R_Hf])
                    cw0 = PC_CW + c * 4
                    S.op(act, lambda e, c=c, cw0=cw0: e.activation(out=XC[:], in_=XRp[:, 0:T], func=AF.Identity,
                                                                   scale=prm[:, cw0:cw0 + 1], bias=prm[:, PC_CB + c:PC_CB + c + 1]),
                         reads=[R_XR, R_prm], writes=[R_XC])
                    S.op(act, lambda e: e.activation(out=T1, in_=T1, func=AF.Exp, scale=-1.0), reads=[R_Hf], writes=[R_Hf])
                    S.op(act, lambda e: e.activation(out=T1, in_=T1, func=AF.Ln, bias=1.0), reads=[R_Hf], writes=[R_Hf])
                    S.op(act, lambda e: e.activation(out=T1, in_=T1, func=AF.Exp, scale=-1.0), reads=[R_Hf], writes=[R_Hf])
                    for tap in range(1, 4):
                        S.op(dve, lambda e, tap=tap, cw0=cw0: e.scalar_tensor_tensor(
                            out=XC[:], in0=XRp[:, tap:tap + T], scalar=prm[:, cw0 + tap:cw0 + tap + 1], in1=XC[:],
                            op0=ALU.mult, op1=ALU.add), reads=[R_XR, R_XC, R_prm], writes=[R_XC])
                    S.op(act, lambda e: e.copy(out=XCb[:], in_=XC[:]), reads=[R_XC], writes=[R_XCb])
                    HBv = XRp
                    for d in range(2):
                        m_r = d * 8 + c
                        m_i = d * 8 + 4 + c
                        col = d * 4 + c
                        order = list(range(NB)) if d == 0 else list(range(NB - 1, -1, -1))
                        for bi, b in enumerate(order):
                            blk = slice(b * TB, (b + 1) * TB)
                            pr, pi_ = bi % 2, 2 + bi % 2
                            S.op(pe, lambda e, pr=pr, blk=blk: e.matmul(pz[pr][:], lhsT=wbd[:, m_r, :], rhs=XCb[:, blk], start=True, stop=True),
                                 reads=[R_wbd, R_XCb], writes=[R_pz[pr]])
                            S.op(pe, lambda e, pi_=pi_, blk=blk: e.matmul(pz[pi_][:], lhsT=wbd[:, m_i, :], rhs=XCb[:, blk], start=True, stop=True),
                                 reads=[R_wbd, R_XCb], writes=[R_pz[pi_]])
                            S.op(act, lambda e, pr=pr, blk=blk: e.activation(out=Rb[:, blk], in_=pz[pr][:], func=AF.Sigmoid,
                                                                             bias=prm[:, PC_BR + col:PC_BR + col + 1]),
                                 reads=[R_pz[pr], R_prm], writes=[R_Rb[b]])
                            S.op(act, lambda e, pi_=pi_, blk=blk: e.activation(out=IXb[:, blk], in_=pz[pi_][:], func=AF.Sigmoid,
                                                                               bias=prm[:, PC_BI + col:PC_BI + col + 1]),
                                 reads=[R_pz[pi_], R_prm], writes=[R_IX[b]])
                            S.op(dve, lambda e, blk=blk: e.tensor_tensor(out=IXb[:, blk], in0=IXb[:, blk], in1=XC[:, blk], op=ALU.mult),
                                 reads=[R_IX[b], R_XC], writes=[R_IX[b]])
                        if d == 0:
                            S.op(dve, lambda e: e.tensor_tensor(out=Gb[:], in0=Gb[:], in1=T1, op=ALU.mult), reads=[R_Hf, R_Gb], writes=[R_Gb])
                        for bi, b in enumerate(order):
                            blk = slice(b * TB, (b + 1) * TB)
                            k2 = bi % 2
                            A, A2, U = tA[k2], tA2[k2], tU[k2]
                            rA, rA2, rU = R_A[k2], R_A2[k2], R_U[k2]
                            S.op(act, lambda e, A=A, blk=blk: e.activation(out=A[:], in_=Rb[:, blk], func=AF.Exp,
                                                                           scale=cst[:, CC_CS + col:CC_CS + col + 1]),
                                 reads=[R_Rb[b], R_cst], writes=[rA])
                            S.op(act, lambda e, A2=A2, blk=blk: e.activation(out=A2[:], in_=Rb[:, blk], func=AF.Exp,
                                                                             scale=cst[:, CC_CS2 + col:CC_CS2 + col + 1]),
                                 reads=[R_Rb[b], R_cst], writes=[rA2])
                            S.op(act, lambda e, A2=A2: e.activation(out=A2[:], in_=A2[:], func=AF.Ln, scale=-1.0, bias=1.0),
                                 reads=[rA2], writes=[rA2])
                            S.op(act, lambda e, A2=A2: e.activation(out=A2[:], in_=A2[:], func=AF.Exp, scale=0.5), reads=[rA2], writes=[rA2])
                            S.op(dve, lambda e, blk=blk, U=U, A2=A2: e.tensor_tensor(out=U[:], in0=A2[:], in1=IXb[:, blk], op=ALU.mult),
                                 reads=[rA2, R_IX[b]], writes=[rU])
                            if d == 0:
                                init = 0.0 if b == 0 else Hf[:, b * TB - 1:b * TB]
                                S.op(dve, lambda e, blk=blk, init=init, A=A, U=U: e.tensor_tensor_scan(
                                    out=Hf[:, blk], data0=A[:], data1=U[:], initial=init, op0=ALU.mult, op1=ALU.add),
                                     reads=[rA, rU, R_Hf], writes=[R_Hf])
                            else:
                                lo, hi = 2 + b * TB, 2 + (b + 1) * TB
                                init = 0.0 if bi == 0 else HBv[:, hi:hi + 1]
                                S.op(dve, lambda e, lo=lo, hi=hi, init=init, A=A, U=U: e.tensor_tensor_scan(
                                    out=HBv[:, hi - 1:lo - 1:-1], data0=A[:, ::-1], data1=U[:, ::-1], initial=init,
                                    op0=ALU.mult, op1=ALU.add),
                                     reads=[rA, rU, R_XR], writes=[R_XR])
                    S.op(dve, lambda e: e.tensor_tensor(out=Hf[:], in0=Hf[:], in1=HBv[:, 2:T + 2], op=ALU.add),
                         reads=[R_Hf, R_XR], writes=[R_Hf])
                    S.op(dve, lambda e, c=c: e.tensor_tensor(out=mixL[:, c, :], in0=Hf[:], in1=Gb[:], op=ALU.mult),
                         reads=[R_Hf, R_Gb], writes=R_mix[4 + c])
                S.barrier()
            if stage <= 2:
                return nc
            st_qkv = contextlib.ExitStack()
            st_qkv.__enter__()
            qT = lsbuf(st_qkv, "qT", [P, NH, T], BF16)
            kT = lsbuf(st_qkv, "kT", [P, NH, T], BF16)
            Vt = lsbuf(st_qkv, "Vt", [P, NT, 512], BF16)
            st1c = contextlib.ExitStack()
            with st1c:
                pq = [psum(st1c, "pq%d" % i, [P, TB], F32) for i in range(4)]
                R_pq = [Res() for _ in range(4)]
                stv = contextlib.ExitStack()
                with stv:
                    wv = sbuf(stv, "wv", [P, DC, 256], BF16)
                    R_wv = Res(); d_wv = S.dsem("d_wv")
                    for vh in range(2):
                        S.dma(sp, wv[:], w_in_v[:, :, 1024 + vh * 256:1024 + (vh + 1) * 256], d_wv, reads=[R_wscr["in_qkv"]], writes=[R_wv])
                        for i in range(NT):
                            pp = i % 4
                            for j in range(DC):
                                S.op(pe, lambda e, j=j, pp=pp, i=i: e.matmul(pq[pp][:, 0:256], lhsT=nT[:, j, i * P:(i + 1) * P], rhs=wv[:, j, :],
                                                                           start=(j == 0), stop=(j == DC - 1)),
                                     reads=[R_wv, R_nT[i]], writes=[R_pq[pp]], inc=(j == DC - 1))
                            vcols = slice(vh * 256, (vh + 1) * 256)
                            if i % 2 == 0:
                                S.op(act, lambda e, pp=pp, i=i, vcols=vcols: e.copy(out=Vt[:, i, vcols], in_=pq[pp][:, 0:256]),
                                     reads=[R_pq[pp]], writes=[R_v[i]])
                            else:
                                S.op(dve, lambda e, pp=pp, i=i, vcols=vcols: e.tensor_copy(out=Vt[:, i, vcols], in_=pq[pp][:, 0:256]),
                                     reads=[R_pq[pp]], writes=[R_v[i]])
                    S.barrier()
                wqk = [sbuf(st1c, "wqk%d" % i, [P, DC, P], BF16) for i in range(2)]
                R_wqk = [Res(), Res()]
                d_wqk = [S.dsem("d_wqk0"), S.dsem("d_wqk1")]
                n = 0
                for h in range(NH):
                    for typ in range(2):
                        ws = n % 2
                        c0 = typ * 512 + h * P
                        S.dma(sp, wqk[ws][:], w_in_v[:, :, c0:c0 + P], d_wqk[ws], reads=[R_wscr["in_qkv"]], writes=[R_wqk[ws]])
                        for b in range(NB):
                            blk = slice(b * TB, (b + 1) * TB)
                            pp = (n * NB + b) % 4
                            for j in range(DC):
                                S.op(pe, lambda e, j=j, pp=pp, blk=blk, ws=ws: e.matmul(pq[pp][:], lhsT=wqk[ws][:, j, :], rhs=nT[:, j, blk],
                                                                                       start=(j == 0), stop=(j == DC - 1)),
                                     reads=[R_wqk[ws]] + R_nT[4 * b:4 * b + 4], writes=[R_pq[pp]], inc=(j == DC - 1))
                            if typ == 0:
                                S.op(act, lambda e, pp=pp, blk=blk, h=h: e.activation(out=qT[:, h, blk], in_=pq[pp][:], func=AF.Copy, scale=0.125),
                                     reads=[R_pq[pp]], writes=[R_q[h][b]])
                            else:
                                S.op(dve, lambda e, pp=pp, blk=blk, h=h: e.tensor_copy(out=kT[:, h, blk], in_=pq[pp][:]),
                                     reads=[R_pq[pp]], writes=[R_k[h][b]])
                        n += 1
                S.barrier()
        if stage <= 3:
            st_qkv.close()
            return nc
        st_mix = contextlib.ExitStack()
        st_mix.__enter__()
        mixA = sbuf(st_mix, "mixA", [P, 4, T], BF16)
        mixs = (mixA, mixL)
        strip32 = lsbuf(st_qkv, "strip32", [P, NH, STRIP_W], F32)
        setup_strips(strip32)
        st2 = contextlib.ExitStack()
        with st2:
            NPT = 4
            NSL = 3
            PT = [sbuf(st2, "PT%d" % i, [P, 2, TB], BF16) for i in range(NPT)]
            accb = [sbuf(st2, "accb0", [P, 2, TB], BF16)] * 2
            acc32 = sbuf(st2, "acc32", [P, 2, TB], F32)
            rz = sbuf(st2, "rz", [P, 2, TB], F32)
            oo = sbuf(st2, "oo", [P, 2, TB], F32)
            R_acc = [Res()] * 2; R_a32 = Res()
            R_rz = Res(); R_oo = Res()
            ps_s = [psum(st2, "ps_s%d" % i, [P, 2, TB], F32) for i in range(NSL)]
            ps_o = psum(st2, "ps_o", [P, 2, TB], F32)
            R_sl = [Res() for _ in range(NSL)]; R_PT = [Res() for _ in range(NPT)]; R_o = Res()
            units = [(h, qb) for h in range(NH) for qb in range(NB)]
            steps = [(u, kt) for u in range(len(units)) for kt in range(NT)]

            def emit_S(g):
                u, kt = steps[g]
                h, qb = units[u]
                sl = g % NSL
                qblk = slice(qb * TB, (qb + 1) * TB)
                ktl = slice(kt * P, (kt + 1) * P)
                delta = kt * P - qb * TB
                near = (-218 < delta < 602)
                for w in range(2):
                    rows = slice(w * 64, (w + 1) * 64)
                    S.op(pe, lambda e, w=w, rows=rows: e.matmul(
                        ps_s[sl][:, w, :], lhsT=kT[rows, h, ktl], rhs=qT[rows, h, qblk], start=True, stop=True),
                         reads=[R_k[h][kt // 4], R_q[h][qb]], writes=[R_sl[sl]], inc=(w == 1))
                pl = g % NPT
                if near:
                    J = delta + 640
                    S.op(dve, lambda e: e.tensor_tensor(out=ps_s[sl][:], in0=ps_s[sl][:],
                                                        in1=strip32[:, h, J:J - TB:-1].unsqueeze(1).to_broadcast([P, 2, TB]), op=ALU.add),
                         reads=[R_sl[sl], R_strip], writes=[R_sl[sl]])
                    S.op(act, lambda e: e.activation(out=PT[pl][:], in_=ps_s[sl][:], func=AF.Exp),
                         reads=[R_sl[sl]], writes=[R_PT[pl]])
                else:
                    cc = STRIP_W - 1 if delta > 0 else 0
                    S.op(act, lambda e: e.activation(out=PT[pl][:], in_=ps_s[sl][:], func=AF.Exp, bias=strip32[:, h, cc:cc + 1]),
                         reads=[R_sl[sl], R_strip], writes=[R_PT[pl]])

            def emit_PV(g):
                u, kt = steps[g]
                h, qb = units[u]
                pl = g % NPT
                for w in range(2):
                    S.op(pe, lambda e, w=w: e.matmul(ps_o[:, w, :], lhsT=Vt[:, kt, h * P:(h + 1) * P], rhs=PT[pl][:, w, :],
                                                     start=(kt == 0), stop=(kt == NT - 1)),
                         reads=[R_v[kt], R_PT[pl]], writes=[R_o], inc=(w == 1))
                j = kt % 4
                ab = (g // 4) % 2
                if j == 1:
                    pp = (g - 1) % NPT
                    S.op(dve, lambda e: e.tensor_tensor(out=accb[ab][:], in0=PT[pp][:], in1=PT[pl][:], op=ALU.add),
                         reads=[R_PT[pp], R_PT[pl]], writes=[R_acc[ab]])
                elif j >= 2:
                    S.op(dve, lambda e: e.tensor_tensor(out=accb[ab][:], in0=accb[ab][:], in1=PT[pl][:], op=ALU.add),
                         reads=[R_PT[pl], R_acc[ab]], writes=[R_acc[ab]])
                if j == 3:
                    if kt == 3:
                        S.op(dve, lambda e: e.tensor_copy(out=acc32[:], in_=accb[ab][:]), reads=[R_acc[ab]], writes=[R_a32])
                    else:
                        S.op(dve, lambda e: e.tensor_tensor(out=acc32[:], in0=acc32[:], in1=accb[ab][:], op=ALU.add),
                             reads=[R_acc[ab], R_a32], writes=[R_a32])

            def epi_A1(u):
                S.op(act, lambda e: e.copy(out=oo[:], in_=ps_o[:]), reads=[R_o], writes=[R_oo])

            def ep_z(u, sla):
                for w in range(2):
                    S.op(pe, lambda e, w=w: e.matmul(ps_s[sla][:, w, :], lhsT=ones_f[:], rhs=acc32[:, w, :], start=True, stop=True),
                         reads=[R_a32, R_const, R_ones], writes=[R_sl[sla]], inc=(w == 1))
                S.op(act, lambda e: e.activation(out=rz[:], in_=ps_s[sla][:], func=AF.Ln), reads=[R_sl[sla]], writes=[R_rz])
                S.op(act, lambda e: e.activation(out=rz[:], in_=rz[:], func=AF.Exp, scale=-1.0), reads=[R_rz], writes=[R_rz])

            def ep_o(u):
                S.op(dve, lambda e: e.tensor_tensor(out=oo[:], in0=oo[:], in1=rz[:], op=ALU.mult), reads=[R_oo, R_rz], writes=[R_oo])
                S.op(dve, lambda e: e.scalar_tensor_tensor(out=oo[:, 0, :], in0=oo[:, 1, :], scalar=cst[:, CC_NLAM:CC_NLAM + 1],
                                                           in1=oo[:, 0, :], op0=ALU.mult, op1=ALU.add),
                     reads=[R_oo, R_cst], writes=[R_oo])

            def ep_sq(u):
                S.op(act, lambda e: e.activation(out=rz[:, 1, :], in_=oo[:, 0, :], func=AF.Square), reads=[R_oo, R_rz], writes=[R_rz])

            def ep_ss(u, slb):
                S.op(pe, lambda e: e.matmul(ps_s[slb][:, 0, :], lhsT=ones_f[:], rhs=rz[:, 1, :], start=True, stop=True),
                     reads=[R_rz, R_const, R_ones], writes=[R_sl[slb]])
                S.op(act, lambda e: e.activation(out=rz[:, 0, :], in_=ps_s[slb][:, 0, :], func=AF.Ln, scale=1.0 / P, bias=EPS),
                     reads=[R_sl[slb]], writes=[R_rz])
                S.op(act, lambda e: e.activation(out=rz[:, 0, :], in_=rz[:, 0, :], func=AF.Exp, scale=-0.5), reads=[R_rz], writes=[R_rz])

            def ep_out(u):
                h, qb = units[u]
                qblk = slice(qb * TB, (qb + 1) * TB)
                S.op(dve, lambda e: e.scalar_tensor_tensor(out=mixA[:, h, qblk], in0=oo[:, 0, :], scalar=cst[:, CC_SG:CC_SG + 1],
                                                           in1=rz[:, 0, :], op0=ALU.mult, op1=ALU.mult),
                     reads=[R_oo, R_rz, R_cst], writes=[R_mix[h][qb]])

            pend = None
            NS = len(steps)
            DEPTH = 2
            for g in range(NS + DEPTH):
                if g < NS:
                    emit_S(g)
                if g >= DEPTH:
                    gp = g - DEPTH
                    emit_PV(gp)
                    u1, kt1 = steps[gp]
                    if kt1 == NT - 1:
                        epi_A1(u1)
                        pend = u1
                        if gp == NS - 1:
                            ep_z(pend, 0); ep_o(pend); ep_sq(pend); ep_ss(pend, 1); ep_out(pend)
                            pend = None
                    elif pend is not None:
                        if kt1 == 1:
                            ep_z(pend, (g + 1) % NSL)
                        elif kt1 == 3:
                            ep_o(pend)
                        elif kt1 == 5:
                            ep_sq(pend)
                        elif kt1 == 7:
                            ep_ss(pend, (g + 1) % NSL)
                        elif kt1 == 9:
                            ep_out(pend)
                            pend = None
            S.barrier()
        st_qkv.close()
        if stage <= 4:
            st_mix.close()
            return nc
        st3 = contextlib.ExitStack()
        sbuf3 = lsbuf
        with st3:
            wout = lsbuf(st3, "wout", [P, DC, D], BF16)
            wd = lsbuf(st3, "wd", [P, FC, D], BF16)
            hblk = lsbuf(st3, "hblk", [P, 4, D], F32)
            nb2s = [lsbuf(st3, "nb2_%d" % i, [P, D], BF16) for i in range(2)]
            junk3 = lsbuf(st3, "junk3", [P, D], BF16)
            actT = lsbuf(st3, "actT", [P, FC, TB], BF16)
            wg = [lsbuf(st3, "wg%d" % i, [P, DC, 256], BF16) for i in range(2)]
            wu = [lsbuf(st3, "wu%d" % i, [P, DC, 256], BF16) for i in range(2)]
            sg = [lsbuf(st3, "sg%d" % i, [P, TB], F32) for i in range(2)]
            ot = [lsbuf(st3, "ot%d" % i, [P, D], F32) for i in range(2)]
            stat3 = lsbuf(st3, "stat3", [P, 8], F32)
            gfin = lsbuf(st3, "gfin", [P, D], F32)
            R_gfin = Res(); d_gfin = S.dsem("d_gfin")
            S.dma(sp, gfin[:], gfin_d, d_gfin, writes=[R_gfin])
            R_wout = Res(); R_wd = Res(); R_h = [Res() for _ in range(4)]; R_nb2s = [Res(), Res()]; R_j3 = Res(); R_act = [Res() for _ in range(FC)]
            R_wg = [Res(), Res()]; R_wu = [Res(), Res()]; R_sg = [Res(), Res()]; R_ot = [Res(), Res()]; R_st3 = [Res() for _ in range(4)]
            d_wout = S.dsem("d_wout"); d_wd = S.dsem("d_wd"); d_hs = [S.dsem("d_h%d" % i) for i in range(4)]
            d_wg = [S.dsem("d_wg0"), S.dsem("d_wg1")]; d_wu = [S.dsem("d_wu0"), S.dsem("d_wu1")]
            d_ot = [S.dsem("d_ot0"), S.dsem("d_ot1")]
            po = [psum(st3, "po%d" % i, [P, TB], F32) for i in range(2)]
            pg = [psum(st3, "pg%d" % i, [P, TB], F32) for i in range(2)]
            pu = [psum(st3, "pu%d" % i, [P, TB], F32) for i in range(2)]
            ptr3s = [psum(st3, "ptr3_%d" % i, [P, DC, P], BF16) for i in range(2)]
            R_po = [Res(), Res()]; R_pg = [Res(), Res()]; R_pu = [Res(), Res()]; R_ptr3s = [Res(), Res()]
            S.dma(sp, wout[:], w_out_b.rearrange("(e p) n -> p e n", p=P), d_wout, reads=[R_wscr["out"]], writes=[R_wout])
            S.dma(sp, wd[:], w_down_b.rearrange("(f p) n -> p f n", p=P), d_wd, reads=[R_wscr["down"]], writes=[R_wd])
            w_gate_v = w_gate_b.rearrange("(j p) n -> p j n", p=P)
            w_up_v = w_up_b.rearrange("(j p) n -> p j n", p=P)
            npo = 0
            not_ = 0
            nfg = 0
            for b in range(NB):
                if b == 0:
                    for s in range(4):
                        S.dma(sp, hblk[:, s, :], x_d[s * P:(s + 1) * P, :], d_hs[s], writes=[R_h[s]])
                for s in range(4):
                    tok = slice(b * TB + s * P, b * TB + (s + 1) * P)
                    for n2 in range(2):
                        pp = npo % 2; npo += 1
                        cols = slice(n2 * 512, (n2 + 1) * 512)
                        for e_ in range(DC):
                            S.op(pe, lambda e, e_=e_, pp=pp, tok=tok, cols=cols: e.matmul(po[pp][:], lhsT=mixs[e_ // 4][:, e_ % 4, tok], rhs=wout[:, e_, cols],
                                                                                        start=(e_ == 0), stop=(e_ == DC - 1)),
                                 reads=[R_wout, R_mix[e_][b]], writes=[R_po[pp]], inc=(e_ == DC - 1))
                        S.op(dve, lambda e, pp=pp, s=s, cols=cols: e.tensor_tensor(out=hblk[:, s, cols], in0=hblk[:, s, cols], in1=po[pp][:], op=ALU.add),
                             reads=[R_po[pp], R_h[s]], writes=[R_h[s]])
                def n2_front(s):
                    S.op(act, lambda e: e.activation(out=junk3[:], in_=hblk[:, s, :], func=AF.Square, accum_out=stat3[:, s:s + 1]),
                         reads=[R_h[s]], writes=[R_j3, R_st3[s]])
                    S.op(act, lambda e: e.activation(out=stat3[:, s:s + 1], in_=stat3[:, s:s + 1], func=AF.Ln, scale=1.0 / D, bias=EPS),
                         reads=[R_st3[s]], writes=[R_st3[s]])
                    S.op(act, lambda e: e.activation(out=stat3[:, s:s + 1], in_=stat3[:, s:s + 1], func=AF.Exp, scale=-0.5),
                         reads=[R_st3[s]], writes=[R_st3[s]])
                    nb2, R_nb2, ptr3, R_ptr3 = nb2s[s % 2], R_nb2s[s % 2], ptr3s[s % 2], R_ptr3s[s % 2]
                    S.op(dve, lambda e: e.tensor_scalar(out=nb2[:], in0=hblk[:, s, :], scalar1=stat3[:, s:s + 1], scalar2=None, op0=ALU.mult),
                         reads=[R_h[s], R_st3[s]], writes=[R_nb2])
                    for j in range(DC):
                        S.op(pe, lambda e, j=j: e.transpose(out=ptr3[:, j, :], in_=nb2[:, j * P:(j + 1) * P], identity=ident[:]),
                             reads=[R_nb2, R_const], writes=[R_ptr3], inc=(j == DC - 1))

                def n2_back(s):
                    tok = slice(b * TB + s * P, b * TB + (s + 1) * P)
                    ptr3, R_ptr3 = ptr3s[s % 2], R_ptr3s[s % 2]
                    for hf in range(2):
                        S.op(dve, lambda e, hf=hf: e.tensor_tensor(
                            out=mixs[hf][:, :, tok], in0=ptr3[:, 4 * hf:4 * hf + 4, :],
                            in1=prm[:, PC_G2 + 4 * hf:PC_G2 + 4 * hf + 4].unsqueeze(2).to_broadcast([P, 4, P]), op=ALU.mult),
                             reads=[R_ptr3, R_prm], writes=[R_mix[c][b] for c in range(4 * hf, 4 * hf + 4)])

                for s in range(5):
                    if s < 4:
                        n2_front(s)
                    if s >= 1:
                        n2_back(s - 1)
                blk = slice(b * TB, (b + 1) * TB)
                for fg in range(FC // 2):
                    ws = nfg % 2; nfg += 1
                    S.dma(sp, wg[ws][:], w_gate_v[:, :, fg * 256:(fg + 1) * 256], d_wg[ws], reads=[R_wscr["gate"]], writes=[R_wg[ws]])
                    S.dma(sp, wu[ws][:], w_up_v[:, :, fg * 256:(fg + 1) * 256], d_wu[ws], reads=[R_wscr["up"]], writes=[R_wu[ws]])
                    for f2 in range(2):
                        f = fg * 2 + f2
                        pp = f % 2
                        fc = slice(f2 * P, (f2 + 1) * P)
                        for j in range(DC):
                            S.op(pe, lambda e, j=j, pp=pp, ws=ws, fc=fc: e.matmul(pg[pp][:], lhsT=wg[ws][:, j, fc], rhs=mixs[j // 4][:, j % 4, blk],
                                                                                 start=(j == 0), stop=(j == DC - 1)),
                                 reads=[R_wg[ws]] + [R_mix[j][b]], writes=[R_pg[pp]], inc=(j == DC - 1))
                        for j in range(DC):
                            S.op(pe, lambda e, j=j, pp=pp, ws=ws, fc=fc: e.matmul(pu[pp][:], lhsT=wu[ws][:, j, fc], rhs=mixs[j // 4][:, j % 4, blk],
                                                                                 start=(j == 0), stop=(j == DC - 1)),
                                 reads=[R_wu[ws]] + [R_mix[j][b]], writes=[R_pu[pp]], inc=(j == DC - 1))
                        S.op(act, lambda e, pp=pp: e.activation(out=sg[pp][:], in_=pg[pp][:], func=AF.Silu), reads=[R_pg[pp]], writes=[R_sg[pp]])
                        S.op(dve, lambda e, pp=pp, f=f: e.tensor_tensor(out=actT[:, f, :], in0=pu[pp][:], in1=sg[pp][:], op=ALU.mult),
                             reads=[R_pu[pp], R_sg[pp]], writes=[R_act[f]])
                for s in range(4):
                    for n2 in range(2):
                        pp = npo % 2; npo += 1
                        cols = slice(n2 * 512, (n2 + 1) * 512)
                        for f in range(FC):
                            S.op(pe, lambda e, f=f, pp=pp, s=s, cols=cols: e.matmul(po[pp][:], lhsT=actT[:, f, s * P:(s + 1) * P], rhs=wd[:, f, cols],
                                                                                  start=(f == 0), stop=(f == FC - 1)),
                                 reads=[R_wd, R_act[f]], writes=[R_po[pp]], inc=(f == FC - 1))
                        S.op(dve, lambda e, pp=pp, s=s, cols=cols: e.tensor_tensor(out=hblk[:, s, cols], in0=hblk[:, s, cols], in1=po[pp][:], op=ALU.add),
                             reads=[R_po[pp], R_h[s]], writes=[R_h[s]])
                    S.op(act, lambda e, s=s: e.activation(out=junk3[:], in_=hblk[:, s, :], func=AF.Square, accum_out=stat3[:, 4 + s:5 + s]),
                         reads=[R_h[s]], writes=[R_j3, R_st3[s]])
                    S.op(act, lambda e, s=s: e.activation(out=stat3[:, 4 + s:5 + s], in_=stat3[:, 4 + s:5 + s], func=AF.Ln, scale=1.0 / D, bias=EPS),
                         reads=[R_st3[s]], writes=[R_st3[s]])
                    S.op(act, lambda e, s=s: e.activation(out=stat3[:, 4 + s:5 + s], in_=stat3[:, 4 + s:5 + s], func=AF.Exp, scale=-0.5),
                         reads=[R_st3[s]], writes=[R_st3[s]])
                    os_ = not_ % 2; not_ += 1
                    S.op(dve, lambda e, s=s, os_=os_: e.scalar_tensor_tensor(out=ot[os_][:], in0=hblk[:, s, :], scalar=stat3[:, 4 + s:5 + s],
                                                                            in1=gfin[:], op0=ALU.mult, op1=ALU.mult),
                         reads=[R_h[s], R_st3[s], R_gfin], writes=[R_ot[os_]])
                    S.dma(pool, out_d[b * TB + s * P:b * TB + (s + 1) * P, :], ot[os_][:], d_ot[os_], reads=[R_ot[os_]])
                    if b + 1 < NB:
                        S.dma(sp, hblk[:, s, :], x_d[(b + 1) * TB + s * P:(b + 1) * TB + (s + 1) * P, :], d_hs[s], writes=[R_h[s]])
            S.barrier(final=True)
        st_mix.close()
    return nc


def t5_bucket_np(rel):
    half = 16
    ret = np.where(rel > 0, half, 0)
    n = np.abs(rel)
    max_exact = half // 2
    nf = np.maximum(n, 1).astype(np.float32)
    large = max_exact + (np.log(nf / max_exact) / math.log(128 / max_exact) * (half - max_exact)).astype(np.int32)
    large = np.minimum(large, half - 1)
    return ret + np.where(n < max_exact, n, large)


def host_constants():
    m = np.arange(GV_LEN)
    bucket = t5_bucket_np(m - 640)
    oh = np.zeros((32, GV_LEN), np.float32)
    oh[bucket, m] = 1.0
    ident = np.eye(P, dtype=np.float32).astype(ml_dtypes.bfloat16)
    aident = np.ascontiguousarray(np.eye(P, dtype=np.float32)[::-1]).astype(ml_dtypes.bfloat16)
    return oh, ident, aident


def pack_inputs(inp):
    f = np.float32
    prm = np.zeros((P, 64), f)
    prm[:, 0:8] = inp["attn_norm_g"][0].reshape(DC, P).T
    prm[:, 8:16] = inp["ffn_norm_g"][0].reshape(DC, P).T
    cw = inp["conv_w"][0]
    prm[:, 16:32] = cw.reshape(4, 4, P).transpose(2, 1, 0).reshape(P, 16)
    prm[:, 32:36] = inp["conv_b"][0].reshape(4, P).T
    prm[:, 36:44] = inp["b_rg"][0].reshape(2, 4, P).transpose(2, 0, 1).reshape(P, 8)
    prm[:, 44:52] = inp["b_ig"][0].reshape(2, 4, P).transpose(2, 0, 1).reshape(P, 8)
    prm[:, 52:60] = inp["lru_lambda"][0].reshape(2, 4, P).transpose(2, 0, 1).reshape(P, 8)
    prm[:, 60] = inp["subln_g"][0]
    lamv = np.zeros((P, 256), f)
    lamv[:, 0:64] = inp["lambda_q1"][0][None]
    lamv[:, 64:128] = inp["lambda_k1"][0][None]
    lamv[:, 128:192] = inp["lambda_q2"][0][None]
    lamv[:, 192:256] = inp["lambda_k2"][0][None]
    gfin = np.ascontiguousarray(np.broadcast_to(inp["final_norm_g"][None, :], (P, D))).astype(f)
    wbd = np.zeros((P, 16, P), f)
    for d in range(2):
        for typ, key in enumerate(("w_rg", "w_ig")):
            w = inp[key][0, d]
            for c in range(4):
                m = d * 8 + typ * 4 + c
                wbd[0:64, m, 0:64] = w[2 * c]
                wbd[64:128, m, 64:128] = w[2 * c + 1]
    oh, ident, aident = host_constants()
    shared = {
        "w_in": np.ascontiguousarray(inp["w_in"][0]), "w_out": np.ascontiguousarray(inp["w_out"][0]),
        "w_gate": np.ascontiguousarray(inp["w_gate"][0]), "w_up": np.ascontiguousarray(inp["w_up"][0]),
        "w_down": np.ascontiguousarray(inp["w_down"][0]),
        "prm": prm, "lamv": lamv, "gfin": gfin, "wbd": wbd,
        "relb": np.ascontiguousarray(inp["rel_bias"]).astype(f), "oh": oh, "ident": ident, "aident": aident,
    }
    return shared


def kernel(**inputs):
    inp = {k: np.asarray(v) for k, v in inputs.items()}
    shared = pack_inputs(inp)
    nc = build_nc()
    x = inp["x"]
    in_maps = []
    for c in range(8):
        m = dict(shared)
        m["x"] = np.ascontiguousarray(x[c])
        in_maps.append(m)
    res = run_bass_kernel_spmd(nc, in_maps, core_ids=list(range(8)))
    out = np.stack([np.asarray(r["out"]) for r in res.results], axis=0)
    return out.astype(np.float32)
```

```python
import math
import contextlib
import numpy as np
import ml_dtypes
import concourse.bass as bass
import concourse.mybir as mybir
from concourse.bass_utils import run_bass_kernel_spmd

F32 = mybir.dt.float32
BF16 = mybir.dt.bfloat16
ALU = mybir.AluOpType
AF = mybir.ActivationFunctionType

P = 128
T = 4096
D = 1024
NT = T // P
TB = 512
NB = T // TB
DC = D // P
D_IN = 2560
D_FF = 2816
FC = D_FF // P
NH = 4
EPS = 1e-6
LAMBDA_INIT = 0.8 - 0.6 * math.exp(-0.3 * 0)
GV_LEN = 1280
STRIP_W = 1153
GELU_C1 = 2.0 * math.sqrt(2.0 / math.pi)
GELU_C2 = GELU_C1 * 0.044715


class Res:
    __slots__ = ("name", "w", "rs")

    def __init__(self, name=""):
        self.name = name
        self.w = None
        self.rs = []


class DmaSem:
    def __init__(self, nc, es, name):
        self.sem = es.enter_context(nc.semaphore(name))
        self.cnt = 0


class Eng:
    def __init__(self, nc, es, e, name, is_pe=False):
        self.e = e
        self.name = name
        self.sem = es.enter_context(nc.semaphore("s_" + name))
        self.cnt = 0
        self.seen = {}
        self.is_pe = is_pe

    def wait(self, tok):
        sem, val = tok[0], tok[1]
        k = tok[3]
        if self.seen.get(k, 0) >= val:
            return
        self.e.wait_ge(sem, val)
        self.seen[k] = val


class Sched:
    def __init__(self, nc, es):
        self.nc = nc
        self.es = es
        self.pe = Eng(nc, es, nc.tensor, "pe", is_pe=True)
        self.act = Eng(nc, es, nc.scalar, "act")
        self.dve = Eng(nc, es, nc.vector, "dve")
        self.pool = Eng(nc, es, nc.gpsimd, "pool")
        self.sp = Eng(nc, es, nc.sync, "sp")
        self.engs = [self.pe, self.act, self.dve, self.pool, self.sp]
        self.dsems = []
        self.all_dsems = []
        self.nsem = 0

    def dsem(self, name, in_barrier=True):
        d = DmaSem(self.nc, self.es, name)
        if in_barrier:
            self.dsems.append(d)
        self.all_dsems.append(d)
        return d

    def _need(self, eng, tok, raw):
        if tok[2] is eng and eng.is_pe:
            return
        eng.wait(tok)

    def _deps(self, eng, reads, writes):
        for r in reads:
            if r.w is not None:
                self._need(eng, r.w, True)
        for w in writes:
            if w.w is not None:
                self._need(eng, w.w, False)
            for t in w.rs:
                self._need(eng, t, False)

    def _record(self, tok, reads, writes):
        for r in reads:
            rs = [t for t in r.rs if t[3] != tok[3]]
            rs.append(tok)
            r.rs = rs
        for w in writes:
            w.w = tok
            w.rs = []

    def op(self, eng, fn, reads=(), writes=(), inc=True):
        self._deps(eng, reads, writes)
        ins = fn(eng.e)
        if inc:
            eng.cnt += 1
            ins.then_inc(eng.sem, 1)
            tok = (eng.sem, eng.cnt, eng, eng.name)
        else:
            tok = (eng.sem, eng.cnt + 1, eng, eng.name)
        self._record(tok, reads, writes)
        return ins

    def dma(self, q, out, in_, dsem, reads=(), writes=()):
        self._deps(q, reads, writes)
        ins = q.e.dma_start(out=out, in_=in_)
        dsem.cnt += 16
        ins.then_inc(dsem.sem, 16)
        tok = (dsem.sem, dsem.cnt, None, id(dsem))
        self._record(tok, reads, writes)
        return ins

    def barrier(self, final=False):
        for e in self.engs:
            if final:
                for d in self.all_dsems:
                    if d.cnt > 0:
                        e.wait((d.sem, d.cnt, None, id(d)))
            for f in self.engs:
                if f is not e and f.cnt > 0:
                    e.wait((f.sem, f.cnt, f, f.name))
            for d in self.dsems:
                if d.cnt > 0:
                    e.wait((d.sem, d.cnt, None, id(d)))


def build_nc(stage=99, dbg=None):
    nc = bass.Bass("TRN2", target_bir_lowering=False)

    def din(name, shape, dt=F32):
        return nc.dram_tensor(name, list(shape), dt, kind="ExternalInput")

    x_d = din("x", [T, D]).ap()
    w_in_d = din("w_in", [D, D_IN]).ap()
    w_out_d = din("w_out", [D, D]).ap()
    w_gate_d = din("w_gate", [D, D_FF]).ap()
    w_up_d = din("w_up", [D, D_FF]).ap()
    w_down_d = din("w_down", [D_FF, D]).ap()
    prm_d = din("prm", [P, 64]).ap()
    lamv_d = din("lamv", [P, 256]).ap()
    gfin_d = din("gfin", [P, D]).ap()
    wbd_d = din("wbd", [P, 16, P]).ap()
    relb_d = din("relb", [32, 4]).ap()
    oh_d = din("oh", [32, GV_LEN]).ap()
    ident_d = din("ident", [P, P], BF16).ap()
    aident_d = din("aident", [P, P], BF16).ap()
    out_d = nc.dram_tensor("out", [T, D], F32, kind="ExternalOutput").ap()

    w_in_b = nc.dram_tensor("w_in_b", [D, D_IN], BF16, kind="Internal").ap()
    w_out_b = nc.dram_tensor("w_out_b", [D, D], BF16, kind="Internal").ap()
    w_gate_b = nc.dram_tensor("w_gate_b", [D, D_FF], BF16, kind="Internal").ap()
    w_up_b = nc.dram_tensor("w_up_b", [D, D_FF], BF16, kind="Internal").ap()
    w_down_b = nc.dram_tensor("w_down_b", [D_FF, D], BF16, kind="Internal").ap()
    gv_t = nc.dram_tensor("gv_scr", [NH, GV_LEN], F32, kind="Internal")
    dbg_outs = {}
    if dbg:
        for nm, shp in dbg.items():
            dbg_outs[nm] = nc.dram_tensor("dbg_" + nm, list(shp), F32, kind="ExternalOutput").ap()

    es = contextlib.ExitStack()
    with es:
        S = Sched(nc, es)
        pe, act, dve, pool, sp = S.pe, S.act, S.dve, S.pool, S.sp

        def sbuf(st, name, shape, dt, side="right"):
            return st.enter_context(nc.sbuf_tensor("sb_" + name, list(shape), dt, side=side))

        def lsbuf(st, name, shape, dt):
            return sbuf(st, name, shape, dt, side="left")

        def psum(st, name, shape, dt=F32):
            return st.enter_context(nc.psum_tensor("ps_" + name, list(shape), dt))

        prm = lsbuf(es, "prm", [P, 64], F32)
        lamv = lsbuf(es, "lamv", [P, 256], F32)
        ident = lsbuf(es, "ident", [P, P], BF16)
        aident = lsbuf(es, "aident", [P, P], BF16)
        ones_b = lsbuf(es, "ones_b", [P, P], BF16)
        ones_f = lsbuf(es, "ones_f", [P, P], F32)
        wbd = lsbuf(es, "wbd", [P, 16, P], BF16)
        cst = lsbuf(es, "cst", [P, 48], F32)
        cbias = lsbuf(es, "cbias", [P, 2 * NH], F32)
        mixL = lsbuf(es, "mixL", [P, 4, T], BF16)
        R_q = [[Res() for b in range(NB)] for h in range(NH)]
        R_k = [[Res() for b in range(NB)] for h in range(NH)]
        R_v = [Res() for i in range(NT)]
        R_ones = Res("ones"); R_prm = Res("prm"); R_cst = Res("cst"); R_strip = Res("strip"); R_const = Res("const")
        R_wbd = Res("wbd")
        R_mix = [[Res("mix%d_%d" % (c, b)) for b in range(NB)] for c in range(DC)]
        R_wscr = {k: Res("wscr_" + k) for k in ("in_lru", "in_qkv", "out", "gate", "up", "down")}

        PC_G1 = 0
        PC_G2 = 8
        PC_CW = 16
        PC_CB = 32
        PC_BR = 36
        PC_BI = 44
        PC_LL = 52
        PC_SG = 60
        CC_NLAM = 0
        CC_SG = 1
        CC_CS = 2
        CC_CS2 = 10
        CC_NB = 18
        d_prm = S.dsem("d_prm")
        d_wbd = S.dsem("d_wbd", in_barrier=False)
        cast_list = (("in_lru", w_in_d[:, 1536:D_IN], w_in_b[:, 1536:D_IN], D, 1024),
                     ("in_qkv", w_in_d[:, 0:1536], w_in_b[:, 0:1536], D, 1536), ("out", w_out_d, w_out_b, D, D),
                     ("gate", w_gate_d, w_gate_b, D, D_FF), ("up", w_up_d, w_up_b, D, D_FF),
                     ("down", w_down_d, w_down_b, D_FF, D))

        def emit_casts(keys, after=()):
            for key, src, dst, rows, cols in sorted((c for c in cast_list if c[0] in keys), key=lambda c: keys.index(c[0])):
                d_c = S.dsem("d_cast_" + key, in_barrier=False)
                for r0 in range(0, rows, 256):
                    S.dma(pool, dst[r0:r0 + 256, :], src[r0:r0 + 256, :], d_c, reads=list(after), writes=[R_wscr[key]])

        S.op(pool, lambda e: e.memset(ones_b[:], 1.0), writes=[R_ones])
        S.op(pool, lambda e: e.memset(ones_f[:], 1.0), writes=[R_ones])
        for (o, i) in ((prm, prm_d), (lamv, lamv_d), (ident, ident_d), (aident, aident_d)):
            S.dma(sp, o[:], i, d_prm, writes=[R_prm, R_const])
        emit_casts(("in_lru",), after=[R_prm])
        S.dma(pool, wbd[:], wbd_d, d_wbd, reads=[R_prm], writes=[R_wbd])
        if True:
            tmpl = lsbuf(es, "tmpl", [P, 64], F32)
            tmpc = lsbuf(es, "tmpc", [P, 16], F32)
            R_tmp = Res()
            for i in range(2):
                S.op(dve, lambda e, i=i: e.tensor_tensor(out=tmpl[:], in0=lamv[:, i * 128:i * 128 + 64],
                                                         in1=lamv[:, i * 128 + 64:i * 128 + 128], op=ALU.mult),
                     reads=[R_prm], writes=[R_tmp])
                S.op(dve, lambda e, i=i: e.reduce_sum(out=tmpc[:, i:i + 1], in_=tmpl[:], axis=mybir.AxisListType.X),
                     reads=[R_tmp], writes=[R_tmp])
            S.op(act, lambda e: e.activation(out=tmpc[:, 2:4], in_=tmpc[:, 0:2], func=AF.Exp), reads=[R_tmp], writes=[R_tmp])
            S.op(dve, lambda e: e.scalar_tensor_tensor(out=cst[:, CC_NLAM:CC_NLAM + 1], in0=tmpc[:, 3:4], scalar=-LAMBDA_INIT,
                                                       in1=tmpc[:, 2:3], op0=ALU.add, op1=ALU.subtract),
                 reads=[R_tmp], writes=[R_cst])
            S.op(dve, lambda e: e.tensor_scalar(out=cst[:, CC_SG:CC_SG + 1], in0=prm[:, PC_SG:PC_SG + 1],
                                                scalar1=(1.0 - LAMBDA_INIT), scalar2=None, op0=ALU.mult),
                 reads=[R_prm], writes=[R_cst])
            S.op(act, lambda e: e.activation(out=tmpc[:, 4:12], in_=prm[:, PC_LL:PC_LL + 8], func=AF.Exp, scale=-1.0),
                 reads=[R_prm], writes=[R_tmp])
            S.op(act, lambda e: e.activation(out=tmpc[:, 4:12], in_=tmpc[:, 4:12], func=AF.Ln, bias=1.0),
                 reads=[R_tmp], writes=[R_tmp])
            S.op(dve, lambda e: e.tensor_scalar(out=cst[:, CC_CS:CC_CS + 8], in0=tmpc[:, 4:12], scalar1=-8.0, scalar2=None,
                                                op0=ALU.mult), reads=[R_tmp], writes=[R_cst])
            S.op(dve, lambda e: e.tensor_scalar(out=cst[:, CC_CS2:CC_CS2 + 8], in0=tmpc[:, 4:12], scalar1=-16.0, scalar2=None,
                                                op0=ALU.mult), reads=[R_tmp], writes=[R_cst])
            S.op(dve, lambda e: e.tensor_scalar(out=cst[:, CC_NB:CC_NB + 16], in0=prm[:, PC_BR:PC_BR + 16], scalar1=-1.0,
                                                scalar2=None, op0=ALU.mult), reads=[R_prm], writes=[R_cst])

        def setup_strips(strip32):
            st0 = contextlib.ExitStack()
            with st0:
                relb = sbuf(st0, "relb", [32, 4], F32)
                oh = sbuf(st0, "oh", [32, GV_LEN], F32)
                gv_sb = sbuf(st0, "gv_sb", [NH, GV_LEN], F32)
                ps_gv = psum(st0, "ps_gv", [NH, 3, 512], F32)
                R_rel = Res(); R_gv = Res(); R_psgv = Res(); R_tmp = Res()
                d_gv = S.dsem("d_gv")
                d_rel = S.dsem("d_rel")
                S.dma(sp, relb[:], relb_d, d_rel, writes=[R_rel])
                S.dma(sp, oh[:], oh_d, d_rel, writes=[R_rel])
                for k in range(3):
                    n = min(512, GV_LEN - k * 512)
                    S.op(pe, lambda e, k=k, n=n: e.matmul(ps_gv[:, k, 0:n], lhsT=relb[:], rhs=oh[:, k * 512:k * 512 + n],
                                                          start=True, stop=True), reads=[R_rel], writes=[R_psgv])
                for k in range(3):
                    n = min(512, GV_LEN - k * 512)
                    S.op(dve, lambda e, k=k, n=n: e.tensor_copy(out=gv_sb[:, k * 512:k * 512 + n], in_=ps_gv[:, k, 0:n]),
                         reads=[R_psgv], writes=[R_gv])
                S.dma(sp, gv_t.ap(), gv_sb[:], d_gv, reads=[R_gv], writes=[R_tmp])
                for h in range(NH):
                    src = bass.AP(tensor=gv_t, offset=h * GV_LEN, ap=[[1, P], [1, STRIP_W]])
                    S.dma(sp, strip32[:, h, :], src, d_gv, reads=[R_tmp], writes=[R_strip])
                S.barrier()

        st1 = contextlib.ExitStack()
        with st1:
            nT = sbuf(st1, "nT", [P, DC, T], BF16)
            R_nT = [Res("nT%d" % i) for i in range(NT)]
            st1a = contextlib.ExitStack()
            with st1a:
                XR = 3
                xt = [sbuf(st1a, "xt%d" % i, [P, D], F32) for i in range(XR)]
                R_xt = [Res() for _ in range(XR)]
                d_xt = [S.dsem("d_xt%d" % i) for i in range(XR)]
                nb = [sbuf(st1a, "nb%d" % i, [P, D], BF16) for i in range(2)]
                R_nb = [Res() for _ in range(2)]
                junk = sbuf(st1a, "junk", [P, D], BF16)
                R_junk = Res()
                stat = sbuf(st1a, "stat", [P, 3 * NT], F32)
                R_st = [Res() for _ in range(NT)]
                ptr = [psum(st1a, "ptr%d" % i, [P, DC, P], BF16) for i in range(2)]
                R_ptr = [Res() for _ in range(2)]
                def norm_back(i):
                    s2 = i % 2
                    S.op(dve, lambda e: e.tensor_tensor(
                        out=nT[:, :, i * P:(i + 1) * P], in0=ptr[s2][:],
                        in1=prm[:, PC_G1:PC_G1 + DC].unsqueeze(2).to_broadcast([P, DC, P]), op=ALU.mult),
                         reads=[R_ptr[s2], R_prm], writes=[R_nT[i]])

                for i in range(NT):
                    s3, s2 = i % XR, i % 2
                    S.dma(sp, xt[s3][:], x_d[i * P:(i + 1) * P, :], d_xt[s3], writes=[R_xt[s3]])
                    S.op(act, lambda e, i=i, s3=s3: e.activation(out=junk[:], in_=xt[s3][:], func=AF.Square,
                                                                 accum_out=stat[:, i:i + 1]),
                         reads=[R_xt[s3]], writes=[R_junk, R_st[i]])
                    S.op(act, lambda e, i=i: e.activation(out=stat[:, NT + i:NT + i + 1], in_=stat[:, i:i + 1], func=AF.Ln,
                                                          scale=1.0 / D, bias=EPS), reads=[R_st[i]], writes=[R_st[i]])
                    S.op(act, lambda e, i=i: e.activation(out=stat[:, 2 * NT + i:2 * NT + i + 1],
                                                          in_=stat[:, NT + i:NT + i + 1], func=AF.Exp, scale=-0.5),
                         reads=[R_st[i]], writes=[R_st[i]])
                    S.op(dve, lambda e, i=i, s3=s3, s2=s2: e.tensor_scalar(out=nb[s2][:], in0=xt[s3][:],
                                                                          scalar1=stat[:, 2 * NT + i:2 * NT + i + 1],
                                                                          scalar2=None, op0=ALU.mult),
                         reads=[R_xt[s3], R_st[i]], writes=[R_nb[s2]])
                    for j in range(DC):
                        S.op(pe, lambda e, j=j, s2=s2: e.transpose(out=ptr[s2][:, j, :], in_=nb[s2][:, j * P:(j + 1) * P],
                                                                   identity=ident[:]),
                             reads=[R_nb[s2], R_const], writes=[R_ptr[s2]], inc=(j == DC - 1))
                    if i >= 1:
                        norm_back(i - 1)
                norm_back(NT - 1)
                emit_casts(("in_qkv", "out", "down", "gate", "up"), after=[R_xt[(NT - 1) % XR]])
                S.barrier()
            if "nT" in dbg_outs:
                stx = contextlib.ExitStack()
                with stx:
                    tmpf = sbuf(stx, "tmpf", [P, DC, 512], F32)
                    R_t = Res(); d_dbg = S.dsem("d_dbg1")
                    for b in range(NB):
                        S.op(dve, lambda e, b=b: e.tensor_copy(out=tmpf[:], in_=nT[:, :, b * TB:(b + 1) * TB]),
                             reads=R_nT, writes=[R_t])
                        S.dma(sp, dbg_outs["nT"][:, :, b * TB:(b + 1) * TB], tmpf[:], d_dbg, reads=[R_t])
                    S.barrier()
            if stage <= 1:
                return nc
            w_in_v = w_in_b.rearrange("(j p) n -> p j n", p=P)
            st1b = contextlib.ExitStack()
            with st1b:
                XRp = sbuf(st1b, "XRp", [P, T + 4], F32)
                Gb = sbuf(st1b, "Gb", [P, T], BF16)
                XC = sbuf(st1b, "XC", [P, T], F32)
                XCb = sbuf(st1b, "XCb", [P, T], BF16)
                Hf = sbuf(st1b, "Hf", [P, T], F32)
                wxr = sbuf(st1b, "wxr", [P, DC, P], BF16)
                wgr = sbuf(st1b, "wgr", [P, DC, P], BF16)
                tA = [sbuf(st1b, "tA_%d" % i, [P, TB], F32) for i in range(2)]
                tA2 = [sbuf(st1b, "tA2_%d" % i, [P, TB], F32) for i in range(2)]
                tU = [sbuf(st1b, "tU_%d" % i, [P, TB], F32) for i in range(2)]
                Rb = sbuf(st1b, "Rb", [P, T], BF16)
                IXb = sbuf(st1b, "IXb", [P, T], BF16)
                R_Rb = [Res() for _ in range(NB)]; R_IX = [Res() for _ in range(NB)]
                pz = [psum(st1b, "pz%d" % i, [P, TB], F32) for i in range(4)]
                R_XR = Res(); R_Gb = Res(); R_XC = Res(); R_XCb = Res(); R_Hf = Res()
                R_wxr = Res(); R_wgr = Res()
                R_E1 = [Res(), Res()]; R_A = [Res(), Res()]; R_A2 = [Res(), Res()]; R_E2 = [Res(), Res()]; R_U = [Res(), Res()]
                R_pz = [Res() for _ in range(4)]
                d_wxr = S.dsem("d_wxr"); d_wgr = S.dsem("d_wgr")
                S.op(dve, lambda e: e.memset(XRp[:, 0:2], 0.0), writes=[R_XR])
                S.op(dve, lambda e: e.memset(XRp[:, T + 2:T + 4], 0.0), writes=[R_XR])
                for c in range(4):
                    S.dma(sp, wxr[:], w_in_v[:, :, 1536 + c * P:1536 + (c + 1) * P], d_wxr, reads=[R_wscr["in_lru"]], writes=[R_wxr])
                    S.dma(sp, wgr[:], w_in_v[:, :, 2048 + c * P:2048 + (c + 1) * P], d_wgr, reads=[R_wscr["in_lru"]], writes=[R_wgr])
                    for b in range(NB):
                        blk = slice(b * TB, (b + 1) * TB)
                        px, pg_ = b % 2, 2 + b % 2
                        for j in range(DC):
                            S.op(pe, lambda e, j=j, px=px, blk=blk: e.matmul(pz[px][:], lhsT=wxr[:, j, :], rhs=nT[:, j, blk],
                                                                             start=(j == 0), stop=(j == DC - 1)),
                                 reads=[R_wxr] + R_nT[4 * b:4 * b + 4], writes=[R_pz[px]], inc=(j == DC - 1))
                        for j in range(DC):
                            S.op(pe, lambda e, j=j, pg_=pg_, blk=blk: e.matmul(pz[pg_][:], lhsT=wgr[:, j, :], rhs=nT[:, j, blk],
                                                                               start=(j == 0), stop=(j == DC - 1)),
                                 reads=[R_wgr] + R_nT[4 * b:4 * b + 4], writes=[R_pz[pg_]], inc=(j == DC - 1))
                        S.op(act, lambda e, px=px, b=b: e.copy(out=XRp[:, 2 + b * TB:2 + (b + 1) * TB], in_=pz[px][:]),
                             reads=[R_pz[px]], writes=[R_XR])
                        S.op(act, lambda e, pg_=pg_, blk=blk: e.copy(out=Gb[:, blk], in_=pz[pg_][:]),
                             reads=[R_pz[pg_]], writes=[R_Gb])
                    T1 = Hf[:]
                    S.op(dve, lambda e: e.tensor_tensor(out=T1, in0=Gb[:], in1=Gb[:], op=ALU.mult), reads=[R_Gb], writes=[R_Hf])
                    S.op(dve, lambda e: e.tensor_scalar(out=T1, in0=T1, scalar1=GELU_C2, scalar2=GELU_C1, op0=ALU.mult, op1=ALU.add),
                         reads=[R_Hf], writes=[R_Hf])
                    S.op(dve, lambda e: e.tensor_tensor(out=T1, in0=T1, in1=Gb[:], op=ALU.mult), reads=[R_Hf, R_Gb], writes=[R_Hf])
                    cw0 = PC_CW + c * 4
                    S.op(act, lambda e, c=c, cw0=cw0: e.activation(out=XC[:], in_=XRp[:, 0:T], func=AF.Identity,
                                                                   scale=prm[:, cw0:cw0 + 1], bias=prm[:, PC_CB + c:PC_CB + c + 1]),
                         reads=[R_XR, R_prm], writes=[R_XC])
                    S.op(act, lambda e: e.activation(out=T1, in_=T1, func=AF.Exp, scale=-1.0), reads=[R_Hf], writes=[R_Hf])
                    S.op(act, lambda e: e.activation(out=T1, in_=T1, func=AF.Ln, bias=1.0), reads=[R_Hf], writes=[R_Hf])
                    S.op(act, lambda e: e.activation(out=T1, in_=T1, func=AF.Exp, scale=-1.0), reads=[R_Hf], writes=[R_Hf])
                    for tap in range(1, 4):
                        S.op(dve, lambda e, tap=tap, cw0=cw0: e.scalar_tensor_tensor(
                            out=XC[:], in0=XRp[:, tap:tap + T], scalar=prm[:, cw0 + tap:cw0 + tap + 1], in1=XC[:],
                            op0=ALU.mult, op1=ALU.add), reads=[R_XR, R_XC, R_prm], writes=[R_XC])
                    S.op(act, lambda e: e.copy(out=XCb[:], in_=XC[:]), reads=[R_XC], writes=[R_XCb])
                    HBv = XRp
                    for d in range(2):
                        m_r = d * 8 + c
                        m_i = d * 8 + 4 + c
                        col = d * 4 + c
                        order = list(range(NB)) if d == 0 else list(range(NB - 1, -1, -1))
                        for bi, b in enumerate(order):
                            blk = slice(b * TB, (b + 1) * TB)
                            pr, pi_ = bi % 2, 2 + bi % 2
                            S.op(pe, lambda e, pr=pr, blk=blk: e.matmul(pz[pr][:], lhsT=wbd[:, m_r, :], rhs=XCb[:, blk], start=True, stop=True),
                                 reads=[R_wbd, R_XCb], writes=[R_pz[pr]])
                            S.op(pe, lambda e, pi_=pi_, blk=blk: e.matmul(pz[pi_][:], lhsT=wbd[:, m_i, :], rhs=XCb[:, blk], start=True, stop=True),
                                 reads=[R_wbd, R_XCb], writes=[R_pz[pi_]])
                            S.op(act, lambda e, pr=pr, blk=blk: e.activation(out=Rb[:, blk], in_=pz[pr][:], func=AF.Sigmoid,
                                                                             bias=prm[:, PC_BR + col:PC_BR + col + 1]),
                                 reads=[R_pz[pr], R_prm], writes=[R_Rb[b]])
                            S.op(act, lambda e, pi_=pi_, blk=blk: e.activation(out=IXb[:, blk], in_=pz[pi_][:], func=AF.Sigmoid,
                                                                               bias=prm[:, PC_BI + col:PC_BI + col + 1]),
                                 reads=[R_pz[pi_], R_prm], writes=[R_IX[b]])
                            S.op(dve, lambda e, blk=blk: e.tensor_tensor(out=IXb[:, blk], in0=IXb[:, blk], in1=XC[:, blk], op=ALU.mult),
                                 reads=[R_IX[b], R_XC], writes=[R_IX[b]])
                        if d == 0:
                            S.op(dve, lambda e: e.tensor_tensor(out=Gb[:], in0=Gb[:], in1=T1, op=ALU.mult), reads=[R_Hf, R_Gb], writes=[R_Gb])
                        for bi, b in enumerate(order):
                            blk = slice(b * TB, (b + 1) * TB)
                            k2 = bi % 2
                            A, A2, U = tA[k2], tA2[k2], tU[k2]
                            rA, rA2, rU = R_A[k2], R_A2[k2], R_U[k2]
                            S.op(act, lambda e, A=A, blk=blk: e.activation(out=A[:], in_=Rb[:, blk], func=AF.Exp,
                                                                           scale=cst[:, CC_CS + col:CC_CS + col + 1]),
                                 reads=[R_Rb[b], R_cst], writes=[rA])
                            S.op(act, lambda e, A2=A2, blk=blk: e.activation(out=A2[:], in_=Rb[:, blk], func=AF.Exp,
                                                                             scale=cst[:, CC_CS2 + col:CC_CS2 + col + 1]),
                                 reads=[R_Rb[b], R_cst], writes=[rA2])
                            S.op(act, lambda e, A2=A2: e.activation(out=A2[:], in_=A2[:], func=AF.Ln, scale=-1.0, bias=1.0),
                                 reads=[rA2], writes=[rA2])
                            S.op(act, lambda e, A2=A2: e.activation(out=A2[:], in_=A2[:], func=AF.Exp, scale=0.5), reads=[rA2], writes=[rA2])
                            S.op(dve, lambda e, blk=blk, U=U, A2=A2: e.tensor_tensor(out=U[:], in0=A2[:], in1=IXb[:, blk], op=ALU.mult),
                                 reads=[rA2, R_IX[b]], writes=[rU])
                            if d == 0:
                                init = 0.0 if b == 0 else Hf[:, b * TB - 1:b * TB]
                                S.op(dve, lambda e, blk=blk, init=init, A=A, U=U: e.tensor_tensor_scan(
                                    out=Hf[:, blk], data0=A[:], data1=U[:], initial=init, op0=ALU.mult, op1=ALU.add),
                                     reads=[rA, rU, R_Hf], writes=[R_Hf])
                            else:
                                lo, hi = 2 + b * TB, 2 + (b + 1) * TB
                                init = 0.0 if bi == 0 else HBv[:, hi:hi + 1]
                                S.op(dve, lambda e, lo=lo, hi=hi, init=init, A=A, U=U: e.tensor_tensor_scan(
                                    out=HBv[:, hi - 1:lo - 1:-1], data0=A[:, ::-1], data1=U[:, ::-1], initial=init,
                                    op0=ALU.mult, op1=ALU.add),
                                     reads=[rA, rU, R_XR], writes=[R_XR])
                    S.op(dve, lambda e: e.tensor_tensor(out=Hf[:], in0=Hf[:], in1=HBv[:, 2:T + 2], op=ALU.add),
                         reads=[R_Hf, R_XR], writes=[R_Hf])
                    S.op(dve, lambda e, c=c: e.tensor_tensor(out=mixL[:, c, :], in0=Hf[:], in1=Gb[:], op=ALU.mult),
                         reads=[R_Hf, R_Gb], writes=R_mix[4 + c])
                S.barrier()
            if stage <= 2:
                return nc
            st_qkv = contextlib.ExitStack()
            st_qkv.__enter__()
            qT = lsbuf(st_qkv, "qT", [P, NH, T], BF16)
            kT = lsbuf(st_qkv, "kT", [P, NH, T], BF16)
            Vt = lsbuf(st_qkv, "Vt", [P, NT, 512], BF16)
            st1c = contextlib.ExitStack()
            with st1c:
                pq = [psum(st1c, "pq%d" % i, [P, TB], F32) for i in range(4)]
                R_pq = [Res() for _ in range(4)]
                stv = contextlib.ExitStack()
                with stv:
                    wv = sbuf(stv, "wv", [P, DC, 256], BF16)
                    R_wv = Res(); d_wv = S.dsem("d_wv")
                    for vh in range(2):
                        S.dma(sp, wv[:], w_in_v[:, :, 1024 + vh * 256:1024 + (vh + 1) * 256], d_wv, reads=[R_wscr["in_qkv"]], writes=[R_wv])
                        for i in range(NT):
                            pp = i % 4
                            for j in range(DC):
                                S.op(pe, lambda e, j=j, pp=pp, i=i: e.matmul(pq[pp][:, 0:256], lhsT=nT[:, j, i * P:(i + 1) * P], rhs=wv[:, j, :],
                                                                           start=(j == 0), stop=(j == DC - 1)),
                                     reads=[R_wv, R_nT[i]], writes=[R_pq[pp]], inc=(j == DC - 1))
                            vcols = slice(vh * 256, (vh + 1) * 256)
                            if i % 2 == 0:
                                S.op(act, lambda e, pp=pp, i=i, vcols=vcols: e.copy(out=Vt[:, i, vcols], in_=pq[pp][:, 0:256]),
                                     reads=[R_pq[pp]], writes=[R_v[i]])
                            else:
                                S.op(dve, lambda e, pp=pp, i=i, vcols=vcols: e.tensor_copy(out=Vt[:, i, vcols], in_=pq[pp][:, 0:256]),
                                     reads=[R_pq[pp]], writes=[R_v[i]])
                    S.barrier()
                wqk = [sbuf(st1c, "wqk%d" % i, [P, DC, P], BF16) for i in range(2)]
                R_wqk = [Res(), Res()]
                d_wqk = [S.dsem("d_wqk0"), S.dsem("d_wqk1")]
                n = 0
                for h in range(NH):
                    for typ in range(2):
                        ws = n % 2
                        c0 = typ * 512 + h * P
                        S.dma(sp, wqk[ws][:], w_in_v[:, :, c0:c0 + P], d_wqk[ws], reads=[R_wscr["in_qkv"]], writes=[R_wqk[ws]])
                        for b in range(NB):
                            blk = slice(b * TB, (b + 1) * TB)
                            pp = (n * NB + b) % 4
                            for j in range(DC):
                                S.op(pe, lambda e, j=j, pp=pp, blk=blk, ws=ws: e.matmul(pq[pp][:], lhsT=wqk[ws][:, j, :], rhs=nT[:, j, blk],
                                                                                       start=(j == 0), stop=(j == DC - 1)),
                                     reads=[R_wqk[ws]] + R_nT[4 * b:4 * b + 4], writes=[R_pq[pp]], inc=(j == DC - 1))
                            if typ == 0:
                                S.op(act, lambda e, pp=pp, blk=blk, h=h: e.activation(out=qT[:, h, blk], in_=pq[pp][:], func=AF.Copy, scale=0.125),
                                     reads=[R_pq[pp]], writes=[R_q[h][b]])
                            else:
                                S.op(dve, lambda e, pp=pp, blk=blk, h=h: e.tensor_copy(out=kT[:, h, blk], in_=pq[pp][:]),
                                     reads=[R_pq[pp]], writes=[R_k[h][b]])
                        n += 1
                S.barrier()
        if stage <= 3:
            st_qkv.close()
            return nc
        st_mix = contextlib.ExitStack()
        st_mix.__enter__()
        mixA = sbuf(st_mix, "mixA", [P, 4, T], BF16)
        mixs = (mixA, mixL)
        strip32 = lsbuf(st_qkv, "strip32", [P, NH, STRIP_W], F32)
        setup_strips(strip32)
        st2 = contextlib.ExitStack()
        with st2:
            NPT = 4
            NSL = 3
            PT = [sbuf(st2, "PT%d" % i, [P, 2, TB], BF16) for i in range(NPT)]
            accb = [sbuf(st2, "accb0", [P, 2, TB], BF16)] * 2
            acc32 = sbuf(st2, "acc32", [P, 2, TB], F32)
            rz = sbuf(st2, "rz", [P, 2, TB], F32)
            oo = sbuf(st2, "oo", [P, 2, TB], F32)
            R_acc = [Res()] * 2; R_a32 = Res()
            R_rz = Res(); R_oo = Res()
            ps_s = [psum(st2, "ps_s%d" % i, [P, 2, TB], F32) for i in range(NSL)]
            ps_o = psum(st2, "ps_o", [P, 2, TB], F32)
            R_sl = [Res() for _ in range(NSL)]; R_PT = [Res() for _ in range(NPT)]; R_o = Res()
            units = [(h, qb) for h in range(NH) for qb in range(NB)]
            steps = [(u, kt) for u in range(len(units)) for kt in range(NT)]

            def emit_S(g):
                u, kt = steps[g]
                h, qb = units[u]
                sl = g % NSL
                qblk = slice(qb * TB, (qb + 1) * TB)
                ktl = slice(kt * P, (kt + 1) * P)
                delta = kt * P - qb * TB
                near = (-218 < delta < 602)
                for w in range(2):
                    rows = slice(w * 64, (w + 1) * 64)
                    S.op(pe, lambda e, w=w, rows=rows: e.matmul(
                        ps_s[sl][:, w, :], lhsT=kT[rows, h, ktl], rhs=qT[rows, h, qblk], start=True, stop=True),
                         reads=[R_k[h][kt // 4], R_q[h][qb]], writes=[R_sl[sl]], inc=(w == 1))
                pl = g % NPT
                if near:
                    J = delta + 640
                    S.op(dve, lambda e: e.tensor_tensor(out=ps_s[sl][:], in0=ps_s[sl][:],
                                                        in1=strip32[:, h, J:J - TB:-1].unsqueeze(1).to_broadcast([P, 2, TB]), op=ALU.add),
                         reads=[R_sl[sl], R_strip], writes=[R_sl[sl]])
                    S.op(act, lambda e: e.activation(out=PT[pl][:], in_=ps_s[sl][:], func=AF.Exp),
                         reads=[R_sl[sl]], writes=[R_PT[pl]])
                else:
                    cc = STRIP_W - 1 if delta > 0 else 0
                    S.op(act, lambda e: e.activation(out=PT[pl][:], in_=ps_s[sl][:], func=AF.Exp, bias=strip32[:, h, cc:cc + 1]),
                         reads=[R_sl[sl], R_strip], writes=[R_PT[pl]])

            def emit_PV(g):
                u, kt = steps[g]
                h, qb = units[u]
                pl = g % NPT
                for w in range(2):
                    S.op(pe, lambda e, w=w: e.matmul(ps_o[:, w, :], lhsT=Vt[:, kt, h * P:(h + 1) * P], rhs=PT[pl][:, w, :],
                                                     start=(kt == 0), stop=(kt == NT - 1)),
                         reads=[R_v[kt], R_PT[pl]], writes=[R_o], inc=(w == 1))
                GRP = 8
                j = kt % GRP
                ab = (g // GRP) % 2
                if j == 1:
                    pp = (g - 1) % NPT
                    S.op(dve, lambda e: e.tensor_tensor(out=accb[ab][:], in0=PT[pp][:], in1=PT[pl][:], op=ALU.add),
                         reads=[R_PT[pp], R_PT[pl]], writes=[R_acc[ab]])
                elif j >= 2:
                    S.op(dve, lambda e: e.tensor_tensor(out=accb[ab][:], in0=accb[ab][:], in1=PT[pl][:], op=ALU.add),
                         reads=[R_PT[pl], R_acc[ab]], writes=[R_acc[ab]])
                if j == GRP - 1:
                    if kt == GRP - 1:
                        S.op(dve, lambda e: e.tensor_copy(out=acc32[:], in_=accb[ab][:]), reads=[R_acc[ab]], writes=[R_a32])
                    else:
                        S.op(dve, lambda e: e.tensor_tensor(out=acc32[:], in0=acc32[:], in1=accb[ab][:], op=ALU.add),
                             reads=[R_acc[ab], R_a32], writes=[R_a32])

            def epi_A1(u):
                S.op(act, lambda e: e.copy(out=oo[:], in_=ps_o[:]), reads=[R_o], writes=[R_oo])

            def ep_z(u, sla):
                for w in range(2):
                    S.op(pe, lambda e, w=w: e.matmul(ps_s[sla][:, w, :], lhsT=ones_f[:], rhs=acc32[:, w, :], start=True, stop=True),
                         reads=[R_a32, R_const, R_ones], writes=[R_sl[sla]], inc=(w == 1))
                S.op(act, lambda e: e.activation(out=rz[:], in_=ps_s[sla][:], func=AF.Ln), reads=[R_sl[sla]], writes=[R_rz])
                S.op(act, lambda e: e.activation(out=rz[:], in_=rz[:], func=AF.Exp, scale=-1.0), reads=[R_rz], writes=[R_rz])

            def ep_o(u):
                S.op(dve, lambda e: e.tensor_tensor(out=oo[:], in0=oo[:], in1=rz[:], op=ALU.mult), reads=[R_oo, R_rz], writes=[R_oo])
                S.op(dve, lambda e: e.scalar_tensor_tensor(out=oo[:, 0, :], in0=oo[:, 1, :], scalar=cst[:, CC_NLAM:CC_NLAM + 1],
                                                           in1=oo[:, 0, :], op0=ALU.mult, op1=ALU.add),
                     reads=[R_oo, R_cst], writes=[R_oo])

            def ep_sq(u):
                S.op(act, lambda e: e.activation(out=rz[:, 1, :], in_=oo[:, 0, :], func=AF.Square), reads=[R_oo, R_rz], writes=[R_rz])

            def ep_ss(u, slb):
                S.op(pe, lambda e: e.matmul(ps_s[slb][:, 0, :], lhsT=ones_f[:], rhs=rz[:, 1, :], start=True, stop=True),
                     reads=[R_rz, R_const, R_ones], writes=[R_sl[slb]])
                S.op(act, lambda e: e.activation(out=rz[:, 0, :], in_=ps_s[slb][:, 0, :], func=AF.Ln, scale=1.0 / P, bias=EPS),
                     reads=[R_sl[slb]], writes=[R_rz])
                S.op(act, lambda e: e.activation(out=rz[:, 0, :], in_=rz[:, 0, :], func=AF.Exp, scale=-0.5), reads=[R_rz], writes=[R_rz])

            def ep_out(u):
                h, qb = units[u]
                qblk = slice(qb * TB, (qb + 1) * TB)
                S.op(dve, lambda e: e.scalar_tensor_tensor(out=mixA[:, h, qblk], in0=oo[:, 0, :], scalar=cst[:, CC_SG:CC_SG + 1],
                                                           in1=rz[:, 0, :], op0=ALU.mult, op1=ALU.mult),
                     reads=[R_oo, R_rz, R_cst], writes=[R_mix[h][qb]])

            pend = None
            NS = len(steps)
            DEPTH = 2
            for g in range(NS + DEPTH):
                if g < NS:
                    emit_S(g)
                if g >= DEPTH:
                    gp = g - DEPTH
                    emit_PV(gp)
                    u1, kt1 = steps[gp]
                    if kt1 == NT - 1:
                        epi_A1(u1)
                        pend = u1
                        if gp == NS - 1:
                            ep_z(pend, 0); ep_o(pend); ep_sq(pend); ep_ss(pend, 1); ep_out(pend)
                            pend = None
                    elif pend is not None:
                        if kt1 == 1:
                            ep_z(pend, (g + 1) % NSL)
                        elif kt1 == 3:
                            ep_o(pend)
                        elif kt1 == 5:
                            ep_sq(pend)
                        elif kt1 == 7:
                            ep_ss(pend, (g + 1) % NSL)
                        elif kt1 == 9:
                            ep_out(pend)
                            pend = None
            S.barrier()
        st_qkv.close()
        if stage <= 4:
            st_mix.close()
            return nc
        st3 = contextlib.ExitStack()
        sbuf3 = lsbuf
        with st3:
            wout = lsbuf(st3, "wout", [P, DC, D], BF16)
            wd = lsbuf(st3, "wd", [P, FC, D], BF16)
            hblk = lsbuf(st3, "hblk", [P, 4, D], F32)
            nb2s = [lsbuf(st3, "nb2_%d" % i, [P, D], BF16) for i in range(2)]
            junk3 = lsbuf(st3, "junk3", [P, D], BF16)
            actT = lsbuf(st3, "actT", [P, FC, TB], BF16)
            wg = [lsbuf(st3, "wg%d" % i, [P, DC, 256], BF16) for i in range(2)]
            wu = [lsbuf(st3, "wu%d" % i, [P, DC, 256], BF16) for i in range(2)]
            sg = [lsbuf(st3, "sg%d" % i, [P, TB], F32) for i in range(2)]
            ot = [lsbuf(st3, "ot%d" % i, [P, D], F32) for i in range(2)]
            stat3 = lsbuf(st3, "stat3", [P, 8], F32)
            gfin = lsbuf(st3, "gfin", [P, D], F32)
            R_gfin = Res(); d_gfin = S.dsem("d_gfin")
            S.dma(sp, gfin[:], gfin_d, d_gfin, writes=[R_gfin])
            R_wout = Res(); R_wd = Res(); R_h = [Res() for _ in range(4)]; R_nb2s = [Res(), Res()]; R_j3 = Res(); R_act = [Res() for _ in range(FC)]
            R_wg = [Res(), Res()]; R_wu = [Res(), Res()]; R_sg = [Res(), Res()]; R_ot = [Res(), Res()]; R_st3 = [Res() for _ in range(4)]
            d_wout = S.dsem("d_wout"); d_wd = S.dsem("d_wd"); d_hs = [S.dsem("d_h%d" % i) for i in range(4)]
            d_wg = [S.dsem("d_wg0"), S.dsem("d_wg1")]; d_wu = [S.dsem("d_wu0"), S.dsem("d_wu1")]
            d_ot = [S.dsem("d_ot0"), S.dsem("d_ot1")]
            po = [psum(st3, "po%d" % i, [P, TB], F32) for i in range(2)]
            pg = [psum(st3, "pg%d" % i, [P, TB], F32) for i in range(2)]
            pu = [psum(st3, "pu%d" % i, [P, TB], F32) for i in range(2)]
            ptr3s = [psum(st3, "ptr3_%d" % i, [P, DC, P], BF16) for i in range(2)]
            R_po = [Res(), Res()]; R_pg = [Res(), Res()]; R_pu = [Res(), Res()]; R_ptr3s = [Res(), Res()]
            S.dma(sp, wout[:], w_out_b.rearrange("(e p) n -> p e n", p=P), d_wout, reads=[R_wscr["out"]], writes=[R_wout])
            S.dma(sp, wd[:], w_down_b.rearrange("(f p) n -> p f n", p=P), d_wd, reads=[R_wscr["down"]], writes=[R_wd])
            w_gate_v = w_gate_b.rearrange("(j p) n -> p j n", p=P)
            w_up_v = w_up_b.rearrange("(j p) n -> p j n", p=P)
            npo = 0
            not_ = 0
            nfg = 0
            for b in range(NB):
                if b == 0:
                    for s in range(4):
                        S.dma(sp, hblk[:, s, :], x_d[s * P:(s + 1) * P, :], d_hs[s], writes=[R_h[s]])
                for s in range(4):
                    tok = slice(b * TB + s * P, b * TB + (s + 1) * P)
                    for n2 in range(2):
                        pp = npo % 2; npo += 1
                        cols = slice(n2 * 512, (n2 + 1) * 512)
                        for e_ in range(DC):
                            S.op(pe, lambda e, e_=e_, pp=pp, tok=tok, cols=cols: e.matmul(po[pp][:], lhsT=mixs[e_ // 4][:, e_ % 4, tok], rhs=wout[:, e_, cols],
                                                                                        start=(e_ == 0), stop=(e_ == DC - 1)),
                                 reads=[R_wout, R_mix[e_][b]], writes=[R_po[pp]], inc=(e_ == DC - 1))
                        S.op(dve, lambda e, pp=pp, s=s, cols=cols: e.tensor_tensor(out=hblk[:, s, cols], in0=hblk[:, s, cols], in1=po[pp][:], op=ALU.add),
                             reads=[R_po[pp], R_h[s]], writes=[R_h[s]])
                def n2_front(s):
                    S.op(act, lambda e: e.activation(out=junk3[:], in_=hblk[:, s, :], func=AF.Square, accum_out=stat3[:, s:s + 1]),
                         reads=[R_h[s]], writes=[R_j3, R_st3[s]])
                    S.op(act, lambda e: e.activation(out=stat3[:, s:s + 1], in_=stat3[:, s:s + 1], func=AF.Ln, scale=1.0 / D, bias=EPS),
                         reads=[R_st3[s]], writes=[R_st3[s]])
                    S.op(act, lambda e: e.activation(out=stat3[:, s:s + 1], in_=stat3[:, s:s + 1], func=AF.Exp, scale=-0.5),
                         reads=[R_st3[s]], writes=[R_st3[s]])
                    nb2, R_nb2, ptr3, R_ptr3 = nb2s[s % 2], R_nb2s[s % 2], ptr3s[s % 2], R_ptr3s[s % 2]
                    S.op(dve, lambda e: e.tensor_scalar(out=nb2[:], in0=hblk[:, s, :], scalar1=stat3[:, s:s + 1], scalar2=None, op0=ALU.mult),
                         reads=[R_h[s], R_st3[s]], writes=[R_nb2])
                    for j in range(DC):
                        S.op(pe, lambda e, j=j: e.transpose(out=ptr3[:, j, :], in_=nb2[:, j * P:(j + 1) * P], identity=ident[:]),
                             reads=[R_nb2, R_const], writes=[R_ptr3], inc=(j == DC - 1))

                def n2_back(s):
                    tok = slice(b * TB + s * P, b * TB + (s + 1) * P)
                    ptr3, R_ptr3 = ptr3s[s % 2], R_ptr3s[s % 2]
                    for hf in range(2):
                        S.op(dve, lambda e, hf=hf: e.tensor_tensor(
                            out=mixs[hf][:, :, tok], in0=ptr3[:, 4 * hf:4 * hf + 4, :],
                            in1=prm[:, PC_G2 + 4 * hf:PC_G2 + 4 * hf + 4].unsqueeze(2).to_broadcast([P, 4, P]), op=ALU.mult),
                             reads=[R_ptr3, R_prm], writes=[R_mix[c][b] for c in range(4 * hf, 4 * hf + 4)])

                for s in range(5):
                    if s < 4:
                        n2_front(s)
                    if s >= 1:
                        n2_back(s - 1)
                blk = slice(b * TB, (b + 1) * TB)
                for fg in range(FC // 2):
                    ws = nfg % 2; nfg += 1
                    S.dma(sp, wg[ws][:], w_gate_v[:, :, fg * 256:(fg + 1) * 256], d_wg[ws], reads=[R_wscr["gate"]], writes=[R_wg[ws]])
                    S.dma(sp, wu[ws][:], w_up_v[:, :, fg * 256:(fg + 1) * 256], d_wu[ws], reads=[R_wscr["up"]], writes=[R_wu[ws]])
                    for f2 in range(2):
                        f = fg * 2 + f2
                        pp = f % 2
                        fc = slice(f2 * P, (f2 + 1) * P)
                        for j in range(DC):
                            S.op(pe, lambda e, j=j, pp=pp, ws=ws, fc=fc: e.matmul(pg[pp][:], lhsT=wg[ws][:, j, fc], rhs=mixs[j // 4][:, j % 4, blk],
                                                                                 start=(j == 0), stop=(j == DC - 1)),
                                 reads=[R_wg[ws]] + [R_mix[j][b]], writes=[R_pg[pp]], inc=(j == DC - 1))
                        for j in range(DC):
                            S.op(pe, lambda e, j=j, pp=pp, ws=ws, fc=fc: e.matmul(pu[pp][:], lhsT=wu[ws][:, j, fc], rhs=mixs[j // 4][:, j % 4, blk],
                                                                                 start=(j == 0), stop=(j == DC - 1)),
                                 reads=[R_wu[ws]] + [R_mix[j][b]], writes=[R_pu[pp]], inc=(j == DC - 1))
                        S.op(act, lambda e, pp=pp: e.activation(out=sg[pp][:], in_=pg[pp][:], func=AF.Silu), reads=[R_pg[pp]], writes=[R_sg[pp]])
                        S.op(dve, lambda e, pp=pp, f=f: e.tensor_tensor(out=actT[:, f, :], in0=pu[pp][:], in1=sg[pp][:], op=ALU.mult),
                             reads=[R_pu[pp], R_sg[pp]], writes=[R_act[f]])
                for s in range(4):
                    for n2 in range(2):
                        pp = npo % 2; npo += 1
                        cols = slice(n2 * 512, (n2 + 1) * 512)
                        for f in range(FC):
                            S.op(pe, lambda e, f=f, pp=pp, s=s, cols=cols: e.matmul(po[pp][:], lhsT=actT[:, f, s * P:(s + 1) * P], rhs=wd[:, f, cols],
                                                                                  start=(f == 0), stop=(f == FC - 1)),
                                 reads=[R_wd, R_act[f]], writes=[R_po[pp]], inc=(f == FC - 1))
                        S.op(dve, lambda e, pp=pp, s=s, cols=cols: e.tensor_tensor(out=hblk[:, s, cols], in0=hblk[:, s, cols], in1=po[pp][:], op=ALU.add),
                             reads=[R_po[pp], R_h[s]], writes=[R_h[s]])
                    S.op(act, lambda e, s=s: e.activation(out=junk3[:], in_=hblk[:, s, :], func=AF.Square, accum_out=stat3[:, 4 + s:5 + s]),
                         reads=[R_h[s]], writes=[R_j3, R_st3[s]])
                    S.op(act, lambda e, s=s: e.activation(out=stat3[:, 4 + s:5 + s], in_=stat3[:, 4 + s:5 + s], func=AF.Ln, scale=1.0 / D, bias=EPS),
                         reads=[R_st3[s]], writes=[R_st3[s]])
                    S.op(act, lambda e, s=s: e.activation(out=stat3[:, 4 + s:5 + s], in_=stat3[:, 4 + s:5 + s], func=AF.Exp, scale=-0.5),
                         reads=[R_st3[s]], writes=[R_st3[s]])
                    os_ = not_ % 2; not_ += 1
                    S.op(dve, lambda e, s=s, os_=os_: e.scalar_tensor_tensor(out=ot[os_][:], in0=hblk[:, s, :], scalar=stat3[:, 4 + s:5 + s],
                                                                            in1=gfin[:], op0=ALU.mult, op1=ALU.mult),
                         reads=[R_h[s], R_st3[s], R_gfin], writes=[R_ot[os_]])
                    S.dma(pool, out_d[b * TB + s * P:b * TB + (s + 1) * P, :], ot[os_][:], d_ot[os_], reads=[R_ot[os_]])
                    if b + 1 < NB:
                        S.dma(sp, hblk[:, s, :], x_d[(b + 1) * TB + s * P:(b + 1) * TB + (s + 1) * P, :], d_hs[s], writes=[R_h[s]])
            S.barrier(final=True)
        st_mix.close()
    return nc


def t5_bucket_np(rel):
    half = 16
    ret = np.where(rel > 0, half, 0)
    n = np.abs(rel)
    max_exact = half // 2
    nf = np.maximum(n, 1).astype(np.float32)
    large = max_exact + (np.log(nf / max_exact) / math.log(128 / max_exact) * (half - max_exact)).astype(np.int32)
    large = np.minimum(large, half - 1)
    return ret + np.where(n < max_exact, n, large)


def host_constants():
    m = np.arange(GV_LEN)
    bucket = t5_bucket_np(m - 640)
    oh = np.zeros((32, GV_LEN), np.float32)
    oh[bucket, m] = 1.0
    ident = np.eye(P, dtype=np.float32).astype(ml_dtypes.bfloat16)
    aident = np.ascontiguousarray(np.eye(P, dtype=np.float32)[::-1]).astype(ml_dtypes.bfloat16)
    return oh, ident, aident


def pack_inputs(inp):
    f = np.float32
    prm = np.zeros((P, 64), f)
    prm[:, 0:8] = inp["attn_norm_g"][0].reshape(DC, P).T
    prm[:, 8:16] = inp["ffn_norm_g"][0].reshape(DC, P).T
    cw = inp["conv_w"][0]
    prm[:, 16:32] = cw.reshape(4, 4, P).transpose(2, 1, 0).reshape(P, 16)
    prm[:, 32:36] = inp["conv_b"][0].reshape(4, P).T
    prm[:, 36:44] = inp["b_rg"][0].reshape(2, 4, P).transpose(2, 0, 1).reshape(P, 8)
    prm[:, 44:52] = inp["b_ig"][0].reshape(2, 4, P).transpose(2, 0, 1).reshape(P, 8)
    prm[:, 52:60] = inp["lru_lambda"][0].reshape(2, 4, P).transpose(2, 0, 1).reshape(P, 8)
    prm[:, 60] = inp["subln_g"][0]
    lamv = np.zeros((P, 256), f)
    lamv[:, 0:64] = inp["lambda_q1"][0][None]
    lamv[:, 64:128] = inp["lambda_k1"][0][None]
    lamv[:, 128:192] = inp["lambda_q2"][0][None]
    lamv[:, 192:256] = inp["lambda_k2"][0][None]
    gfin = np.ascontiguousarray(np.broadcast_to(inp["final_norm_g"][None, :], (P, D))).astype(f)
    wbd = np.zeros((P, 16, P), f)
    for d in range(2):
        for typ, key in enumerate(("w_rg", "w_ig")):
            w = inp[key][0, d]
            for c in range(4):
                m = d * 8 + typ * 4 + c
                wbd[0:64, m, 0:64] = w[2 * c]
                wbd[64:128, m, 64:128] = w[2 * c + 1]
    oh, ident, aident = host_constants()
    shared = {
        "w_in": np.ascontiguousarray(inp["w_in"][0]), "w_out": np.ascontiguousarray(inp["w_out"][0]),
        "w_gate": np.ascontiguousarray(inp["w_gate"][0]), "w_up": np.ascontiguousarray(inp["w_up"][0]),
        "w_down": np.ascontiguousarray(inp["w_down"][0]),
        "prm": prm, "lamv": lamv, "gfin": gfin, "wbd": wbd,
        "relb": np.ascontiguousarray(inp["rel_bias"]).astype(f), "oh": oh, "ident": ident, "aident": aident,
    }
    return shared


def kernel(**inputs):
    inp = {k: np.asarray(v) for k, v in inputs.items()}
    shared = pack_inputs(inp)
    nc = build_nc()
    x = inp["x"]
    in_maps = []
    for c in range(8):
        m = dict(shared)
        m["x"] = np.ascontiguousarray(x[c])
        in_maps.append(m)
    res = run_bass_kernel_spmd(nc, in_maps, core_ids=list(range(8)))
    out = np.stack([np.asarray(r["out"]) for r in res.results], axis=0)
    return out.astype(np.float32)
```

```python
import math
import contextlib
import numpy as np
import ml_dtypes
import concourse.bass as bass
import concourse.mybir as mybir
from concourse.bass_utils import run_bass_kernel_spmd

F32 = mybir.dt.float32
BF16 = mybir.dt.bfloat16
ALU = mybir.AluOpType
AF = mybir.ActivationFunctionType

P = 128
T = 4096
D = 1024
NT = T // P
TB = 512
NB = T // TB
DC = D // P
D_IN = 2560
D_FF = 2816
FC = D_FF // P
NH = 4
EPS = 1e-6
LAMBDA_INIT = 0.8 - 0.6 * math.exp(-0.3 * 0)
GV_LEN = 1280
STRIP_W = 1153
GELU_C1 = 2.0 * math.sqrt(2.0 / math.pi)
GELU_C2 = GELU_C1 * 0.044715


class Res:
    __slots__ = ("name", "w", "rs")

    def __init__(self, name=""):
        self.name = name
        self.w = None
        self.rs = []


class DmaSem:
    def __init__(self, nc, es, name):
        self.sem = es.enter_context(nc.semaphore(name))
        self.cnt = 0


class Eng:
    def __init__(self, nc, es, e, name, is_pe=False):
        self.e = e
        self.name = name
        self.sem = es.enter_context(nc.semaphore("s_" + name))
        self.cnt = 0
        self.seen = {}
        self.is_pe = is_pe

    def wait(self, tok):
        sem, val = tok[0], tok[1]
        k = tok[3]
        if self.seen.get(k, 0) >= val:
            return
        self.e.wait_ge(sem, val)
        self.seen[k] = val


class Sched:
    def __init__(self, nc, es):
        self.nc = nc
        self.es = es
        self.pe = Eng(nc, es, nc.tensor, "pe", is_pe=True)
        self.act = Eng(nc, es, nc.scalar, "act")
        self.dve = Eng(nc, es, nc.vector, "dve")
        self.pool = Eng(nc, es, nc.gpsimd, "pool")
        self.sp = Eng(nc, es, nc.sync, "sp")
        self.engs = [self.pe, self.act, self.dve, self.pool, self.sp]
        self.dsems = []
        self.all_dsems = []
        self.nsem = 0

    def dsem(self, name, in_barrier=True):
        d = DmaSem(self.nc, self.es, name)
        if in_barrier:
            self.dsems.append(d)
        self.all_dsems.append(d)
        return d

    def _need(self, eng, tok, raw):
        if tok[2] is eng and eng.is_pe:
            return
        eng.wait(tok)

    def _deps(self, eng, reads, writes):
        for r in reads:
            if r.w is not None:
                self._need(eng, r.w, True)
        for w in writes:
            if w.w is not None:
                self._need(eng, w.w, False)
            for t in w.rs:
                self._need(eng, t, False)

    def _record(self, tok, reads, writes):
        for r in reads:
            rs = [t for t in r.rs if t[3] != tok[3]]
            rs.append(tok)
            r.rs = rs
        for w in writes:
            w.w = tok
            w.rs = []

    def op(self, eng, fn, reads=(), writes=(), inc=True):
        self._deps(eng, reads, writes)
        ins = fn(eng.e)
        if inc:
            eng.cnt += 1
            ins.then_inc(eng.sem, 1)
            tok = (eng.sem, eng.cnt, eng, eng.name)
        else:
            tok = (eng.sem, eng.cnt + 1, eng, eng.name)
        self._record(tok, reads, writes)
        return ins

    def dma(self, q, out, in_, dsem, reads=(), writes=()):
        self._deps(q, reads, writes)
        ins = q.e.dma_start(out=out, in_=in_)
        dsem.cnt += 16
        ins.then_inc(dsem.sem, 16)
        tok = (dsem.sem, dsem.cnt, None, id(dsem))
        self._record(tok, reads, writes)
        return ins

    def barrier(self, final=False):
        for e in self.engs:
            if final:
                for d in self.all_dsems:
                    if d.cnt > 0:
                        e.wait((d.sem, d.cnt, None, id(d)))
            for f in self.engs:
                if f is not e and f.cnt > 0:
                    e.wait((f.sem, f.cnt, f, f.name))
            for d in self.dsems:
                if d.cnt > 0:
                    e.wait((d.sem, d.cnt, None, id(d)))


def build_nc(stage=99, dbg=None):
    nc = bass.Bass("TRN2", target_bir_lowering=False)

    def din(name, shape, dt=F32):
        return nc.dram_tensor(name, list(shape), dt, kind="ExternalInput")

    x_d = din("x", [T, D]).ap()
    w_in_d = din("w_in", [D, D_IN]).ap()
    w_out_d = din("w_out", [D, D]).ap()
    w_gate_d = din("w_gate", [D, D_FF]).ap()
    w_up_d = din("w_up", [D, D_FF]).ap()
    w_down_d = din("w_down", [D_FF, D]).ap()
    prm_d = din("prm", [P, 64]).ap()
    lamv_d = din("lamv", [P, 256]).ap()
    gfin_d = din("gfin", [P, D]).ap()
    wbd_d = din("wbd", [P, 16, P]).ap()
    relb_d = din("relb", [32, 4]).ap()
    oh_d = din("oh", [32, GV_LEN]).ap()
    ident_d = din("ident", [P, P], BF16).ap()
    aident_d = din("aident", [P, P], BF16).ap()
    out_d = nc.dram_tensor("out", [T, D], F32, kind="ExternalOutput").ap()

    w_in_b = nc.dram_tensor("w_in_b", [D, D_IN], BF16, kind="Internal").ap()
    w_out_b = nc.dram_tensor("w_out_b", [D, D], BF16, kind="Internal").ap()
    w_gate_b = nc.dram_tensor("w_gate_b", [D, D_FF], BF16, kind="Internal").ap()
    w_up_b = nc.dram_tensor("w_up_b", [D, D_FF], BF16, kind="Internal").ap()
    w_down_b = nc.dram_tensor("w_down_b", [D_FF, D], BF16, kind="Internal").ap()
    gv_t = nc.dram_tensor("gv_scr", [NH, GV_LEN], F32, kind="Internal")
    dbg_outs = {}
    if dbg:
        for nm, shp in dbg.items():
            dbg_outs[nm] = nc.dram_tensor("dbg_" + nm, list(shp), F32, kind="ExternalOutput").ap()

    es = contextlib.ExitStack()
    with es:
        S = Sched(nc, es)
        pe, act, dve, pool, sp = S.pe, S.act, S.dve, S.pool, S.sp

        def sbuf(st, name, shape, dt, side="right"):
            return st.enter_context(nc.sbuf_tensor("sb_" + name, list(shape), dt, side=side))

        def lsbuf(st, name, shape, dt):
            return sbuf(st, name, shape, dt, side="left")

        def psum(st, name, shape, dt=F32):
            return st.enter_context(nc.psum_tensor("ps_" + name, list(shape), dt))

        prm = lsbuf(es, "prm", [P, 64], F32)
        lamv = lsbuf(es, "lamv", [P, 256], F32)
        ident = lsbuf(es, "ident", [P, P], BF16)
        aident = lsbuf(es, "aident", [P, P], BF16)
        ones_b = lsbuf(es, "ones_b", [P, P], BF16)
        ones_f = lsbuf(es, "ones_f", [P, P], F32)
        wbd = lsbuf(es, "wbd", [P, 16, P], BF16)
        cst = lsbuf(es, "cst", [P, 48], F32)
        cbias = lsbuf(es, "cbias", [P, 2 * NH], F32)
        mixL = lsbuf(es, "mixL", [P, 4, T], BF16)
        R_q = [[Res() for b in range(NB)] for h in range(NH)]
        R_k = [[Res() for b in range(NB)] for h in range(NH)]
        R_v = [Res() for i in range(NT)]
        R_ones = Res("ones"); R_prm = Res("prm"); R_cst = Res("cst"); R_strip = Res("strip"); R_const = Res("const")
        R_wbd = Res("wbd")
        R_mix = [[Res("mix%d_%d" % (c, b)) for b in range(NB)] for c in range(DC)]
        R_wscr = {k: Res("wscr_" + k) for k in ("in_lru", "in_qkv", "out", "gate", "up", "down")}

        PC_G1 = 0
        PC_G2 = 8
        PC_CW = 16
        PC_CB = 32
        PC_BR = 36
        PC_BI = 44
        PC_LL = 52
        PC_SG = 60
        CC_NLAM = 0
        CC_SG = 1
        CC_CS = 2
        CC_CS2 = 10
        CC_NB = 18
        d_prm = S.dsem("d_prm")
        d_wbd = S.dsem("d_wbd", in_barrier=False)
        cast_list = (("in_lru", w_in_d[:, 1536:D_IN], w_in_b[:, 1536:D_IN], D, 1024),
                     ("in_qkv", w_in_d[:, 0:1536], w_in_b[:, 0:1536], D, 1536), ("out", w_out_d, w_out_b, D, D),
                     ("gate", w_gate_d, w_gate_b, D, D_FF), ("up", w_up_d, w_up_b, D, D_FF),
                     ("down", w_down_d, w_down_b, D_FF, D))

        def emit_casts(keys, after=()):
            for key, src, dst, rows, cols in sorted((c for c in cast_list if c[0] in keys), key=lambda c: keys.index(c[0])):
                d_c = S.dsem("d_cast_" + key, in_barrier=False)
                for r0 in range(0, rows, 256):
                    S.dma(pool, dst[r0:r0 + 256, :], src[r0:r0 + 256, :], d_c, reads=list(after), writes=[R_wscr[key]])

        S.op(pool, lambda e: e.memset(ones_b[:], 1.0), writes=[R_ones])
        S.op(pool, lambda e: e.memset(ones_f[:], 1.0), writes=[R_ones])
        for (o, i) in ((prm, prm_d), (lamv, lamv_d), (ident, ident_d), (aident, aident_d)):
            S.dma(sp, o[:], i, d_prm, writes=[R_prm, R_const])
        emit_casts(("in_lru",), after=[R_prm])
        S.dma(pool, wbd[:], wbd_d, d_wbd, reads=[R_prm], writes=[R_wbd])
        if True:
            tmpl = lsbuf(es, "tmpl", [P, 64], F32)
            tmpc = lsbuf(es, "tmpc", [P, 16], F32)
            R_tmp = Res()
            for i in range(2):
                S.op(dve, lambda e, i=i: e.tensor_tensor(out=tmpl[:], in0=lamv[:, i * 128:i * 128 + 64],
                                                         in1=lamv[:, i * 128 + 64:i * 128 + 128], op=ALU.mult),
                     reads=[R_prm], writes=[R_tmp])
                S.op(dve, lambda e, i=i: e.reduce_sum(out=tmpc[:, i:i + 1], in_=tmpl[:], axis=mybir.AxisListType.X),
                     reads=[R_tmp], writes=[R_tmp])
            S.op(act, lambda e: e.activation(out=tmpc[:, 2:4], in_=tmpc[:, 0:2], func=AF.Exp), reads=[R_tmp], writes=[R_tmp])
            S.op(dve, lambda e: e.scalar_tensor_tensor(out=cst[:, CC_NLAM:CC_NLAM + 1], in0=tmpc[:, 3:4], scalar=-LAMBDA_INIT,
                                                       in1=tmpc[:, 2:3], op0=ALU.add, op1=ALU.subtract),
                 reads=[R_tmp], writes=[R_cst])
            S.op(dve, lambda e: e.tensor_scalar(out=cst[:, CC_SG:CC_SG + 1], in0=prm[:, PC_SG:PC_SG + 1],
                                                scalar1=(1.0 - LAMBDA_INIT), scalar2=None, op0=ALU.mult),
                 reads=[R_prm], writes=[R_cst])
            S.op(act, lambda e: e.activation(out=tmpc[:, 4:12], in_=prm[:, PC_LL:PC_LL + 8], func=AF.Exp, scale=-1.0),
                 reads=[R_prm], writes=[R_tmp])
            S.op(act, lambda e: e.activation(out=tmpc[:, 4:12], in_=tmpc[:, 4:12], func=AF.Ln, bias=1.0),
                 reads=[R_tmp], writes=[R_tmp])
            S.op(dve, lambda e: e.tensor_scalar(out=cst[:, CC_CS:CC_CS + 8], in0=tmpc[:, 4:12], scalar1=-8.0, scalar2=None,
                                                op0=ALU.mult), reads=[R_tmp], writes=[R_cst])
            S.op(dve, lambda e: e.tensor_scalar(out=cst[:, CC_CS2:CC_CS2 + 8], in0=tmpc[:, 4:12], scalar1=-16.0, scalar2=None,
                                                op0=ALU.mult), reads=[R_tmp], writes=[R_cst])
            S.op(dve, lambda e: e.tensor_scalar(out=cst[:, CC_NB:CC_NB + 16], in0=prm[:, PC_BR:PC_BR + 16], scalar1=-1.0,
                                                scalar2=None, op0=ALU.mult), reads=[R_prm], writes=[R_cst])

        def setup_strips(strip32):
            st0 = contextlib.ExitStack()
            with st0:
                relb = sbuf(st0, "relb", [32, 4], F32)
                oh = sbuf(st0, "oh", [32, GV_LEN], F32)
                gv_sb = sbuf(st0, "gv_sb", [NH, GV_LEN], F32)
                ps_gv = psum(st0, "ps_gv", [NH, 3, 512], F32)
                R_rel = Res(); R_gv = Res(); R_psgv = Res(); R_tmp = Res()
                d_gv = S.dsem("d_gv")
                d_rel = S.dsem("d_rel")
                S.dma(sp, relb[:], relb_d, d_rel, writes=[R_rel])
                S.dma(sp, oh[:], oh_d, d_rel, writes=[R_rel])
                for k in range(3):
                    n = min(512, GV_LEN - k * 512)
                    S.op(pe, lambda e, k=k, n=n: e.matmul(ps_gv[:, k, 0:n], lhsT=relb[:], rhs=oh[:, k * 512:k * 512 + n],
                                                          start=True, stop=True), reads=[R_rel], writes=[R_psgv])
                for k in range(3):
                    n = min(512, GV_LEN - k * 512)
                    S.op(dve, lambda e, k=k, n=n: e.tensor_copy(out=gv_sb[:, k * 512:k * 512 + n], in_=ps_gv[:, k, 0:n]),
                         reads=[R_psgv], writes=[R_gv])
                S.dma(sp, gv_t.ap(), gv_sb[:], d_gv, reads=[R_gv], writes=[R_tmp])
                for h in range(NH):
                    src = bass.AP(tensor=gv_t, offset=h * GV_LEN, ap=[[1, P], [1, STRIP_W]])
                    S.dma(sp, strip32[:, h, :], src, d_gv, reads=[R_tmp], writes=[R_strip])
                S.barrier()

        st1 = contextlib.ExitStack()
        with st1:
            nT = sbuf(st1, "nT", [P, DC, T], BF16)
            R_nT = [Res("nT%d" % i) for i in range(NT)]
            st1a = contextlib.ExitStack()
            with st1a:
                XR = 3
                xt = [sbuf(st1a, "xt%d" % i, [P, D], F32) for i in range(XR)]
                R_xt = [Res() for _ in range(XR)]
                d_xt = [S.dsem("d_xt%d" % i) for i in range(XR)]
                nb = [sbuf(st1a, "nb%d" % i, [P, D], BF16) for i in range(2)]
                R_nb = [Res() for _ in range(2)]
                junk = sbuf(st1a, "junk", [P, D], BF16)
                R_junk = Res()
                stat = sbuf(st1a, "stat", [P, 3 * NT], F32)
                R_st = [Res() for _ in range(NT)]
                ptr = [psum(st1a, "ptr%d" % i, [P, DC, P], BF16) for i in range(2)]
                R_ptr = [Res() for _ in range(2)]
                def norm_back(i):
                    s2 = i % 2
                    S.op(dve, lambda e: e.tensor_tensor(
                        out=nT[:, :, i * P:(i + 1) * P], in0=ptr[s2][:],
                        in1=prm[:, PC_G1:PC_G1 + DC].unsqueeze(2).to_broadcast([P, DC, P]), op=ALU.mult),
                         reads=[R_ptr[s2], R_prm], writes=[R_nT[i]])

                for i in range(NT):
                    s3, s2 = i % XR, i % 2
                    S.dma(sp, xt[s3][:], x_d[i * P:(i + 1) * P, :], d_xt[s3], writes=[R_xt[s3]])
                    S.op(act, lambda e, i=i, s3=s3: e.activation(out=junk[:], in_=xt[s3][:], func=AF.Square,
                                                                 accum_out=stat[:, i:i + 1]),
                         reads=[R_xt[s3]], writes=[R_junk, R_st[i]])
                    S.op(act, lambda e, i=i: e.activation(out=stat[:, NT + i:NT + i + 1], in_=stat[:, i:i + 1], func=AF.Ln,
                                                          scale=1.0 / D, bias=EPS), reads=[R_st[i]], writes=[R_st[i]])
                    S.op(act, lambda e, i=i: e.activation(out=stat[:, 2 * NT + i:2 * NT + i + 1],
                                                          in_=stat[:, NT + i:NT + i + 1], func=AF.Exp, scale=-0.5),
                         reads=[R_st[i]], writes=[R_st[i]])
                    S.op(dve, lambda e, i=i, s3=s3, s2=s2: e.tensor_scalar(out=nb[s2][:], in0=xt[s3][:],
                                                                          scalar1=stat[:, 2 * NT + i:2 * NT + i + 1],
                                                                          scalar2=None, op0=ALU.mult),
                         reads=[R_xt[s3], R_st[i]], writes=[R_nb[s2]])
                    for j in range(DC):
                        S.op(pe, lambda e, j=j, s2=s2: e.transpose(out=ptr[s2][:, j, :], in_=nb[s2][:, j * P:(j + 1) * P],
                                                                   identity=ident[:]),
                             reads=[R_nb[s2], R_const], writes=[R_ptr[s2]], inc=(j == DC - 1))
                    if i >= 1:
                        norm_back(i - 1)
                norm_back(NT - 1)
                emit_casts(("in_qkv", "out", "down", "gate", "up"), after=[R_xt[(NT - 1) % XR]])
                S.barrier()
            if "nT" in dbg_outs:
                stx = contextlib.ExitStack()
                with stx:
                    tmpf = sbuf(stx, "tmpf", [P, DC, 512], F32)
                    R_t = Res(); d_dbg = S.dsem("d_dbg1")
                    for b in range(NB):
                        S.op(dve, lambda e, b=b: e.tensor_copy(out=tmpf[:], in_=nT[:, :, b * TB:(b + 1) * TB]),
                             reads=R_nT, writes=[R_t])
                        S.dma(sp, dbg_outs["nT"][:, :, b * TB:(b + 1) * TB], tmpf[:], d_dbg, reads=[R_t])
                    S.barrier()
            if stage <= 1:
                return nc
            w_in_v = w_in_b.rearrange("(j p) n -> p j n", p=P)
            st1b = contextlib.ExitStack()
            with st1b:
                XRp = sbuf(st1b, "XRp", [P, T + 4], F32)
                Gb = sbuf(st1b, "Gb", [P, T], BF16)
                XC = sbuf(st1b, "XC", [P, T], F32)
                XCb = sbuf(st1b, "XCb", [P, T], BF16)
                Hf = sbuf(st1b, "Hf", [P, T], F32)
                wxr = sbuf(st1b, "wxr", [P, DC, P], BF16)
                wgr = sbuf(st1b, "wgr", [P, DC, P], BF16)
                tA = [sbuf(st1b, "tA_%d" % i, [P, TB], F32) for i in range(2)]
                tA2 = [sbuf(st1b, "tA2_%d" % i, [P, TB], F32) for i in range(2)]
                tU = [sbuf(st1b, "tU_%d" % i, [P, TB], F32) for i in range(2)]
                Rb = sbuf(st1b, "Rb", [P, T], BF16)
                IXb = sbuf(st1b, "IXb", [P, T], BF16)
                R_Rb = [Res() for _ in range(NB)]; R_IX = [Res() for _ in range(NB)]
                pz = [psum(st1b, "pz%d" % i, [P, TB], F32) for i in range(4)]
                R_XR = Res(); R_Gb = Res(); R_XC = Res(); R_XCb = Res(); R_Hf = Res()
                R_wxr = Res(); R_wgr = Res()
                R_E1 = [Res(), Res()]; R_A = [Res(), Res()]; R_A2 = [Res(), Res()]; R_E2 = [Res(), Res()]; R_U = [Res(), Res()]
                R_pz = [Res() for _ in range(4)]
                d_wxr = S.dsem("d_wxr"); d_wgr = S.dsem("d_wgr")
                S.op(dve, lambda e: e.memset(XRp[:, 0:2], 0.0), writes=[R_XR])
                S.op(dve, lambda e: e.memset(XRp[:, T + 2:T + 4], 0.0), writes=[R_XR])
                for c in range(4):
                    S.dma(sp, wxr[:], w_in_v[:, :, 1536 + c * P:1536 + (c + 1) * P], d_wxr, reads=[R_wscr["in_lru"]], writes=[R_wxr])
                    S.dma(sp, wgr[:], w_in_v[:, :, 2048 + c * P:2048 + (c + 1) * P], d_wgr, reads=[R_wscr["in_lru"]], writes=[R_wgr])
                    for b in range(NB):
                        blk = slice(b * TB, (b + 1) * TB)
                        px, pg_ = b % 2, 2 + b % 2
                        for j in range(DC):
                            S.op(pe, lambda e, j=j, px=px, blk=blk: e.matmul(pz[px][:], lhsT=wxr[:, j, :], rhs=nT[:, j, blk],
                                                                             start=(j == 0), stop=(j == DC - 1)),
                                 reads=[R_wxr] + R_nT[4 * b:4 * b + 4], writes=[R_pz[px]], inc=(j == DC - 1))
                        for j in range(DC):
                            S.op(pe, lambda e, j=j, pg_=pg_, blk=blk: e.matmul(pz[pg_][:], lhsT=wgr[:, j, :], rhs=nT[:, j, blk],
                                                                               start=(j == 0), stop=(j == DC - 1)),
                                 reads=[R_wgr] + R_nT[4 * b:4 * b + 4], writes=[R_pz[pg_]], inc=(j == DC - 1))
                        S.op(act, lambda e, px=px, b=b: e.copy(out=XRp[:, 2 + b * TB:2 + (b + 1) * TB], in_=pz[px][:]),
                             reads=[R_pz[px]], writes=[R_XR])
                        S.op(act, lambda e, pg_=pg_, blk=blk: e.copy(out=Gb[:, blk], in_=pz[pg_][:]),
                             reads=[R_pz[pg_]], writes=[R_Gb])
                    T1 = Hf[:]
                    S.op(dve, lambda e: e.tensor_tensor(out=T1, in0=Gb[:], in1=Gb[:], op=ALU.mult), reads=[R_Gb], writes=[R_Hf])
                    S.op(dve, lambda e: e.tensor_scalar(out=T1, in0=T1, scalar1=GELU_C2, scalar2=GELU_C1, op0=ALU.mult, op1=ALU.add),
                         reads=[R_Hf], writes=[R_Hf])
                    S.op(dve, lambda e: e.tensor_tensor(out=T1, in0=T1, in1=Gb[:], op=ALU.mult), reads=[R_Hf, R_Gb], writes=[R_Hf])
                    cw0 = PC_CW + c * 4
                    S.op(act, lambda e, c=c, cw0=cw0: e.activation(out=XC[:], in_=XRp[:, 0:T], func=AF.Identity,
                                                                   scale=prm[:, cw0:cw0 + 1], bias=prm[:, PC_CB + c:PC_CB + c + 1]),
                         reads=[R_XR, R_prm], writes=[R_XC])
                    S.op(act, lambda e: e.activation(out=T1, in_=T1, func=AF.Exp, scale=-1.0), reads=[R_Hf], writes=[R_Hf])
                    S.op(act, lambda e: e.activation(out=T1, in_=T1, func=AF.Ln, bias=1.0), reads=[R_Hf], writes=[R_Hf])
                    S.op(act, lambda e: e.activation(out=T1, in_=T1, func=AF.Exp, scale=-1.0), reads=[R_Hf], writes=[R_Hf])
                    for tap in range(1, 4):
                        S.op(dve, lambda e, tap=tap, cw0=cw0: e.scalar_tensor_tensor(
                            out=XC[:], in0=XRp[:, tap:tap + T], scalar=prm[:, cw0 + tap:cw0 + tap + 1], in1=XC[:],
                            op0=ALU.mult, op1=ALU.add), reads=[R_XR, R_XC, R_prm], writes=[R_XC])
                    S.op(act, lambda e: e.copy(out=XCb[:], in_=XC[:]), reads=[R_XC], writes=[R_XCb])
                    HBv = XRp
                    for d in range(2):
                        m_r = d * 8 + c
                        m_i = d * 8 + 4 + c
                        col = d * 4 + c
                        order = list(range(NB)) if d == 0 else list(range(NB - 1, -1, -1))
                        for bi, b in enumerate(order):
                            blk = slice(b * TB, (b + 1) * TB)
                            pr, pi_ = bi % 2, 2 + bi % 2
                            S.op(pe, lambda e, pr=pr, blk=blk: e.matmul(pz[pr][:], lhsT=wbd[:, m_r, :], rhs=XCb[:, blk], start=True, stop=True),
                                 reads=[R_wbd, R_XCb], writes=[R_pz[pr]])
                            S.op(pe, lambda e, pi_=pi_, blk=blk: e.matmul(pz[pi_][:], lhsT=wbd[:, m_i, :], rhs=XCb[:, blk], start=True, stop=True),
                                 reads=[R_wbd, R_XCb], writes=[R_pz[pi_]])
                            S.op(act, lambda e, pr=pr, blk=blk: e.activation(out=Rb[:, blk], in_=pz[pr][:], func=AF.Sigmoid,
                                                                             bias=prm[:, PC_BR + col:PC_BR + col + 1]),
                                 reads=[R_pz[pr], R_prm], writes=[R_Rb[b]])
                            S.op(act, lambda e, pi_=pi_, blk=blk: e.activation(out=IXb[:, blk], in_=pz[pi_][:], func=AF.Sigmoid,
                                                                               bias=prm[:, PC_BI + col:PC_BI + col + 1]),
                                 reads=[R_pz[pi_], R_prm], writes=[R_IX[b]])
                            S.op(dve, lambda e, blk=blk: e.tensor_tensor(out=IXb[:, blk], in0=IXb[:, blk], in1=XC[:, blk], op=ALU.mult),
                                 reads=[R_IX[b], R_XC], writes=[R_IX[b]])
                        if d == 0:
                            S.op(dve, lambda e: e.tensor_tensor(out=Gb[:], in0=Gb[:], in1=T1, op=ALU.mult), reads=[R_Hf, R_Gb], writes=[R_Gb])
                        for bi, b in enumerate(order):
                            blk = slice(b * TB, (b + 1) * TB)
                            k2 = bi % 2
                            A, A2, U = tA[k2], tA2[k2], tU[k2]
                            rA, rA2, rU = R_A[k2], R_A2[k2], R_U[k2]
                            S.op(act, lambda e, A=A, blk=blk: e.activation(out=A[:], in_=Rb[:, blk], func=AF.Exp,
                                                                           scale=cst[:, CC_CS + col:CC_CS + col + 1]),
                                 reads=[R_Rb[b], R_cst], writes=[rA])
                            S.op(act, lambda e, A2=A2, blk=blk: e.activation(out=A2[:], in_=Rb[:, blk], func=AF.Exp,
                                                                             scale=cst[:, CC_CS2 + col:CC_CS2 + col + 1]),
                                 reads=[R_Rb[b], R_cst], writes=[rA2])
                            S.op(act, lambda e, A2=A2: e.activation(out=A2[:], in_=A2[:], func=AF.Ln, scale=-1.0, bias=1.0),
                                 reads=[rA2], writes=[rA2])
                            S.op(act, lambda e, A2=A2: e.activation(out=A2[:], in_=A2[:], func=AF.Exp, scale=0.5), reads=[rA2], writes=[rA2])
                            S.op(dve, lambda e, blk=blk, U=U, A2=A2: e.tensor_tensor(out=U[:], in0=A2[:], in1=IXb[:, blk], op=ALU.mult),
                                 reads=[rA2, R_IX[b]], writes=[rU])
                            if d == 0:
                                init = 0.0 if b == 0 else Hf[:, b * TB - 1:b * TB]
                                S.op(dve, lambda e, blk=blk, init=init, A=A, U=U: e.tensor_tensor_scan(
                                    out=Hf[:, blk], data0=A[:], data1=U[:], initial=init, op0=ALU.mult, op1=ALU.add),
                                     reads=[rA, rU, R_Hf], writes=[R_Hf])
                            else:
                                lo, hi = 2 + b * TB, 2 + (b + 1) * TB
                                init = 0.0 if bi == 0 else HBv[:, hi:hi + 1]
                                S.op(dve, lambda e, lo=lo, hi=hi, init=init, A=A, U=U: e.tensor_tensor_scan(
                                    out=HBv[:, hi - 1:lo - 1:-1], data0=A[:, ::-1], data1=U[:, ::-1], initial=init,
                                    op0=ALU.mult, op1=ALU.add),
                                     reads=[rA, rU, R_XR], writes=[R_XR])
                    S.op(dve, lambda e: e.tensor_tensor(out=Hf[:], in0=Hf[:], in1=HBv[:, 2:T + 2], op=ALU.add),
                         reads=[R_Hf, R_XR], writes=[R_Hf])
                    S.op(dve, lambda e, c=c: e.tensor_tensor(out=mixL[:, c, :], in0=Hf[:], in1=Gb[:], op=ALU.mult),
                         reads=[R_Hf, R_Gb], writes=R_mix[4 + c])
                S.barrier()
            if stage <= 2:
                return nc
            st_qkv = contextlib.ExitStack()
            st_qkv.__enter__()
            qT = lsbuf(st_qkv, "qT", [P, NH, T], BF16)
            kT = lsbuf(st_qkv, "kT", [P, NH, T], BF16)
            Vt = lsbuf(st_qkv, "Vt", [P, NT, 512], BF16)
            st1c = contextlib.ExitStack()
            with st1c:
                pq = [psum(st1c, "pq%d" % i, [P, TB], F32) for i in range(4)]
                R_pq = [Res() for _ in range(4)]
                stv = contextlib.ExitStack()
                with stv:
                    wv = sbuf(stv, "wv", [P, DC, 256], BF16)
                    R_wv = Res(); d_wv = S.dsem("d_wv")
                    for vh in range(2):
                        S.dma(sp, wv[:], w_in_v[:, :, 1024 + vh * 256:1024 + (vh + 1) * 256], d_wv, reads=[R_wscr["in_qkv"]], writes=[R_wv])
                        for i in range(NT):
                            pp = i % 4
                            for j in range(DC):
                                S.op(pe, lambda e, j=j, pp=pp, i=i: e.matmul(pq[pp][:, 0:256], lhsT=nT[:, j, i * P:(i + 1) * P], rhs=wv[:, j, :],
                                                                           start=(j == 0), stop=(j == DC - 1)),
                                     reads=[R_wv, R_nT[i]], writes=[R_pq[pp]], inc=(j == DC - 1))
                            vcols = slice(vh * 256, (vh + 1) * 256)
                            if i % 2 == 0:
                                S.op(act, lambda e, pp=pp, i=i, vcols=vcols: e.copy(out=Vt[:, i, vcols], in_=pq[pp][:, 0:256]),
                                     reads=[R_pq[pp]], writes=[R_v[i]])
                            else:
                                S.op(dve, lambda e, pp=pp, i=i, vcols=vcols: e.tensor_copy(out=Vt[:, i, vcols], in_=pq[pp][:, 0:256]),
                                     reads=[R_pq[pp]], writes=[R_v[i]])
                    S.barrier()
                wqk = [sbuf(st1c, "wqk%d" % i, [P, DC, P], BF16) for i in range(2)]
                R_wqk = [Res(), Res()]
                d_wqk = [S.dsem("d_wqk0"), S.dsem("d_wqk1")]
                n = 0
                for h in range(NH):
                    for typ in range(2):
                        ws = n % 2
                        c0 = typ * 512 + h * P
                        S.dma(sp, wqk[ws][:], w_in_v[:, :, c0:c0 + P], d_wqk[ws], reads=[R_wscr["in_qkv"]], writes=[R_wqk[ws]])
                        for b in range(NB):
                            blk = slice(b * TB, (b + 1) * TB)
                            pp = (n * NB + b) % 4
                            for j in range(DC):
                                S.op(pe, lambda e, j=j, pp=pp, blk=blk, ws=ws: e.matmul(pq[pp][:], lhsT=wqk[ws][:, j, :], rhs=nT[:, j, blk],
                                                                                       start=(j == 0), stop=(j == DC - 1)),
                                     reads=[R_wqk[ws]] + R_nT[4 * b:4 * b + 4], writes=[R_pq[pp]], inc=(j == DC - 1))
                            if typ == 0:
                                S.op(act, lambda e, pp=pp, blk=blk, h=h: e.activation(out=qT[:, h, blk], in_=pq[pp][:], func=AF.Copy, scale=0.125),
                                     reads=[R_pq[pp]], writes=[R_q[h][b]])
                            else:
                                S.op(dve, lambda e, pp=pp, blk=blk, h=h: e.tensor_copy(out=kT[:, h, blk], in_=pq[pp][:]),
                                     reads=[R_pq[pp]], writes=[R_k[h][b]])
                        n += 1
                S.barrier()
        if stage <= 3:
            st_qkv.close()
            return nc
        st_mix = contextlib.ExitStack()
        st_mix.__enter__()
        mixA = sbuf(st_mix, "mixA", [P, 4, T], BF16)
        mixs = (mixA, mixL)
        strip32 = lsbuf(st_qkv, "strip32", [P, NH, STRIP_W], F32)
        setup_strips(strip32)
        st2 = contextlib.ExitStack()
        with st2:
            NPT = 4
            NSL = 3
            PT = [sbuf(st2, "PT%d" % i, [P, 2, TB], BF16) for i in range(NPT)]
            accb = [sbuf(st2, "accb0", [P, 2, TB], BF16)] * 2
            acc32 = sbuf(st2, "acc32", [P, 2, TB], F32)
            rz = sbuf(st2, "rz", [P, 2, TB], F32)
            oo = sbuf(st2, "oo", [P, 2, TB], F32)
            R_acc = [Res()] * 2; R_a32 = Res()
            R_rz = Res(); R_oo = Res()
            ps_s = [psum(st2, "ps_s%d" % i, [P, 2, TB], F32) for i in range(NSL)]
            ps_o = psum(st2, "ps_o", [P, 2, TB], F32)
            R_sl = [Res() for _ in range(NSL)]; R_PT = [Res() for _ in range(NPT)]; R_o = Res()
            units = [(h, qb) for h in range(NH) for qb in range(NB)]
            steps = [(u, kt) for u in range(len(units)) for kt in range(NT)]

            def emit_S(g):
                u, kt = steps[g]
                h, qb = units[u]
                sl = g % NSL
                qblk = slice(qb * TB, (qb + 1) * TB)
                ktl = slice(kt * P, (kt + 1) * P)
                delta = kt * P - qb * TB
                near = (-218 < delta < 602)
                for w in range(2):
                    rows = slice(w * 64, (w + 1) * 64)
                    S.op(pe, lambda e, w=w, rows=rows: e.matmul(
                        ps_s[sl][:, w, :], lhsT=kT[rows, h, ktl], rhs=qT[rows, h, qblk], start=True, stop=True),
                         reads=[R_k[h][kt // 4], R_q[h][qb]], writes=[R_sl[sl]], inc=(w == 1))
                pl = g % NPT
                if near:
                    J = delta + 640
                    S.op(dve, lambda e: e.tensor_tensor(out=ps_s[sl][:], in0=ps_s[sl][:],
                                                        in1=strip32[:, h, J:J - TB:-1].unsqueeze(1).to_broadcast([P, 2, TB]), op=ALU.add),
                         reads=[R_sl[sl], R_strip], writes=[R_sl[sl]])
                    S.op(act, lambda e: e.activation(out=PT[pl][:], in_=ps_s[sl][:], func=AF.Exp),
                         reads=[R_sl[sl]], writes=[R_PT[pl]])
                else:
                    cc = STRIP_W - 1 if delta > 0 else 0
                    S.op(act, lambda e: e.activation(out=PT[pl][:], in_=ps_s[sl][:], func=AF.Exp, bias=strip32[:, h, cc:cc + 1]),
                         reads=[R_sl[sl], R_strip], writes=[R_PT[pl]])

            def emit_PV(g):
                u, kt = steps[g]
                h, qb = units[u]
                pl = g % NPT
                for w in range(2):
                    S.op(pe, lambda e, w=w: e.matmul(ps_o[:, w, :], lhsT=Vt[:, kt, h * P:(h + 1) * P], rhs=PT[pl][:, w, :],
                                                     start=(kt == 0), stop=(kt == NT - 1)),
                         reads=[R_v[kt], R_PT[pl]], writes=[R_o], inc=(w == 1))
                j = kt % 4
                ab = (g // 4) % 2
                if j == 1:
                    pp = (g - 1) % NPT
                    S.op(dve, lambda e: e.tensor_tensor(out=accb[ab][:], in0=PT[pp][:], in1=PT[pl][:], op=ALU.add),
                         reads=[R_PT[pp], R_PT[pl]], writes=[R_acc[ab]])
                elif j >= 2:
                    S.op(dve, lambda e: e.tensor_tensor(out=accb[ab][:], in0=accb[ab][:], in1=PT[pl][:], op=ALU.add),
                         reads=[R_PT[pl], R_acc[ab]], writes=[R_acc[ab]])
                if j == 3:
                    if kt == 3:
                        S.op(dve, lambda e: e.tensor_copy(out=acc32[:], in_=accb[ab][:]), reads=[R_acc[ab]], writes=[R_a32])
                    else:
                        S.op(dve, lambda e: e.tensor_tensor(out=acc32[:], in0=acc32[:], in1=accb[ab][:], op=ALU.add),
                             reads=[R_acc[ab], R_a32], writes=[R_a32])

            def epi_A1(u):
                S.op(act, lambda e: e.copy(out=oo[:], in_=ps_o[:]), reads=[R_o], writes=[R_oo])

            def ep_z(u, sla):
                for w in range(2):
                    S.op(pe, lambda e, w=w: e.matmul(ps_s[sla][:, w, :], lhsT=ones_f[:], rhs=acc32[:, w, :], start=True, stop=True),
                         reads=[R_a32, R_const, R_ones], writes=[R_sl[sla]], inc=(w == 1))
                S.op(act, lambda e: e.activation(out=rz[:], in_=ps_s[sla][:], func=AF.Ln), reads=[R_sl[sla]], writes=[R_rz])
                S.op(act, lambda e: e.activation(out=rz[:], in_=rz[:], func=AF.Exp, scale=-1.0), reads=[R_rz], writes=[R_rz])

            def ep_o(u):
                S.op(dve, lambda e: e.tensor_tensor(out=oo[:], in0=oo[:], in1=rz[:], op=ALU.mult), reads=[R_oo, R_rz], writes=[R_oo])
                S.op(dve, lambda e: e.scalar_tensor_tensor(out=oo[:, 0, :], in0=oo[:, 1, :], scalar=cst[:, CC_NLAM:CC_NLAM + 1],
                                                           in1=oo[:, 0, :], op0=ALU.mult, op1=ALU.add),
                     reads=[R_oo, R_cst], writes=[R_oo])

            def ep_sq(u):
                S.op(act, lambda e: e.activation(out=rz[:, 1, :], in_=oo[:, 0, :], func=AF.Square), reads=[R_oo, R_rz], writes=[R_rz])

            def ep_ss(u, slb):
                S.op(pe, lambda e: e.matmul(ps_s[slb][:, 0, :], lhsT=ones_f[:], rhs=rz[:, 1, :], start=True, stop=True),
                     reads=[R_rz, R_const, R_ones], writes=[R_sl[slb]])
                S.op(act, lambda e: e.activation(out=rz[:, 0, :], in_=ps_s[slb][:, 0, :], func=AF.Ln, scale=1.0 / P, bias=EPS),
                     reads=[R_sl[slb]], writes=[R_rz])
                S.op(act, lambda e: e.activation(out=rz[:, 0, :], in_=rz[:, 0, :], func=AF.Exp, scale=-0.5), reads=[R_rz], writes=[R_rz])

            def ep_out(u):
                h, qb = units[u]
                qblk = slice(qb * TB, (qb + 1) * TB)
                S.op(dve, lambda e: e.scalar_tensor_tensor(out=mixA[:, h, qblk], in0=oo[:, 0, :], scalar=cst[:, CC_SG:CC_SG + 1],
                                                           in1=rz[:, 0, :], op0=ALU.mult, op1=ALU.mult),
                     reads=[R_oo, R_rz, R_cst], writes=[R_mix[h][qb]])

            pend = None
            NS = len(steps)
            DEPTH = 2
            for g in range(NS + DEPTH):
                if g < NS:
                    emit_S(g)
                if g >= DEPTH:
                    gp = g - DEPTH
                    emit_PV(gp)
                    u1, kt1 = steps[gp]
                    if kt1 == NT - 1:
                        epi_A1(u1)
                        pend = u1
                        if gp == NS - 1:
                            ep_z(pend, 0); ep_o(pend); ep_sq(pend); ep_ss(pend, 1); ep_out(pend)
                            pend = None
                    elif pend is not None:
                        if kt1 == 2:
                            ep_z(pend, (g + 1) % NSL)
                        elif kt1 == 4:
                            ep_o(pend)
                        elif kt1 == 6:
                            ep_sq(pend)
                        elif kt1 == 8:
                            ep_ss(pend, (g + 1) % NSL)
                        elif kt1 == 10:
                            ep_out(pend)
                            pend = None
            S.barrier()
        st_qkv.close()
        if stage <= 4:
            st_mix.close()
            return nc
        st3 = contextlib.ExitStack()
        sbuf3 = lsbuf
        with st3:
            wout = lsbuf(st3, "wout", [P, DC, D], BF16)
            wd = lsbuf(st3, "wd", [P, FC, D], BF16)
            hblk = lsbuf(st3, "hblk", [P, 4, D], F32)
            nb2s = [lsbuf(st3, "nb2_%d" % i, [P, D], BF16) for i in range(2)]
            junk3 = lsbuf(st3, "junk3", [P, D], BF16)
            actT = lsbuf(st3, "actT", [P, FC, TB], BF16)
            wg = [lsbuf(st3, "wg%d" % i, [P, DC, 256], BF16) for i in range(2)]
            wu = [lsbuf(st3, "wu%d" % i, [P, DC, 256], BF16) for i in range(2)]
            sg = [lsbuf(st3, "sg%d" % i, [P, TB], F32) for i in range(2)]
            ot = [lsbuf(st3, "ot%d" % i, [P, D], F32) for i in range(2)]
            stat3 = lsbuf(st3, "stat3", [P, 8], F32)
            gfin = lsbuf(st3, "gfin", [P, D], F32)
            R_gfin = Res(); d_gfin = S.dsem("d_gfin")
            S.dma(sp, gfin[:], gfin_d, d_gfin, writes=[R_gfin])
            R_wout = Res(); R_wd = Res(); R_h = [Res() for _ in range(4)]; R_nb2s = [Res(), Res()]; R_j3 = Res(); R_act = [Res() for _ in range(FC)]
            R_wg = [Res(), Res()]; R_wu = [Res(), Res()]; R_sg = [Res(), Res()]; R_ot = [Res(), Res()]; R_st3 = [Res() for _ in range(4)]
            d_wout = S.dsem("d_wout"); d_wd = S.dsem("d_wd"); d_hs = [S.dsem("d_h%d" % i) for i in range(4)]
            d_wg = [S.dsem("d_wg0"), S.dsem("d_wg1")]; d_wu = [S.dsem("d_wu0"), S.dsem("d_wu1")]
            d_ot = [S.dsem("d_ot0"), S.dsem("d_ot1")]
            po = [psum(st3, "po%d" % i, [P, TB], F32) for i in range(2)]
            pg = [psum(st3, "pg%d" % i, [P, TB], F32) for i in range(2)]
            pu = [psum(st3, "pu%d" % i, [P, TB], F32) for i in range(2)]
            ptr3s = [psum(st3, "ptr3_%d" % i, [P, DC, P], BF16) for i in range(2)]
            R_po = [Res(), Res()]; R_pg = [Res(), Res()]; R_pu = [Res(), Res()]; R_ptr3s = [Res(), Res()]
            S.dma(sp, wout[:], w_out_b.rearrange("(e p) n -> p e n", p=P), d_wout, reads=[R_wscr["out"]], writes=[R_wout])
            S.dma(sp, wd[:], w_down_b.rearrange("(f p) n -> p f n", p=P), d_wd, reads=[R_wscr["down"]], writes=[R_wd])
            w_gate_v = w_gate_b.rearrange("(j p) n -> p j n", p=P)
            w_up_v = w_up_b.rearrange("(j p) n -> p j n", p=P)
            npo = 0
            not_ = 0
            nfg = 0
            for b in range(NB):
                if b == 0:
                    for s in range(4):
                        S.dma(sp, hblk[:, s, :], x_d[s * P:(s + 1) * P, :], d_hs[s], writes=[R_h[s]])
                for s in range(4):
                    tok = slice(b * TB + s * P, b * TB + (s + 1) * P)
                    for n2 in range(2):
                        pp = npo % 2; npo += 1
                        cols = slice(n2 * 512, (n2 + 1) * 512)
                        for e_ in range(DC):
                            S.op(pe, lambda e, e_=e_, pp=pp, tok=tok, cols=cols: e.matmul(po[pp][:], lhsT=mixs[e_ // 4][:, e_ % 4, tok], rhs=wout[:, e_, cols],
                                                                                        start=(e_ == 0), stop=(e_ == DC - 1)),
                                 reads=[R_wout, R_mix[e_][b]], writes=[R_po[pp]], inc=(e_ == DC - 1))
                        S.op(dve, lambda e, pp=pp, s=s, cols=cols: e.tensor_tensor(out=hblk[:, s, cols], in0=hblk[:, s, cols], in1=po[pp][:], op=ALU.add),
                             reads=[R_po[pp], R_h[s]], writes=[R_h[s]])
                def n2_front(s):
                    S.op(act, lambda e: e.activation(out=junk3[:], in_=hblk[:, s, :], func=AF.Square, accum_out=stat3[:, s:s + 1]),
                         reads=[R_h[s]], writes=[R_j3, R_st3[s]])
                    S.op(act, lambda e: e.activation(out=stat3[:, s:s + 1], in_=stat3[:, s:s + 1], func=AF.Ln, scale=1.0 / D, bias=EPS),
                         reads=[R_st3[s]], writes=[R_st3[s]])
                    S.op(act, lambda e: e.activation(out=stat3[:, s:s + 1], in_=stat3[:, s:s + 1], func=AF.Exp, scale=-0.5),
                         reads=[R_st3[s]], writes=[R_st3[s]])
                    nb2, R_nb2, ptr3, R_ptr3 = nb2s[s % 2], R_nb2s[s % 2], ptr3s[s % 2], R_ptr3s[s % 2]
                    S.op(dve, lambda e: e.tensor_scalar(out=nb2[:], in0=hblk[:, s, :], scalar1=stat3[:, s:s + 1], scalar2=None, op0=ALU.mult),
                         reads=[R_h[s], R_st3[s]], writes=[R_nb2])
                    for j in range(DC):
                        S.op(pe, lambda e, j=j: e.transpose(out=ptr3[:, j, :], in_=nb2[:, j * P:(j + 1) * P], identity=ident[:]),
                             reads=[R_nb2, R_const], writes=[R_ptr3], inc=(j == DC - 1))

                def n2_back(s):
                    tok = slice(b * TB + s * P, b * TB + (s + 1) * P)
                    ptr3, R_ptr3 = ptr3s[s % 2], R_ptr3s[s % 2]
                    for hf in range(2):
                        S.op(dve, lambda e, hf=hf: e.tensor_tensor(
                            out=mixs[hf][:, :, tok], in0=ptr3[:, 4 * hf:4 * hf + 4, :],
                            in1=prm[:, PC_G2 + 4 * hf:PC_G2 + 4 * hf + 4].unsqueeze(2).to_broadcast([P, 4, P]), op=ALU.mult),
                             reads=[R_ptr3, R_prm], writes=[R_mix[c][b] for c in range(4 * hf, 4 * hf + 4)])

                for s in range(5):
                    if s < 4:
                        n2_front(s)
                    if s >= 1:
                        n2_back(s - 1)
                blk = slice(b * TB, (b + 1) * TB)
                for fg in range(FC // 2):
                    ws = nfg % 2; nfg += 1
                    S.dma(sp, wg[ws][:], w_gate_v[:, :, fg * 256:(fg + 1) * 256], d_wg[ws], reads=[R_wscr["gate"]], writes=[R_wg[ws]])
                    S.dma(sp, wu[ws][:], w_up_v[:, :, fg * 256:(fg + 1) * 256], d_wu[ws], reads=[R_wscr["up"]], writes=[R_wu[ws]])
                    for f2 in range(2):
                        f = fg * 2 + f2
                        pp = f % 2
                        fc = slice(f2 * P, (f2 + 1) * P)
                        for j in range(DC):
                            S.op(pe, lambda e, j=j, pp=pp, ws=ws, fc=fc: e.matmul(pg[pp][:], lhsT=wg[ws][:, j, fc], rhs=mixs[j // 4][:, j % 4, blk],
                                                                                 start=(j == 0), stop=(j == DC - 1)),
                                 reads=[R_wg[ws]] + [R_mix[j][b]], writes=[R_pg[pp]], inc=(j == DC - 1))
                        for j in range(DC):
                            S.op(pe, lambda e, j=j, pp=pp, ws=ws, fc=fc: e.matmul(pu[pp][:], lhsT=wu[ws][:, j, fc], rhs=mixs[j // 4][:, j % 4, blk],
                                                                                 start=(j == 0), stop=(j == DC - 1)),
                                 reads=[R_wu[ws]] + [R_mix[j][b]], writes=[R_pu[pp]], inc=(j == DC - 1))
                        S.op(act, lambda e, pp=pp: e.activation(out=sg[pp][:], in_=pg[pp][:], func=AF.Silu), reads=[R_pg[pp]], writes=[R_sg[pp]])
                        S.op(dve, lambda e, pp=pp, f=f: e.tensor_tensor(out=actT[:, f, :], in0=pu[pp][:], in1=sg[pp][:], op=ALU.mult),
                             reads=[R_pu[pp], R_sg[pp]], writes=[R_act[f]])
                for s in range(4):
                    for n2 in range(2):
                        pp = npo % 2; npo += 1
                        cols = slice(n2 * 512, (n2 + 1) * 512)
                        for f in range(FC):
                            S.op(pe, lambda e, f=f, pp=pp, s=s, cols=cols: e.matmul(po[pp][:], lhsT=actT[:, f, s * P:(s + 1) * P], rhs=wd[:, f, cols],
                                                                                  start=(f == 0), stop=(f == FC - 1)),
                                 reads=[R_wd, R_act[f]], writes=[R_po[pp]], inc=(f == FC - 1))
                        S.op(dve, lambda e, pp=pp, s=s, cols=cols: e.tensor_tensor(out=hblk[:, s, cols], in0=hblk[:, s, cols], in1=po[pp][:], op=ALU.add),
                             reads=[R_po[pp], R_h[s]], writes=[R_h[s]])
                    S.op(act, lambda e, s=s: e.activation(out=junk3[:], in_=hblk[:, s, :], func=AF.Square, accum_out=stat3[:, 4 + s:5 + s]),
                         reads=[R_h[s]], writes=[R_j3, R_st3[s]])
                    S.op(act, lambda e, s=s: e.activation(out=stat3[:, 4 + s:5 + s], in_=stat3[:, 4 + s:5 + s], func=AF.Ln, scale=1.0 / D, bias=EPS),
                         reads=[R_st3[s]], writes=[R_st3[s]])
                    S.op(act, lambda e, s=s: e.activation(out=stat3[:, 4 + s:5 + s], in_=stat3[:, 4 + s:5 + s], func=AF.Exp, scale=-0.5),
                         reads=[R_st3[s]], writes=[R_st3[s]])
                    os_ = not_ % 2; not_ += 1
                    S.op(dve, lambda e, s=s, os_=os_: e.scalar_tensor_tensor(out=ot[os_][:], in0=hblk[:, s, :], scalar=stat3[:, 4 + s:5 + s],
                                                                            in1=gfin[:], op0=ALU.mult, op1=ALU.mult),
                         reads=[R_h[s], R_st3[s], R_gfin], writes=[R_ot[os_]])
                    S.dma(pool, out_d[b * TB + s * P:b * TB + (s + 1) * P, :], ot[os_][:], d_ot[os_], reads=[R_ot[os_]])
                    if b + 1 < NB:
                        S.dma(sp, hblk[:, s, :], x_d[(b + 1) * TB + s * P:(b + 1) * TB + (s + 1) * P, :], d_hs[s], writes=[R_h[s]])
            S.barrier(final=True)
        st_mix.close()
    return nc


def t5_bucket_np(rel):
    half = 16
    ret = np.where(rel > 0, half, 0)
    n = np.abs(rel)
    max_exact = half // 2
    nf = np.maximum(n, 1).astype(np.float32)
    large = max_exact + (np.log(nf / max_exact) / math.log(128 / max_exact) * (half - max_exact)).astype(np.int32)
    large = np.minimum(large, half - 1)
    return ret + np.where(n < max_exact, n, large)


def host_constants():
    m = np.arange(GV_LEN)
    bucket = t5_bucket_np(m - 640)
    oh = np.zeros((32, GV_LEN), np.float32)
    oh[bucket, m] = 1.0
    ident = np.eye(P, dtype=np.float32).astype(ml_dtypes.bfloat16)
    aident = np.ascontiguousarray(np.eye(P, dtype=np.float32)[::-1]).astype(ml_dtypes.bfloat16)
    return oh, ident, aident


def pack_inputs(inp):
    f = np.float32
    prm = np.zeros((P, 64), f)
    prm[:, 0:8] = inp["attn_norm_g"][0].reshape(DC, P).T
    prm[:, 8:16] = inp["ffn_norm_g"][0].reshape(DC, P).T
    cw = inp["conv_w"][0]
    prm[:, 16:32] = cw.reshape(4, 4, P).transpose(2, 1, 0).reshape(P, 16)
    prm[:, 32:36] = inp["conv_b"][0].reshape(4, P).T
    prm[:, 36:44] = inp["b_rg"][0].reshape(2, 4, P).transpose(2, 0, 1).reshape(P, 8)
    prm[:, 44:52] = inp["b_ig"][0].reshape(2, 4, P).transpose(2, 0, 1).reshape(P, 8)
    prm[:, 52:60] = inp["lru_lambda"][0].reshape(2, 4, P).transpose(2, 0, 1).reshape(P, 8)
    prm[:, 60] = inp["subln_g"][0]
    lamv = np.zeros((P, 256), f)
    lamv[:, 0:64] = inp["lambda_q1"][0][None]
    lamv[:, 64:128] = inp["lambda_k1"][0][None]
    lamv[:, 128:192] = inp["lambda_q2"][0][None]
    lamv[:, 192:256] = inp["lambda_k2"][0][None]
    gfin = np.ascontiguousarray(np.broadcast_to(inp["final_norm_g"][None, :], (P, D))).astype(f)
    wbd = np.zeros((P, 16, P), f)
    for d in range(2):
        for typ, key in enumerate(("w_rg", "w_ig")):
            w = inp[key][0, d]
            for c in range(4):
                m = d * 8 + typ * 4 + c
                wbd[0:64, m, 0:64] = w[2 * c]
                wbd[64:128, m, 64:128] = w[2 * c + 1]
    oh, ident, aident = host_constants()
    shared = {
        "w_in": np.ascontiguousarray(inp["w_in"][0]), "w_out": np.ascontiguousarray(inp["w_out"][0]),
        "w_gate": np.ascontiguousarray(inp["w_gate"][0]), "w_up": np.ascontiguousarray(inp["w_up"][0]),
        "w_down": np.ascontiguousarray(inp["w_down"][0]),
        "prm": prm, "lamv": lamv, "gfin": gfin, "wbd": wbd,
        "relb": np.ascontiguousarray(inp["rel_bias"]).astype(f), "oh": oh, "ident": ident, "aident": aident,
    }
    return shared


def kernel(**inputs):
    inp = {k: np.asarray(v) for k, v in inputs.items()}
    shared = pack_inputs(inp)
    nc = build_nc()
    x = inp["x"]
    in_maps = []
    for c in range(8):
        m = dict(shared)
        m["x"] = np.ascontiguousarray(x[c])
        in_maps.append(m)
    res = run_bass_kernel_spmd(nc, in_maps, core_ids=list(range(8)))
    out = np.stack([np.asarray(r["out"]) for r in res.results], axis=0)
    return out.astype(np.float32)
```
